# Optimizing a Trainium2 kernel written in Bass

```python
import jax, jax.numpy as jnp
from jax import lax
import numpy as np

D_MODEL = 2048
BATCH = 8
SEQ = 4096
DEPTH = 2

N_EVEN = (DEPTH + 1) // 2
N_ODD = DEPTH // 2
RMS_EPS = 1e-6
NEG_BIG = -1e30

MLSTM_WIDTH = D_MODEL // 2
MLSTM_HEADS = 4
MLSTM_DV = MLSTM_WIDTH // MLSTM_HEADS
MLSTM_DK = MLSTM_DV // 2
MLSTM_CHUNK = 64
F_BIAS_INIT = 3.0

RNN_WIDTH = D_MODEL // 2
RNN_BLOCKS = 8
RNN_BLOCK = RNN_WIDTH // RNN_BLOCKS
CONV_WIDTH = 4
CONV_LEFT = 2
RGLRU_C = 8.0

IN_SIZES = (MLSTM_HEADS * MLSTM_DK,
            MLSTM_HEADS * MLSTM_DK,
            MLSTM_WIDTH,
            MLSTM_WIDTH,
            4 * MLSTM_HEADS,
            RNN_WIDTH,
            RNN_WIDTH)
IN_SPLITS = tuple(sum(IN_SIZES[:i + 1]) for i in range(len(IN_SIZES) - 1))
D_IN = sum(IN_SIZES)

ATTN_HEADS = 16
ATTN_DH = D_MODEL // ATTN_HEADS
DILATED_PATTERNS = ((128, 1), (512, 4), (2048, 16))
ATTN_BLOCK = 64
ALIBI_MAX_BIAS = 8.0

D_FF = ((8 * D_MODEL + 3 * 256 - 1) // (3 * 256)) * 256

kernel_name = 'hybrid_mlstm_rglru_dilated_encoder'


def rmsnorm(x, g):
    xf = x.astype(jnp.float32)
    y = xf * lax.rsqrt(jnp.mean(jnp.square(xf), axis=-1, keepdims=True) + RMS_EPS)
    return (y * g.astype(jnp.float32)).astype(x.dtype)


def mlstm_chunkwise(q, k, v, log_i, log_f):
    B, H, S, dk = q.shape
    dv = v.shape[-1]
    L = MLSTM_CHUNK
    nc = S // L

    def to_chunks(a):
        return jnp.moveaxis(a.reshape(a.shape[:2] + (nc, L) + a.shape[3:]), 2, 0)

    xs = tuple(to_chunks(a) for a in (q, k, v, log_i, log_f))
    lower = jnp.tril(jnp.ones((L, L), dtype=bool))

    def step(carry, chunk):
        C, n, m = carry
        qj, kj, vj, ij, fj = chunk
        b = jnp.cumsum(fj, axis=-1)
        D = b[..., :, None] - b[..., None, :] + ij[..., None, :]
        D = jnp.where(lower, D, NEG_BIG)
        inter = b + m[..., None]
        m_out = jnp.maximum(inter, jnp.max(D, axis=-1))
        w_intra = jnp.exp(D - m_out[..., None])
        w_inter = jnp.exp(inter - m_out)
        s = jnp.einsum('bhld,bhsd->bhls', qj, kj) * w_intra
        num = (jnp.einsum('bhls,bhsv->bhlv', s, vj)
               + w_inter[..., None] * jnp.einsum('bhvd,bhld->bhlv', C, qj))
        den = jnp.sum(s, axis=-1) + w_inter * jnp.einsum('bhd,bhld->bhl', n, qj)
        h = num / jnp.maximum(jnp.abs(den), jnp.exp(-m_out))[..., None]
        bL = b[..., -1]
        src = bL[..., None] - b + ij
        m_new = jnp.maximum(bL + m, jnp.max(src, axis=-1))
        w_src = jnp.exp(src - m_new[..., None])
        decay = jnp.exp(bL + m - m_new)
        C_new = decay[..., None, None] * C + jnp.einsum('bhl,bhlv,bhld->bhvd', w_src, vj, kj)
        n_new = decay[..., None] * n + jnp.einsum('bhl,bhld->bhd', w_src, kj)
        return (C_new, n_new, m_new), h

    init = (jnp.zeros((B, H, dv, dk), jnp.float32),
            jnp.zeros((B, H, dk), jnp.float32),
            jnp.full((B, H), NEG_BIG, jnp.float32))
    _, hc = lax.scan(step, init, xs)
    return jnp.moveaxis(hc, 0, 2).reshape(B, H, S, dv)


def bidirectional_mlstm(q, k, v, gates):
    g = jnp.transpose(gates, (2, 0, 3, 1))
    fwd = mlstm_chunkwise(q, k, v, g[0], jax.nn.log_sigmoid(g[1]))
    flip = lambda a: jnp.flip(a, axis=2)
    bwd = flip(mlstm_chunkwise(flip(q), flip(k), flip(v), flip(g[2]),
                               flip(jax.nn.log_sigmoid(g[3]))))
    return fwd + bwd


def centred_depthwise_conv(x, w, b):
    S = x.shape[1]
    xp = jnp.pad(x, ((0, 0), (CONV_LEFT, CONV_WIDTH - 1 - CONV_LEFT), (0, 0)))
    return sum(xp[:, j:j + S, :] * w[j] for j in range(CONV_WIDTH)) + b


def rglru_scan(x, w_a, b_a, w_x, b_x, lam, reverse):
    B, S, _ = x.shape
    xf = x.astype(jnp.float32)
    xb = xf.reshape(B, S, RNN_BLOCKS, RNN_BLOCK)
    pre_a = jnp.einsum('bsni,nij->bsnj', xb, w_a.astype(jnp.float32)).reshape(B, S, RNN_WIDTH)
    pre_x = jnp.einsum('bsni,nij->bsnj', xb, w_x.astype(jnp.float32)).reshape(B, S, RNN_WIDTH)
    r = jax.nn.sigmoid(pre_a + b_a.astype(jnp.float32))
    i = jax.nn.sigmoid(pre_x + b_x.astype(jnp.float32))
    log_a = -RGLRU_C * r * jax.nn.softplus(-lam.astype(jnp.float32))
    a = jnp.exp(log_a)
    u = jnp.sqrt(-jnp.expm1(2.0 * log_a)) * (i * xf)

    def combine(left, right):
        a_l, u_l = left
        a_r, u_r = right
        return a_l * a_r, a_r * u_l + u_r

    _, h = lax.associative_scan(combine, (a, u), axis=1, reverse=reverse)
    return h


def even_mixer(h, w_in, gate_b, conv_w, conv_b, rg_wa, rg_ba, rg_wx, rg_bx, rg_lam,
               head_g, w_out):
    B, S, _ = h.shape
    z = h @ w_in
    q, k, v, o, g, xr, gr = jnp.split(z, IN_SPLITS, axis=-1)
    heads = lambda a, d: jnp.transpose(a.reshape(B, S, MLSTM_HEADS, d), (0, 2, 1, 3)).astype(jnp.float32)
    qh = heads(q, MLSTM_DK) * (MLSTM_DK ** -0.5)
    kh = heads(k, MLSTM_DK)
    vh = heads(v, MLSTM_DV)
    gates = (g.astype(jnp.float32) + gate_b.astype(jnp.float32)).reshape(B, S, 4, MLSTM_HEADS)
    hm = bidirectional_mlstm(qh, kh, vh, gates)
    hm = hm * lax.rsqrt(jnp.mean(jnp.square(hm), axis=-1, keepdims=True) + RMS_EPS)
    hm = jnp.transpose(hm, (0, 2, 1, 3)).reshape(B, S, MLSTM_WIDTH) * head_g.astype(jnp.float32)
    y_a = hm * jax.nn.sigmoid(o.astype(jnp.float32))
    xc = centred_depthwise_conv(xr, conv_w, conv_b)
    hr = (rglru_scan(xc, rg_wa[0], rg_ba[0], rg_wx[0], rg_bx[0], rg_lam[0], False)
          + rglru_scan(xc, rg_wa[1], rg_ba[1], rg_wx[1], rg_bx[1], rg_lam[1], True))
    y_b = hr * jax.nn.gelu(gr.astype(jnp.float32))
    y = jnp.concatenate([y_a, y_b], axis=-1).astype(h.dtype)
    return y @ w_out


def dilated_branch(q, k, v, dil, half, slopes):
    B, S, H, dh = q.shape
    sp = S // dil
    nb = -(-sp // ATTN_BLOCK)
    sq = nb * ATTN_BLOCK
    kw = ATTN_BLOCK + 2 * half

    def by_residue(a):
        return jnp.transpose(a.reshape(B, sp, dil, H, dh), (0, 2, 3, 1, 4))

    qr = jnp.pad(by_residue(q), ((0, 0), (0, 0), (0, 0), (0, sq - sp), (0, 0)))
    kv_pad = ((0, 0), (0, 0), (0, 0), (half, half + sq - sp), (0, 0))
    kr = jnp.pad(by_residue(k), kv_pad)
    vr = jnp.pad(by_residue(v), kv_pad)
    idx = jnp.arange(nb)[:, None] * ATTN_BLOCK + jnp.arange(kw)[None, :]
    kb = kr[:, :, :, idx, :]
    vb = vr[:, :, :, idx, :]
    qb = qr.reshape(B, dil, H, nb, ATTN_BLOCK, dh)
    qpos = jnp.arange(sq).reshape(nb, ATTN_BLOCK)
    kpos = idx - half
    rel = kpos[:, None, :] - qpos[:, :, None]
    valid = (jnp.abs(rel) <= half) & (kpos[:, None, :] >= 0) & (kpos[:, None, :] < sp)
    penalty = slopes[:, None, None, None] * (jnp.abs(rel) * dil).astype(jnp.float32)
    s = jnp.einsum('bdhnqc,bdhnkc->bdhnqk', qb.astype(jnp.float32), kb.astype(jnp.float32)) - penalty
    s = jnp.where(valid, s, NEG_BIG)
    m = jnp.max(s, axis=-1, keepdims=True)
    p = jnp.exp(s - m)
    denom = jnp.sum(p, axis=-1)
    o = jnp.einsum('bdhnqk,bdhnkc->bdhnqc', p, vb.astype(jnp.float32)) / denom[..., None]
    lse = m[..., 0] + jnp.log(denom)
    o = o.reshape(B, dil, H, sq, dh)[:, :, :, :sp]
    lse = lse.reshape(B, dil, H, sq)[:, :, :, :sp]
    o = jnp.transpose(o, (0, 3, 1, 2, 4)).reshape(B, S, H, dh)
    lse = jnp.transpose(lse, (0, 3, 1, 2)).reshape(B, S, H)
    return o, lse


def dilated_attention(h, w_qkv, w_o):
    B, S, _ = h.shape
    qkv = (h @ w_qkv).reshape(B, S, 3, ATTN_HEADS, ATTN_DH)
    q = qkv[:, :, 0] * (ATTN_DH ** -0.5)
    k = qkv[:, :, 1]
    v = qkv[:, :, 2]
    slopes = jnp.exp2(-ALIBI_MAX_BIAS * jnp.arange(1, ATTN_HEADS + 1, dtype=jnp.float32) / ATTN_HEADS)
    outs, lses = [], []
    for window, dil in DILATED_PATTERNS:
        o_g, lse_g = dilated_branch(q, k, v, dil, window // (2 * dil), slopes)
        outs.append(o_g)
        lses.append(lse_g)
    wts = jax.nn.softmax(jnp.stack(lses, axis=0), axis=0)
    o = jnp.sum(wts[..., None] * jnp.stack(outs, axis=0), axis=0)
    return o.reshape(B, S, D_MODEL).astype(h.dtype) @ w_o


def swiglu(h, w1, w3, w2):
    return (jax.nn.silu(h @ w1) * (h @ w3)) @ w2


def setup_inputs(seed: int = 0) -> dict:
    key = jax.random.key(seed)
    ks = jax.random.split(key, 24)
    f32 = jnp.float32
    nrm = lambda k, shape, scale: scale * jax.random.normal(k, shape, f32)
    gain = lambda k, shape: 1.0 + nrm(k, shape, 0.02)
    gate_offset = jnp.repeat(jnp.array([0.0, F_BIAS_INIT, 0.0, F_BIAS_INIT], f32), MLSTM_HEADS)
    u = jax.random.uniform(ks[10], (N_EVEN, 2, RNN_WIDTH), f32, minval=0.9, maxval=0.999)
    a0 = u ** (1.0 / RGLRU_C)
    lam = jnp.log(a0) - jnp.log1p(-a0)
    return {
        'x': nrm(ks[0], (BATCH, SEQ, D_MODEL), 1.0),
        'e_norm': gain(ks[1], (N_EVEN, D_MODEL)),
        'e_w_in': nrm(ks[2], (N_EVEN, D_MODEL, D_IN), D_MODEL ** -0.5),
        'e_gate_b': gate_offset + nrm(ks[3], (N_EVEN, 4 * MLSTM_HEADS), 0.5),
        'e_conv_w': nrm(ks[4], (N_EVEN, CONV_WIDTH, RNN_WIDTH), 0.5),
        'e_conv_b': nrm(ks[5], (N_EVEN, RNN_WIDTH), 0.02),
        'e_rg_wa': nrm(ks[6], (N_EVEN, 2, RNN_BLOCKS, RNN_BLOCK, RNN_BLOCK), RNN_BLOCK ** -0.5),
        'e_rg_ba': nrm(ks[7], (N_EVEN, 2, RNN_WIDTH), 0.02),
        'e_rg_wx': nrm(ks[8], (N_EVEN, 2, RNN_BLOCKS, RNN_BLOCK, RNN_BLOCK), RNN_BLOCK ** -0.5),
        'e_rg_bx': nrm(ks[9], (N_EVEN, 2, RNN_WIDTH), 0.02),
        'e_rg_lam': lam,
        'e_head_g': gain(ks[11], (N_EVEN, MLSTM_WIDTH)),
        'e_w_out': nrm(ks[12], (N_EVEN, D_MODEL, D_MODEL), D_MODEL ** -0.5),
        'o_norm': gain(ks[13], (N_ODD, D_MODEL)),
        'o_w_qkv': nrm(ks[14], (N_ODD, D_MODEL, 3 * D_MODEL), D_MODEL ** -0.5),
        'o_w_o': nrm(ks[15], (N_ODD, D_MODEL, D_MODEL), D_MODEL ** -0.5),
        'f_norm': gain(ks[16], (DEPTH, D_MODEL)),
        'f_w1': nrm(ks[17], (DEPTH, D_MODEL, D_FF), D_MODEL ** -0.5),
        'f_w3': nrm(ks[18], (DEPTH, D_MODEL, D_FF), D_MODEL ** -0.5),
        'f_w2': nrm(ks[19], (DEPTH, D_FF, D_MODEL), D_FF ** -0.5),
        'final_norm': gain(ks[20], (D_MODEL,)),
    }


def reference(x, e_norm, e_w_in, e_gate_b, e_conv_w, e_conv_b, e_rg_wa, e_rg_ba, e_rg_wx,
              e_rg_bx, e_rg_lam, e_head_g, e_w_out, o_norm, o_w_qkv, o_w_o, f_norm, f_w1,
              f_w3, f_w2, final_norm):
    for l in range(DEPTH):
        if l % 2 == 0:
            e = l // 2
            h = rmsnorm(x, e_norm[e])
            x = x + even_mixer(h, e_w_in[e], e_gate_b[e], e_conv_w[e], e_conv_b[e], e_rg_wa[e],
                               e_rg_ba[e], e_rg_wx[e], e_rg_bx[e], e_rg_lam[e], e_head_g[e],
                               e_w_out[e]).astype(x.dtype)
        else:
            o = l // 2
            h = rmsnorm(x, o_norm[o])
            x = x + dilated_attention(h, o_w_qkv[o], o_w_o[o]).astype(x.dtype)
        h = rmsnorm(x, f_norm[l])
        x = x + swiglu(h, f_w1[l], f_w3[l], f_w2[l]).astype(x.dtype)
    return rmsnorm(x, final_norm)
```

```python
from contextlib import ExitStack
import numpy as np
import concourse.bass as bass
import concourse.mybir as mybir
from concourse.bass_utils import run_bass_kernel_spmd

F32 = mybir.dt.float32
BF16 = mybir.dt.bfloat16
AF = mybir.ActivationFunctionType
ALU = mybir.AluOpType

S = 4096
D = 2048
DFF = 5632
DIN = 5136
EPS = 1e-6
ENGS = ("pe", "act", "dve", "pool", "sp")


class Res:
    __slots__ = ("name", "w", "r")

    def __init__(self, name=""):
        self.name = name
        self.w = None
        self.r = []


class _Eng:
    def __init__(self, name, sem):
        self.name = name
        self.sem = sem
        self.cnt = 0
        self.known = {}
        self.ops = []
        self.pending = False


class Prog:
    N_DMA_SEMS = 24

    def __init__(self, nc, es):
        self.nc = nc
        self.es = es
        self.e = {}
        for n in ENGS:
            self.e[n] = _Eng(n, es.enter_context(nc.semaphore("s_" + n)))
        self.dma_sems = {}
        for q in ("sp", "pool", "act"):
            lst = [[es.enter_context(nc.semaphore(f"d_{q}{i}")), 0]
                   for i in range(self.N_DMA_SEMS if q != "act" else 4)]
            self.dma_sems[q] = [lst, 0]
        self.n_ops = 0

    def _deps(self, eng, reads, writes):
        toks = []
        for r in reads:
            if r.w is not None:
                toks.append(r.w)
        for w in writes:
            if w.w is not None:
                toks.append(w.w)
            toks.extend(w.r)
        need = {}
        for (sem, val) in toks:
            k = id(sem)
            if eng.name == "pe" and sem is eng.sem:
                continue
            if eng.known.get(k, 0) >= val:
                continue
            if k not in need or need[k][1] < val:
                need[k] = (sem, val)
        for k, (sem, val) in need.items():
            eng.known[k] = val
        return list(need.values())

    def _commit(self, tok, reads, writes):
        for r in reads:
            r.r.append(tok)
            if len(r.r) > 48:
                best = {}
                for (s, v) in r.r:
                    if id(s) not in best or best[id(s)][1] < v:
                        best[id(s)] = (s, v)
                r.r = list(best.values())
        for w in writes:
            w.w = tok
            w.r = []

    def op(self, eng, fn, reads=(), writes=(), inc=True):
        E = self.e[eng]
        waits = self._deps(E, reads, writes)
        if inc:
            E.cnt += 1
            tok = (E.sem, E.cnt)
            E.pending = False
        else:
            tok = (E.sem, E.cnt + 1)
            E.pending = True
        E.ops.append((waits, fn, (E.sem, 1) if inc else None))
        self._commit(tok, reads, writes)
        self.n_ops += 1

    def dma(self, q, out, in_, reads=(), writes=(), **kw):
        E = self.e[q]
        waits = self._deps(E, reads, writes)
        lst, idx = self.dma_sems[q]
        slot = lst[idx % len(lst)]
        self.dma_sems[q][1] = idx + 1
        sem, cur = slot
        if cur > 0 and E.known.get(id(sem), 0) < cur:
            waits.append((sem, cur))
            E.known[id(sem)] = cur
        slot[1] = cur + 16
        tok = (sem, cur + 16)

        def fn(e, out=out, in_=in_, kw=kw):
            return e.dma_start(out=out, in_=in_, **kw)

        E.ops.append((waits, fn, (sem, 16)))
        self._commit(tok, reads, writes)
        self.n_ops += 1

    def barrier(self):
        toks = []
        for n in ENGS:
            E = self.e[n]
            assert not E.pending, n
            if E.cnt > 0:
                toks.append((E.sem, E.cnt))
        for q in self.dma_sems:
            for sem, cur in self.dma_sems[q][0]:
                if cur > 0:
                    toks.append((sem, cur))
        for n in ENGS:
            E = self.e[n]
            waits = []
            for (sem, val) in toks:
                if sem is E.sem:
                    continue
                if E.known.get(id(sem), 0) < val:
                    E.known[id(sem)] = val
                    waits.append((sem, val))
            if waits:
                E.ops.append((waits, None, None))

    def emit(self):
        nc = self.nc
        for n in ENGS:
            assert not self.e[n].pending, f"engine {n} has pending un-inc'd ops"
        with nc.Block() as block:
            def run(E):
                def body(eng):
                    for (waits, fn, inc) in E.ops:
                        for (sem, val) in waits:
                            eng.wait_ge(sem, val)
                        if fn is not None:
                            ins = fn(eng)
                            if inc is not None:
                                ins.then_inc(inc[0], inc[1])
                return body
            if self.e["sp"].ops:
                block.sync(run(self.e["sp"]))
            if self.e["act"].ops:
                block.scalar(run(self.e["act"]))
            if self.e["dve"].ops:
                block.vector(run(self.e["dve"]))
            if self.e["pool"].ops:
                block.gpsimd(run(self.e["pool"]))
            if self.e["pe"].ops:
                block.tensor(run(self.e["pe"]))


class Arena:
    def __init__(self, ap, words):
        self.ap = ap
        self.words = words
        self.off = 0

    def mark(self):
        return self.off

    def reset(self, m=0):
        self.off = m

    def alloc(self, shape, dt):
        n = 1
        for s in shape:
            n *= s
        w = n if dt == F32 else (n + 1) // 2
        w = (w + 7) // 8 * 8
        assert self.off + w <= self.words, f"arena overflow {self.off}+{w}>{self.words}"
        v = self.ap[:, self.off:self.off + w]
        self.off += w
        if dt != F32:
            v = v.bitcast(dt)
        v = v[:, 0:n]
        if len(shape) == 2:
            v = v.rearrange("p (a b) -> p a b", a=shape[0])
        elif len(shape) == 3:
            v = v.rearrange("p (a b c) -> p a b c", a=shape[0], b=shape[1])
        return v


class Ring:
    def __init__(self, items):
        self.items = items
        self.i = 0

    def next(self):
        it = self.items[self.i % len(self.items)]
        self.i += 1
        return it


def ring(A, n, shape, dt, name="r"):
    return Ring([(A.alloc(shape, dt), Res(f"{name}{i}")) for i in range(n)])


PV = {}
_c = 0
for _n, _w in (("e_norm", 16), ("f_norm0", 16), ("f_norm1", 16), ("o_norm", 16), ("final_norm", 16),
               ("conv_w", 32), ("conv_b", 8), ("rg_ba", 16), ("rg_bx", 16), ("rg_lam", 16),
               ("gate_bI", 1), ("gate_bF", 1)):
    PV[_n] = _c
    _c += _w
NPV = _c


def host_pvec(inp):
    pv = np.zeros((128, NPV), np.float32)
    col = lambda v: np.ascontiguousarray(np.asarray(v, np.float32).reshape(-1, 128).T)
    pv[:, PV["e_norm"]:PV["e_norm"] + 16] = col(inp["e_norm"][0])
    pv[:, PV["f_norm0"]:PV["f_norm0"] + 16] = col(inp["f_norm"][0])
    pv[:, PV["f_norm1"]:PV["f_norm1"] + 16] = col(inp["f_norm"][1])
    pv[:, PV["o_norm"]:PV["o_norm"] + 16] = col(inp["o_norm"][0])
    pv[:, PV["final_norm"]:PV["final_norm"] + 16] = col(inp["final_norm"])
    cw = np.asarray(inp["e_conv_w"][0], np.float32)
    for c in range(8):
        for j in range(4):
            pv[:, PV["conv_w"] + c * 4 + j] = cw[j, c * 128:(c + 1) * 128]
    pv[:, PV["conv_b"]:PV["conv_b"] + 8] = col(inp["e_conv_b"][0])
    for d in range(2):
        pv[:, PV["rg_ba"] + d * 8:PV["rg_ba"] + d * 8 + 8] = col(inp["e_rg_ba"][0, d])
        pv[:, PV["rg_bx"] + d * 8:PV["rg_bx"] + d * 8 + 8] = col(inp["e_rg_bx"][0, d])
        pv[:, PV["rg_lam"] + d * 8:PV["rg_lam"] + d * 8 + 8] = col(inp["e_rg_lam"][0, d])
    gb = np.asarray(inp["e_gate_b"][0], np.float32)
    for d in range(2):
        for h in range(4):
            pv[d * 4 + h, PV["gate_bI"]] = gb[(2 * d) * 4 + h]
            pv[d * 4 + h, PV["gate_bF"]] = gb[(2 * d + 1) * 4 + h]
    return pv


def host_consts():
    ident = np.eye(128, dtype=np.float32)
    s = np.arange(128)[:, None]
    t = np.arange(128)[None, :]
    mask_f = (s <= t).astype(np.float32)
    mask_b = (s >= t).astype(np.float32)
    absd = np.zeros((128, 17, 128), np.float32)
    mult = np.zeros((128, 17, 128), np.float32)
    for e in range(17):
        d = (s - t) - (e - 8) * 128
        ad = np.abs(d)
        absd[:, e, :] = ad
        c = (ad <= 64).astype(np.float32) + ((ad <= 256) & (d % 4 == 0)) + ((ad <= 1024) & (d % 16 == 0))
        mult[:, e, :] = c
    return {"c_ident": ident, "c_mask": np.stack([mask_f, mask_b], 1).copy(),
            "c_absd": absd, "c_mult": mult}


class K:
    pass


def mm(out, lhsT, rhs, start, stop):
    return lambda e: e.matmul(out, lhsT, rhs, start=start, stop=stop)


def phase_norm(k, xT, gcol, hT, hres, out_dram=None):
    P, A = k.P, k.A
    m0 = A.mark()
    TW = 256
    xts = ring(A, 2, [16, TW], F32, "nx")
    sqs = ring(A, 2, [16, TW], BF16, "nsq")
    rss = ring(A, 2, [TW], F32, "nr")
    outs = ring(A, 2, [16, TW], F32, "no") if out_dram is not None else None
    pv = k.pv
    for j in range(S // TW):
        xt, rxt = xts.next()
        sq, rsq = sqs.next()
        rs, rrs = rss.next()
        bank, bres = k.psr.next()
        tsl = slice(j * TW, (j + 1) * TW)
        P.dma("sp", xt, xT[:, tsl].rearrange("(k p) n -> p k n", p=128), writes=[rxt])
        P.op("act", lambda e, sq=sq, xt=xt: e.activation(sq, xt, AF.Square), reads=[rxt], writes=[rsq])
        for kc in range(16):
            P.op("pe", mm(bank[:, 0:TW], k.ones_bf, sq[:, kc, :], kc == 0, kc == 15),
                 reads=[rsq, k.rconst], writes=[bres], inc=(kc == 15))
        P.op("act", lambda e, rs=rs, bank=bank: e.activation(rs, bank[:, 0:TW], AF.Sqrt, bias=k.eps_ap, scale=1.0 / D),
             reads=[bres, k.rconst], writes=[rrs])
        P.op("dve", lambda e, rs=rs: e.reciprocal(rs, rs), reads=[rrs], writes=[rrs])
        rb = rs.unsqueeze(1).to_broadcast([128, 16, TW])
        gb = pv[:, gcol:gcol + 16].unsqueeze(2).to_broadcast([128, 16, TW])
        P.op("dve", lambda e, xt=xt, rb=rb: e.tensor_tensor(xt, xt, rb, ALU.mult), reads=[rxt, rrs], writes=[rxt])
        if out_dram is None:
            P.op("pool", lambda e, xt=xt, gb=gb, tsl=tsl: e.tensor_tensor(hT[:, :, tsl], xt, gb, ALU.mult),
                 reads=[rxt, k.rconst], writes=[hres[j * TW // 512]])
        else:
            ot, rot = outs.next()
            P.op("pool", lambda e, xt=xt, gb=gb, ot=ot: e.tensor_tensor(ot, xt, gb, ALU.mult),
                 reads=[rxt, k.rconst], writes=[rot])
            P.dma("sp", out_dram[:, tsl].rearrange("(k p) n -> p k n", p=128), ot, reads=[rot])
    A.reset(m0)


def load_w(k, wring, W, c0, nb, KCn):
    wt, wres = wring.next()
    k.P.dma("pool", wt[:, 0:KCn, 0:nb], W[:, c0:c0 + nb].rearrange("(k p) n -> p k n", p=128), writes=[wres])
    return wt, wres


def linear_fm(k, hT, hres, KCn, W, c0, ncols, wring, blk, ntiles, epi, tok_off=0):
    P = k.P
    for cb in range(0, ncols, blk):
        nb = min(blk, ncols - cb)
        wt, wres = load_w(k, wring, W, c0 + cb, nb, KCn)
        for mo in range(0, nb, 128):
            mw = min(128, nb - mo)
            for n in range(ntiles):
                bank, bres = k.psr.next()
                for kc in range(KCn):
                    P.op("pe", mm(bank[0:mw, :], wt[:, kc, mo:mo + mw], hT[:, kc, n * 512:(n + 1) * 512],
                                  kc == 0, kc == KCn - 1),
                         reads=[wres, hres[n]], writes=[bres], inc=(kc == KCn - 1))
                epi(c0 + cb + mo, mw, n, bank, bres)


def linear_tm(k, hT, hres, KCn, W, c0, ncols, wring, epi):
    P = k.P
    for cb in range(0, ncols, 512):
        nb = min(512, ncols - cb)
        wt, wres = load_w(k, wring, W, c0 + cb, nb, KCn)
        for t in range(S // 128):
            bank, bres = k.psr.next()
            for kc in range(KCn):
                P.op("pe", mm(bank[:, 0:nb], hT[:, kc, t * 128:(t + 1) * 128], wt[:, kc, 0:nb],
                              kc == 0, kc == KCn - 1),
                     reads=[wres, hres[t // 4]], writes=[bres], inc=(kc == KCn - 1))
            epi(cb, nb, t, bank, bres)


def alt_copy(k, out, in_, reads, writes, scale=None):
    P = k.P
    k.alt ^= 1
    if k.alt:
        if scale is None:
            P.op("act", lambda e: e.copy(out, in_), reads=reads, writes=writes)
        else:
            P.op("act", lambda e: e.mul(out, in_, scale), reads=reads, writes=writes)
    else:
        if scale is None:
            P.op("dve", lambda e: e.tensor_copy(out, in_), reads=reads, writes=writes)
        else:
            P.op("dve", lambda e: e.tensor_scalar_mul(out, in_, scale), reads=reads, writes=writes)


def phase_in_proj(k, xT, d):
    P, A = k.P, k.A
    A.reset(k.a0)
    hT = A.alloc([16, S], BF16)
    hres = [Res(f"h{i}") for i in range(8)]
    phase_norm(k, xT, PV["e_norm"], hT, hres)
    wring = ring(A, 3, [16, 512], BF16, "w")
    st_b = ring(A, 4, [512], BF16, "sb")
    st_f = ring(A, 4, [512], F32, "sf")
    W = k.w["e_w_in"]

    def epi_bf(dst, scale, row0):
        def epi(c, mw, n, bank, bres):
            st, rst = st_b.next()
            alt_copy(k, st[0:mw, :], bank[0:mw, :], [bres], [rst], scale)
            P.dma("sp", dst[c - row0:c - row0 + mw, n * 512:(n + 1) * 512], st[0:mw, :], reads=[rst])
        return epi

    linear_fm(k, hT, hres, 16, W, 0, 512, wring, 512, 8, epi_bf(d["qT"], 128.0 ** -0.5, 0))
    linear_fm(k, hT, hres, 16, W, 512, 512, wring, 512, 8, epi_bf(d["kT"], None, 512))

    def epi_v(cb, nb, t, bank, bres):
        st, rst = st_b.next()
        alt_copy(k, st[:, 0:nb], bank[:, 0:nb], [bres], [rst])
        P.dma("sp", d["v"][t * 128:(t + 1) * 128, cb:cb + nb], st[:, 0:nb], reads=[rst])
    linear_tm(k, hT, hres, 16, W, 1024, 1024, wring, epi_v)

    def epi_o(cb, nb, t, bank, bres):
        st, rst = st_f.next()
        P.op("act", lambda e: e.activation(st[:, 0:nb], bank[:, 0:nb], AF.Sigmoid), reads=[bres], writes=[rst])
        P.op("dve", lambda e: e.tensor_tensor(st[:, 0:nb], st[:, 0:nb], k.hg_bc[:, cb:cb + nb], ALU.mult),
             reads=[rst, k.rconst], writes=[rst])
        P.dma("sp", d["og"][t * 128:(t + 1) * 128, cb:cb + nb], st[:, 0:nb], reads=[rst])
    linear_tm(k, hT, hres, 16, W, 2048, 1024, wring, epi_o)

    wgI = A.alloc([16, 8], BF16)
    wgF = A.alloc([16, 8], BF16)
    rwg = Res("wg")
    for dd in range(2):
        for (wg, off) in ((wgI, 0), (wgF, 4)):
            cc = 3072 + dd * 8 + off
            P.dma("pool", wg[:, :, dd * 4:dd * 4 + 4], W[:, cc:cc + 4].rearrange("(k p) n -> p k n", p=128),
                  writes=[rwg])
    for (wg, dst, bcol) in ((wgI, d["gI"], PV["gate_bI"]), (wgF, d["gF"], PV["gate_bF"])):
        for n in range(8):
            bank, bres = k.psr.next()
            for kc in range(16):
                P.op("pe", mm(bank[0:8, :], wg[:, kc, :], hT[:, kc, n * 512:(n + 1) * 512], kc == 0, kc == 15),
                     reads=[rwg, hres[n]], writes=[bres], inc=(kc == 15))
            st, rst = st_f.next()
            P.op("dve", lambda e, st=st, bank=bank, bcol=bcol: e.tensor_scalar(
                st[0:8, :], bank[0:8, :], k.pv[0:8, bcol:bcol + 1], None, ALU.add),
                reads=[bres, k.rconst], writes=[rst])
            P.dma("sp", dst[:, n * 512:(n + 1) * 512], st[0:8, :], reads=[rst])

    def epi_xr(c, mw, n, bank, bres):
        st, rst = st_f.next()
        alt_copy(k, st[0:mw, :], bank[0:mw, :], [bres], [rst])
        P.dma("sp", d["xrT"][c - 3088:c - 3088 + mw, n * 512:(n + 1) * 512], st[0:mw, :], reads=[rst])
    linear_fm(k, hT, hres, 16, W, 3088, 1024, wring, 512, 8, epi_xr)

    def epi_gr(c, mw, n, bank, bres):
        st, rst = st_f.next()
        P.op("act", lambda e: e.activation(st[0:mw, :], bank[0:mw, :], AF.Gelu_apprx_tanh), reads=[bres], writes=[rst])
        P.dma("sp", d["ggT"][c - 4112:c - 4112 + mw, n * 512:(n + 1) * 512], st[0:mw, :], reads=[rst])
    linear_fm(k, hT, hres, 16, W, 4112, 1024, wring, 512, 8, epi_gr)
    P.barrier()


def phase_out_proj(k, srcT, W, xinT, xoutT):
    P, A = k.P, k.A
    A.reset(k.a0)
    hT = A.alloc([16, S], BF16)
    hres = [Res(f"h{i}") for i in range(8)]
    for n in range(8):
        P.dma("sp", hT[:, :, n * 512:(n + 1) * 512],
              srcT[:, n * 512:(n + 1) * 512].rearrange("(k p) n -> p k n", p=128), writes=[hres[n]])
    wring = ring(A, 3, [16, 512], BF16, "w")
    xr_ = ring(A, 4, [512], F32, "xi")

    def epi(c, mw, n, bank, bres):
        xt, rxt = xr_.next()
        P.dma("sp", xt, xinT[c:c + 128, n * 512:(n + 1) * 512], writes=[rxt])
        P.op("dve", lambda e: e.tensor_tensor(xt, xt, bank, ALU.add), reads=[bres, rxt], writes=[rxt])
        P.dma("sp", xoutT[c:c + 128, n * 512:(n + 1) * 512], xt, reads=[rxt])
    linear_fm(k, hT, hres, 16, W, 0, D, wring, 512, 8, epi)
    P.barrier()


def phase_ffn(k, xinT, gcol, W1, W3, W2, uT, xoutT):
    P, A = k.P, k.A
    A.reset(k.a0)
    hT = A.alloc([16, S], BF16)
    hres = [Res(f"h{i}") for i in range(8)]
    phase_norm(k, xinT, gcol, hT, hres)
    m1 = A.mark()
    w1r = ring(A, 2, [16, 256], BF16, "w1")
    w3r = ring(A, 2, [16, 256], BF16, "w3")
    sil = ring(A, 3, [512], F32, "sil")
    ust = ring(A, 4, [512], BF16, "ust")
    for cb in range(0, DFF, 256):
        w1, rw1 = load_w(k, w1r, W1, cb, 256, 16)
        w3, rw3 = load_w(k, w3r, W3, cb, 256, 16)
        for mo in (0, 128):
            for n in range(8):
                b1, rb1 = k.psr.next()
                b3, rb3 = k.psr.next()
                for (bank, bres, wt, wres) in ((b1, rb1, w1, rw1), (b3, rb3, w3, rw3)):
                    for kc in range(16):
                        P.op("pe", mm(bank, wt[:, kc, mo:mo + 128], hT[:, kc, n * 512:(n + 1) * 512], kc == 0, kc == 15),
                             reads=[wres, hres[n]], writes=[bres], inc=(kc == 15))
                sl, rsl = sil.next()
                us, rus = ust.next()
                P.op("act", lambda e, sl=sl, b1=b1: e.activation(sl, b1, AF.Silu), reads=[rb1], writes=[rsl])
                P.op("dve", lambda e, us=us, sl=sl, b3=b3: e.tensor_tensor(us, sl, b3, ALU.mult),
                     reads=[rsl, rb3], writes=[rus])
                P.dma("sp", uT[cb + mo:cb + mo + 128, n * 512:(n + 1) * 512], us, reads=[rus])
    P.barrier()
    A.reset(k.a0)
    TB = 1024
    uts = ring(A, 1, [44, TB], BF16, "ut")
    w2r = ring(A, 3, [44, 256], BF16, "w2")
    xr_ = ring(A, 4, [512], F32, "xi")
    for tb in range(S // TB):
        ut, _ = uts.next()
        ures = [Res("u0"), Res("u1")]
        for n in range(2):
            for half in range(2):
                kk = slice(half * 22, half * 22 + 22)
                P.dma("sp", ut[:, kk, n * 512:(n + 1) * 512],
                      uT[half * 22 * 128:(half + 1) * 22 * 128, tb * TB + n * 512:tb * TB + (n + 1) * 512].rearrange(
                          "(k p) n -> p k n", p=128),
                      writes=[ures[n]])

        def epi(c, mw, n, bank, bres, tb=tb):
            xt, rxt = xr_.next()
            tsl = slice(tb * TB + n * 512, tb * TB + (n + 1) * 512)
            P.dma("sp", xt, xinT[c:c + 128, tsl], writes=[rxt])
            P.op("dve", lambda e: e.tensor_tensor(xt, xt, bank, ALU.add), reads=[bres, rxt], writes=[rxt])
            P.dma("sp", xoutT[c:c + 128, tsl], xt, reads=[rxt])
        linear_fm(k, ut, ures, 44, W2, 0, D, w2r, 256, 2, epi)
        P.barrier()


def act(out, in_, func, **kw):
    return lambda e: e.activation(out, in_, func, **kw)


def phase_rglru(k, d):
    P, A = k.P, k.A
    A.reset(k.a0)
    pv = k.pv
    wa = A.alloc([16, 128], BF16)
    wx = A.alloc([16, 128], BF16)
    rw = Res("rgw")
    P.dma("pool", wa, k.w["rg_wa"].rearrange("d n i j -> i (d n) j"), writes=[rw])
    P.dma("pool", wx, k.w["rg_wx"].rearrange("d n i j -> i (d n) j"), writes=[rw])
    cst = A.alloc([16], F32)
    rc = Res("cst")
    lam = pv[:, PV["rg_lam"]:PV["rg_lam"] + 16]
    P.op("act", act(cst, lam, AF.Exp, scale=-1.0), reads=[k.rconst], writes=[rc])
    P.op("act", act(cst, cst, AF.Ln, bias=1.0), reads=[rc], writes=[rc])
    P.op("dve", lambda e: e.tensor_scalar_mul(cst, cst, -8.0), reads=[rc], writes=[rc])
    xpads = ring(A, 2, [S + 4], F32, "xp")
    for (xp, rxp) in xpads.items:
        P.op("pool", lambda e, xp=xp: e.memset(xp[:, 0:2], 0.0), writes=[rxp])
        P.op("pool", lambda e, xp=xp: e.memset(xp[:, S + 2:S + 4], 0.0), writes=[rxp])
    xc = A.alloc([S], F32); rxc = Res("xc")
    xcb = A.alloc([S], BF16); rxcb = Res("xcb")
    at = A.alloc([S], F32); rat = Res("a")
    ut = A.alloc([S], F32); rut = Res("u")
    tm = A.alloc([S], F32); rtm = Res("tm")
    hd = [A.alloc([S], F32) for _ in range(2)]
    rhd = [Res("hf"), Res("hb")]
    gg = A.alloc([S], F32); rgg = Res("gg")
    yb = A.alloc([S], BF16); ryb = Res("yb")
    for c in range(8):
        xp, rxp = xpads.next()
        P.dma("sp", xp[:, 2:S + 2], d["xrT"][c * 128:(c + 1) * 128, :], writes=[rxp])
        P.dma("sp", gg, d["ggT"][c * 128:(c + 1) * 128, :], writes=[rgg])
        cw = lambda j: pv[:, PV["conv_w"] + c * 4 + j:PV["conv_w"] + c * 4 + j + 1]
        cb = pv[:, PV["conv_b"] + c:PV["conv_b"] + c + 1]
        P.op("dve", lambda e, xp=xp, w0=cw(0), cb=cb: e.tensor_scalar(xc, xp[:, 0:S], w0, cb, ALU.mult, ALU.add),
             reads=[rxp, k.rconst], writes=[rxc])
        for j in (1, 2, 3):
            P.op("dve", lambda e, xp=xp, j=j, wj=cw(j): e.scalar_tensor_tensor(xc, xp[:, j:j + S], wj, xc, ALU.mult, ALU.add),
                 reads=[rxp, rxc, k.rconst], writes=[rxc])
        P.op("act", lambda e: e.copy(xcb, xc), reads=[rxc], writes=[rxcb])
        for dr in range(2):
            wi = dr * 8 + c
            ba = pv[:, PV["rg_ba"] + wi:PV["rg_ba"] + wi + 1]
            bx = pv[:, PV["rg_bx"] + wi:PV["rg_bx"] + wi + 1]
            for (wt_, bias_, dst, rdst) in ((wa, ba, at, rat), (wx, bx, ut, rut)):
                for n in range(8):
                    bank, bres = k.psr.next()
                    P.op("pe", mm(bank, wt_[:, wi, :], xcb[:, n * 512:(n + 1) * 512], True, True),
                         reads=[rw, rxcb], writes=[bres])
                    P.op("act", act(dst[:, n * 512:(n + 1) * 512], bank, AF.Sigmoid, bias=bias_),
                         reads=[bres, k.rconst], writes=[rdst])
            P.op("act", act(at, at, AF.Exp, scale=cst[:, wi:wi + 1]), reads=[rat, rc], writes=[rat])
            P.op("pool", lambda e: e.tensor_tensor(tm, at, at, ALU.mult), reads=[rat], writes=[rtm])
            P.op("act", act(tm, tm, AF.Sqrt, scale=-1.0, bias=1.0), reads=[rtm], writes=[rtm])
            P.op("dve", lambda e: e.tensor_tensor(ut, ut, xc, ALU.mult), reads=[rut, rxc], writes=[rut])
            P.op("pool", lambda e: e.tensor_tensor(ut, ut, tm, ALU.mult), reads=[rut, rtm], writes=[rut])
            if dr == 0:
                P.op("dve", lambda e, h=hd[0]: e.tensor_tensor_scan(h, at, ut, 0.0, ALU.mult, ALU.add),
                     reads=[rat, rut], writes=[rhd[0]])
            else:
                P.op("dve", lambda e, h=hd[1]: e.tensor_tensor_scan(h[:, ::-1], at[:, ::-1], ut[:, ::-1], 0.0, ALU.mult, ALU.add),
                     reads=[rat, rut], writes=[rhd[1]])
        P.op("pool", lambda e: e.tensor_tensor(hd[0], hd[0], hd[1], ALU.add), reads=[rhd[0], rhd[1]], writes=[rhd[0]])
        P.op("dve", lambda e: e.tensor_tensor(yb, hd[0], gg, ALU.mult), reads=[rhd[0], rgg], writes=[ryb])
        P.dma("sp", d["yT"][1024 + c * 128:1024 + (c + 1) * 128, :], yb, reads=[ryb])
    P.barrier()


def tbank_bf(k):
    return k.banks[6][0].bitcast(BF16), k.banks[6][1]


def phase_mlstm(k, d):
    P, A = k.P, k.A
    A.reset(k.a0)
    pv = k.pv
    acol = [A.alloc([32, 8], F32) for _ in range(2)]
    em = [A.alloc([32, 8], F32) for _ in range(2)]
    rcol = Res("cols")
    mg = A.mark()
    T = {}
    for n in ("gi", "lf", "ones", "Bf", "Bb", "mf", "mb", "af", "ab", "tmp"):
        T[n] = (A.alloc([S], F32)[0:8, :], Res(n))
    gi, rgi = T["gi"]; lf, rlf = T["lf"]; ones, rones = T["ones"]
    P.dma("sp", gi, d["gI"], writes=[rgi])
    P.dma("sp", lf, d["gF"], writes=[rlf])
    P.op("act", act(lf, lf, AF.Exp, scale=-1.0), reads=[rlf], writes=[rlf])
    P.op("act", act(lf, lf, AF.Ln, bias=1.0), reads=[rlf], writes=[rlf])
    P.op("dve", lambda e: e.tensor_scalar_mul(lf, lf, -1.0), reads=[rlf], writes=[rlf])
    P.op("pool", lambda e: e.memset(ones, 1.0), writes=[rones])
    rev = lambda a: a[:, ::-1]
    idn = lambda a: a
    for (dr, Bn, mn, an, f) in ((0, "Bf", "mf", "af", idn), (1, "Bb", "mb", "ab", rev)):
        B_, rB = T[Bn]; m_, rm = T[mn]; a_, ra = T[an]; tmp, rtmp = T["tmp"]
        P.op("dve", lambda e, B_=B_, f=f: e.tensor_tensor_scan(f(B_), f(ones), f(lf), 0.0, ALU.mult, ALU.add),
             reads=[rones, rlf], writes=[rB])
        P.op("dve", lambda e, m_=m_, f=f: e.tensor_tensor_scan(f(m_), f(lf), f(gi), -1e30, ALU.add, ALU.max),
             reads=[rlf, rgi], writes=[rm])
        P.op("pool", lambda e, a_=a_, B_=B_: e.tensor_tensor(a_, gi, B_, ALU.subtract), reads=[rgi, rB], writes=[ra])
        P.op("pool", lambda e, m_=m_, B_=B_: e.tensor_tensor(tmp, m_, B_, ALU.subtract), reads=[rm, rB], writes=[rtmp])
        P.dma("sp", d["gsc"][1, dr * 4:dr * 4 + 4, :], T["tmp"][0][dr * 4:dr * 4 + 4, :], reads=[rtmp])
        for (src, rsrc, dst, neg) in ((a_, ra, acol[dr], False), (m_, rm, em[dr], True)):
            bank, bres = k.psr.next()
            for t in range(32):
                P.op("pe", lambda e, bank=bank, src=src, t=t: e.transpose(
                    bank[:, t * 8:(t + 1) * 8], src[:, t * 128:(t + 1) * 128], k.ident_f[0:8, 0:8]),
                    reads=[rsrc, k.rconst], writes=[bres], inc=(t == 31))
            dflat = dst.rearrange("p a b -> p (a b)")
            if neg:
                P.op("act", act(dflat, bank[:, 0:256], AF.Exp, scale=-1.0), reads=[bres], writes=[rcol])
            else:
                P.op("dve", lambda e, dflat=dflat, bank=bank: e.tensor_copy(dflat, bank[:, 0:256]), reads=[bres], writes=[rcol])
    P.barrier()
    A.reset(mg)
    mask = A.alloc([2, 128], BF16); rmask = Res("mask")
    P.dma("pool", mask, k.cin["mask"], writes=[rmask])
    hf = A.alloc([32, 256], F32); rhf = [Res(f"hf{i}") for i in range(32)]
    qT = A.alloc([S], BF16); rq = Res("q")
    kT = A.alloc([S], BF16); rk = Res("k")
    va = A.alloc([32, 257], BF16); rv = Res("v")
    Mbc = A.alloc([S], F32); rM = Res("M")
    wts = ring(A, 3, [512], F32, "wt")
    pts = ring(A, 3, [512], BF16, "pt")
    ogs = ring(A, 2, [4, 256], F32, "og")
    hss = ring(A, 2, [256], F32, "hs")
    junk = A.alloc([256], F32); rjunk = Res("junk")
    ybs = ring(A, 2, [256], BF16, "yb")
    ysts = ring(A, 2, [2, 512], BF16, "yst")
    sms = ring(A, 4, [8], F32, "sm")
    accs = k.banks[0:4]
    sbanks = Ring(k.banks[4:6])
    tb_bf, rtb = tbank_bf(k)
    P.op("pool", lambda e: e.memset(va[:, :, 256:257], 1.0), writes=[rv])
    for h in range(4):
        P.dma("sp", qT, d["qT"][h * 128:(h + 1) * 128, :], writes=[rq])
        P.dma("sp", kT, d["kT"][h * 128:(h + 1) * 128, :], writes=[rk])
        P.dma("sp", va[:, :, 0:256], d["v"][:, h * 256:(h + 1) * 256].rearrange("(t p) c -> p t c", p=128), writes=[rv])
        for dr in range(2):
            r = dr * 4 + h
            P.dma("sp", Mbc, d["gsc"][1, r:r + 1, :].to_broadcast([128, S]), writes=[rM])
            for tb in range(8):
                if dr == 1:
                    og, rog = ogs.next()
                    P.dma("sp", og, d["og"][tb * 512:(tb + 1) * 512, h * 256:(h + 1) * 256].rearrange("(j p) c -> p j c", p=128),
                          writes=[rog])
                    yst, ryst = ysts.next()
                sis = range(0, 4 * tb + 4) if dr == 0 else range(4 * tb, 32)
                for si in sis:
                    if dr == 0:
                        j0, j1 = max(0, si - 4 * tb), 4
                    else:
                        j0, j1 = 0, min(3, si - 4 * tb) + 1
                    c0, c1 = j0 * 128, j1 * 128
                    q0 = tb * 512
                    sb, rsb = sbanks.next()
                    P.op("pe", mm(sb[:, c0:c1], kT[:, si * 128:(si + 1) * 128], qT[:, q0 + c0:q0 + c1], True, True),
                         reads=[rk, rq], writes=[rsb])
                    wt, rwt = wts.next()
                    P.op("act", act(wt[:, c0:c1], Mbc[:, q0 + c0:q0 + c1], AF.Exp, bias=acol[dr][:, si, r:r + 1], scale=-1.0),
                         reads=[rM, rcol], writes=[rwt])
                    jd = si - 4 * tb
                    if 0 <= jd < 4:
                        P.op("pool", lambda e, wt=wt, jd=jd: e.tensor_scalar_min(
                            wt[:, jd * 128:(jd + 1) * 128], wt[:, jd * 128:(jd + 1) * 128], 1.0),
                            reads=[rwt], writes=[rwt])
                    pt, rpt = pts.next()
                    P.op("dve", lambda e, pt=pt, sb=sb, wt=wt, c0=c0, c1=c1: e.tensor_tensor(pt[:, c0:c1], sb[:, c0:c1], wt[:, c0:c1], ALU.mult),
                         reads=[rsb, rwt], writes=[rpt])
                    if 0 <= jd < 4:
                        P.op("pool", lambda e, pt=pt, jd=jd, dr=dr: e.tensor_tensor(
                            pt[:, jd * 128:(jd + 1) * 128], pt[:, jd * 128:(jd + 1) * 128], mask[:, dr, :], ALU.mult),
                            reads=[rpt, rmask], writes=[rpt])
                    for j in range(j0, j1):
                        tj = 4 * tb + j
                        first = (si == 0) if dr == 0 else (si == tj)
                        last = (si == tj) if dr == 0 else (si == 31)
                        P.op("pe", mm(accs[j][0][:, 0:257], pt[:, j * 128:(j + 1) * 128], va[:, si, :], first, last),
                             reads=[rpt, rv], writes=[accs[j][1]], inc=last)
                for j in range(4):
                    tj = 4 * tb + j
                    acc, racc = accs[j]
                    sm, rsm = sms.next()
                    P.op("act", act(sm[:, 0:1], acc[:, 256:257], AF.Abs), reads=[racc], writes=[rsm])
                    P.op("dve", lambda e, sm=sm, tj=tj, dr=dr, r=r: e.tensor_tensor(sm[:, 1:2], sm[:, 0:1], em[dr][:, tj, r:r + 1], ALU.max),
                         reads=[rsm, rcol], writes=[rsm])
                    P.op("dve", lambda e, sm=sm: e.reciprocal(sm[:, 2:3], sm[:, 1:2]), reads=[rsm], writes=[rsm])
                    if dr == 0:
                        P.op("dve", lambda e, sm=sm, acc=acc, tj=tj: e.tensor_scalar(hf[:, tj, :], acc[:, 0:256], sm[:, 2:3], None, ALU.mult),
                             reads=[racc, rsm], writes=[rhf[tj]])
                    else:
                        hs, rhs = hss.next()
                        P.op("dve", lambda e, hs=hs, sm=sm, acc=acc, tj=tj: e.scalar_tensor_tensor(
                            hs, acc[:, 0:256], sm[:, 2:3], hf[:, tj, :], ALU.mult, ALU.add),
                            reads=[racc, rsm, rhf[tj]], writes=[rhs])
                        P.op("act", act(junk, hs, AF.Square, accum_out=sm[:, 3:4]), reads=[rhs, rsm], writes=[rjunk, rsm])
                        P.op("act", act(sm[:, 4:5], sm[:, 3:4], AF.Sqrt, scale=1.0 / 256, bias=k.eps_ap),
                             reads=[rsm, k.rconst], writes=[rsm])
                        P.op("dve", lambda e, sm=sm: e.reciprocal(sm[:, 4:5], sm[:, 4:5]), reads=[rsm], writes=[rsm])
                        yb, ryb = ybs.next()
                        P.op("dve", lambda e, yb=yb, hs=hs, sm=sm, og=og, j=j: e.scalar_tensor_tensor(
                            yb, hs, sm[:, 4:5], og[:, j, :], ALU.mult, ALU.mult),
                            reads=[rhs, rsm, rog], writes=[ryb])
                        for cc in range(2):
                            slot = (j * 2 + cc) * 128
                            P.op("pe", lambda e, yb=yb, cc=cc, slot=slot: e.transpose(
                                tb_bf[:, slot:slot + 128], yb[:, cc * 128:(cc + 1) * 128], k.ident_b),
                                reads=[ryb, k.rconst], writes=[rtb])
                            alt_copy(k, yst[:, cc, j * 128:(j + 1) * 128], tb_bf[:, slot:slot + 128], [rtb], [ryst])
                if dr == 1:
                    for cc in range(2):
                        P.dma("sp", d["yT"][h * 256 + cc * 128:h * 256 + (cc + 1) * 128, tb * 512:(tb + 1) * 512],
                              yst[:, cc, :], reads=[ryst])
    P.barrier()


def phase_qkv(k, xinT, d):
    P, A = k.P, k.A
    A.reset(k.a0)
    hT = A.alloc([16, S], BF16)
    hres = [Res(f"h{i}") for i in range(8)]
    phase_norm(k, xinT, PV["o_norm"], hT, hres)
    wring = ring(A, 3, [16, 512], BF16, "w")
    st_b = ring(A, 4, [512], BF16, "sb")
    W = k.w["o_w_qkv"]

    def epi_bf(dst, scale, row0):
        def epi(c, mw, n, bank, bres):
            st, rst = st_b.next()
            alt_copy(k, st[0:mw, :], bank[0:mw, :], [bres], [rst], scale)
            P.dma("sp", dst[c - row0:c - row0 + mw, n * 512:(n + 1) * 512], st[0:mw, :], reads=[rst])
        return epi
    linear_fm(k, hT, hres, 16, W, 0, D, wring, 512, 8, epi_bf(d["aqT"], 128.0 ** -0.5, 0))
    linear_fm(k, hT, hres, 16, W, D, D, wring, 512, 8, epi_bf(d["akT"], None, D))

    def epi_v(cb, nb, t, bank, bres):
        st, rst = st_b.next()
        alt_copy(k, st[:, 0:nb], bank[:, 0:nb], [bres], [rst])
        P.dma("sp", d["av"][t * 128:(t + 1) * 128, cb:cb + nb], st[:, 0:nb], reads=[rst])
    linear_tm(k, hT, hres, 16, W, 2 * D, D, wring, epi_v)
    P.barrier()


def phase_attn(k, d):
    P, A = k.P, k.A
    A.reset(k.a0)
    absd = A.alloc([17, 128], F32)
    mult = A.alloc([17, 128], F32)
    rcc = Res("acon")
    P.dma("sp", absd, k.cin["absd"], writes=[rcc])
    P.dma("sp", mult, k.cin["mult"], writes=[rcc])
    wr = A.alloc([17, 128], F32); rwr = Res("wr")
    qT = A.alloc([S], BF16); rq = Res("q")
    kT = A.alloc([S], BF16); rk = Res("k")
    va = A.alloc([32, 129], BF16); rv = Res("v")
    ess = ring(A, 3, [512], F32, "es")
    pts = ring(A, 3, [512], BF16, "pt")
    obs = ring(A, 2, [128], BF16, "ob")
    osts = ring(A, 2, [512], BF16, "ost")
    sms = ring(A, 4, [8], F32, "sm")
    accs = k.banks[0:4]
    sbanks = Ring(k.banks[4:6])
    tb_bf, rtb = tbank_bf(k)
    P.op("pool", lambda e: e.memset(va[:, :, 128:129], 1.0), writes=[rv])
    for h in range(16):
        slope = 2.0 ** (-8.0 * (h + 1) / 16.0)
        P.op("act", act(wr, absd, AF.Exp, scale=-slope), reads=[rcc], writes=[rwr])
        P.op("dve", lambda e: e.tensor_tensor(wr, wr, mult, ALU.mult), reads=[rwr, rcc], writes=[rwr])
        P.dma("sp", qT, d["aqT"][h * 128:(h + 1) * 128, :], writes=[rq])
        P.dma("sp", kT, d["akT"][h * 128:(h + 1) * 128, :], writes=[rk])
        P.dma("sp", va[:, :, 0:128], d["av"][:, h * 128:(h + 1) * 128].rearrange("(t p) c -> p t c", p=128), writes=[rv])
        for tb in range(8):
            ost, rost = osts.next()
            q0 = tb * 512
            for si in range(max(0, 4 * tb - 8), min(31, 4 * tb + 11) + 1):
                j0 = max(0, si - 8 - 4 * tb)
                j1 = min(3, si + 8 - 4 * tb) + 1
                c0, c1 = j0 * 128, j1 * 128
                sb, rsb = sbanks.next()
                P.op("pe", mm(sb[:, c0:c1], kT[:, si * 128:(si + 1) * 128], qT[:, q0 + c0:q0 + c1], True, True),
                     reads=[rk, rq], writes=[rsb])
                es_, res_ = ess.next()
                P.op("act", act(es_[:, c0:c1], sb[:, c0:c1], AF.Exp), reads=[rsb], writes=[res_])
                pt, rpt = pts.next()
                e0 = (4 * tb + j0) - si + 8
                nj = j1 - j0
                P.op("dve", lambda e, pt=pt, es_=es_, c0=c0, c1=c1, e0=e0, nj=nj: e.tensor_tensor(
                    pt[:, c0:c1].rearrange("p (j c) -> p j c", j=nj), es_[:, c0:c1].rearrange("p (j c) -> p j c", j=nj),
                    wr[:, e0:e0 + nj, :], ALU.mult), reads=[res_, rwr], writes=[rpt])
                for j in range(j0, j1):
                    tj = 4 * tb + j
                    first = (si == max(0, tj - 8))
                    last = (si == min(31, tj + 8))
                    P.op("pe", mm(accs[j][0][:, 0:129], pt[:, j * 128:(j + 1) * 128], va[:, si, :], first, last),
                         reads=[rpt, rv], writes=[accs[j][1]], inc=last)
            for j in range(4):
                acc, racc = accs[j]
                sm, rsm = sms.next()
                P.op("dve", lambda e, sm=sm, acc=acc: e.reciprocal(sm[:, 0:1], acc[:, 128:129]), reads=[racc], writes=[rsm])
                ob, rob = obs.next()
                P.op("act", act(ob, acc[:, 0:128], AF.Copy, scale=sm[:, 0:1]), reads=[racc, rsm], writes=[rob])
                slot = j * 128
                P.op("pe", lambda e, ob=ob, slot=slot: e.transpose(tb_bf[:, slot:slot + 128], ob, k.ident_b),
                     reads=[rob, k.rconst], writes=[rtb])
                alt_copy(k, ost[:, j * 128:(j + 1) * 128], tb_bf[:, slot:slot + 128], [rtb], [rost])
            P.dma("sp", d["aoT"][h * 128:(h + 1) * 128, q0:q0 + 512], ost, reads=[rost])
    P.barrier()


def phase_final(k, xinT, outT):
    k.A.reset(k.a0)
    phase_norm(k, xinT, PV["final_norm"], None, None, out_dram=outT)
    k.P.barrier()


def build(phases, ext_out, debug=False):
    nc = bass.Bass("TRN2", target_bir_lowering=False)
    k = K()
    k.nc = nc
    k.alt = 0

    def dram(name, shape, dt, kind="Internal"):
        if name in ext_out:
            kind = "ExternalOutput"
        return nc.dram_tensor(name, list(shape), dt, kind=kind).ap()

    ein = lambda name, shape, dt=F32: nc.dram_tensor(name, list(shape), dt, kind="ExternalInput").ap()
    k.w = {
        "e_w_in": ein("e_w_in", [D, DIN]), "e_w_out": ein("e_w_out", [D, D]),
        "rg_wa": ein("rg_wa", [2, 8, 128, 128]), "rg_wx": ein("rg_wx", [2, 8, 128, 128]),
        "o_w_qkv": ein("o_w_qkv", [D, 3 * D]), "o_w_o": ein("o_w_o", [D, D]),
        "f_w1": ein("f_w1", [2, D, DFF]), "f_w3": ein("f_w3", [2, D, DFF]), "f_w2": ein("f_w2", [2, DFF, D]),
    }
    xT = ein("xT", [D, S])
    pvec = ein("pvec", [128, NPV])
    hg_in = ein("hg_bc", [128, 1024])
    c_ident = ein("c_ident", [128, 128])
    c_mask = ein("c_mask", [128, 2, 128])
    c_absd = ein("c_absd", [128, 17, 128])
    c_mult = ein("c_mult", [128, 17, 128])
    d = {}
    for (n, shp, dt) in (("qT", [512, S], BF16), ("kT", [512, S], BF16), ("v", [S, 1024], BF16),
                         ("og", [S, 1024], F32), ("gI", [8, S], F32), ("gF", [8, S], F32),
                         ("xrT", [1024, S], F32), ("ggT", [1024, S], F32), ("yT", [D, S], BF16),
                         ("x1T", [D, S], F32), ("uT", [DFF, S], BF16), ("x2T", [D, S], F32),
                         ("aqT", [D, S], BF16), ("akT", [D, S], BF16), ("av", [S, D], BF16),
                         ("aoT", [D, S], BF16), ("x3T", [D, S], F32), ("x4T", [D, S], F32),
                         ("gsc", [3, 8, S], F32), ("outT", [D, S], F32)):
        if n in phases.get("ext_in", ()):
            d[n] = ein(n, shp, dt)
        else:
            d[n] = dram(n, shp, dt)
    k.d = d
    with ExitStack() as es:
        P = Prog(nc, es)
        k.P = P
        AW = 51 * 1024
        arena = es.enter_context(nc.sbuf_tensor("arena", [128, AW], F32))
        A = Arena(arena, AW)
        k.A = A
        banks = [(es.enter_context(nc.psum_tensor(f"ps{i}", [128, 512], F32))[:, :], Res(f"ps{i}")) for i in range(8)]
        k.banks = banks
        k.psr = Ring(banks[0:6])
        k.rconst = Res("const")
        k.pv = A.alloc([NPV], F32)
        k.hg_bc = A.alloc([1024], F32)
        k.ones_bf = A.alloc([128], BF16)
        k.eps_ap = A.alloc([1], F32)
        k.ident_f = A.alloc([128], F32)
        k.ident_b = A.alloc([128], BF16)
        P.dma("sp", k.pv, pvec, writes=[k.rconst])
        P.dma("sp", k.hg_bc, hg_in, writes=[k.rconst])
        P.dma("sp", k.ident_f, c_ident, writes=[k.rconst])
        P.dma("pool", k.ident_b, c_ident, writes=[k.rconst])
        P.op("dve", lambda e: e.memset(k.ones_bf, 1.0), writes=[k.rconst])
        P.op("dve", lambda e: e.memset(k.eps_ap, EPS), writes=[k.rconst])
        k.cin = {"mask": c_mask, "absd": c_absd, "mult": c_mult}
        P.barrier()
        k.a0 = A.mark()

        run = phases["run"]
        if "in_proj" in run:
            phase_in_proj(k, xT, d)
        if "rglru" in run:
            phase_rglru(k, d)
        if "mlstm" in run:
            phase_mlstm(k, d)
        if "out_proj0" in run:
            phase_out_proj(k, d["yT"], k.w["e_w_out"], xT, d["x1T"])
        if "ffn0" in run:
            phase_ffn(k, d["x1T"], PV["f_norm0"], k.w["f_w1"][0], k.w["f_w3"][0], k.w["f_w2"][0], d["uT"], d["x2T"])
        if "qkv" in run:
            phase_qkv(k, d["x2T"], d)
        if "attn" in run:
            phase_attn(k, d)
        if "out_proj1" in run:
            phase_out_proj(k, d["aoT"], k.w["o_w_o"], d["x2T"], d["x3T"])
        if "ffn1" in run:
            phase_ffn(k, d["x3T"], PV["f_norm1"], k.w["f_w1"][1], k.w["f_w3"][1], k.w["f_w2"][1], d["uT"], d["x4T"])
        if "final" in run:
            phase_final(k, d["x4T"], d["outT"])
        P.barrier()
        P.emit()
    return nc


def host_inputs(inp, b):
    m = {
        "xT": np.ascontiguousarray(inp["x"][b].T),
        "e_w_in": inp["e_w_in"][0], "e_w_out": inp["e_w_out"][0],
        "rg_wa": inp["e_rg_wa"][0], "rg_wx": inp["e_rg_wx"][0],
        "o_w_qkv": inp["o_w_qkv"][0], "o_w_o": inp["o_w_o"][0],
        "f_w1": inp["f_w1"], "f_w3": inp["f_w3"], "f_w2": inp["f_w2"],
        "pvec": host_pvec(inp),
        "hg_bc": np.ascontiguousarray(np.broadcast_to(np.asarray(inp["e_head_g"][0], np.float32)[None, :], (128, 1024))),
    }
    m.update(host_consts())
    return m


ALL_PHASES = ["in_proj", "rglru", "mlstm", "out_proj0", "ffn0", "qkv", "attn", "out_proj1", "ffn1", "final"]
_NC_CACHE = {}


def kernel(**inputs):
    inp = {k_: np.asarray(v) for k_, v in inputs.items()}
    if "full" not in _NC_CACHE:
        _NC_CACHE["full"] = build({"run": ALL_PHASES, "ext_in": []}, {"outT"})
    nc = _NC_CACHE["full"]
    n = 8
    shared = host_inputs(inp, 0)
    in_maps = []
    for b in range(n):
        m = dict(shared)
        m["xT"] = np.ascontiguousarray(inp["x"][b].T)
        in_maps.append(m)
    res = run_bass_kernel_spmd(nc, in_maps, core_ids=list(range(n)))
    out = np.stack([np.asarray(res.results[b]["outT"]).T for b in range(n)], 0)
    return np.ascontiguousarray(out.astype(np.float32))
```

```python
from contextlib import ExitStack
import numpy as np
import concourse.bass as bass
import concourse.mybir as mybir
from concourse.bass_utils import run_bass_kernel_spmd

F32 = mybir.dt.float32
BF16 = mybir.dt.bfloat16
AF = mybir.ActivationFunctionType
ALU = mybir.AluOpType

S = 4096
D = 2048
DFF = 5632
DIN = 5136
EPS = 1e-6
ENGS = ("pe", "act", "dve", "pool", "sp")


class Res:
    __slots__ = ("name", "w", "r")

    def __init__(self, name=""):
        self.name = name
        self.w = None
        self.r = []


class _Eng:
    def __init__(self, name, sem):
        self.name = name
        self.sem = sem
        self.cnt = 0
        self.known = {}
        self.ops = []
        self.pending = False


class Prog:
    N_DMA_SEMS = 24

    def __init__(self, nc, es):
        self.nc = nc
        self.es = es
        self.e = {}
        for n in ENGS:
            self.e[n] = _Eng(n, es.enter_context(nc.semaphore("s_" + n)))
        self.dma_sems = {}
        for q in ("sp", "pool", "act"):
            lst = [[es.enter_context(nc.semaphore(f"d_{q}{i}")), 0]
                   for i in range(self.N_DMA_SEMS if q != "act" else 4)]
            self.dma_sems[q] = [lst, 0]
        self.n_ops = 0

    def _deps(self, eng, reads, writes):
        toks = []
        for r in reads:
            if r.w is not None:
                toks.append(r.w)
        for w in writes:
            if w.w is not None:
                toks.append(w.w)
            toks.extend(w.r)
        need = {}
        for (sem, val) in toks:
            k = id(sem)
            if eng.name == "pe" and sem is eng.sem:
                continue
            if eng.known.get(k, 0) >= val:
                continue
            if k not in need or need[k][1] < val:
                need[k] = (sem, val)
        for k, (sem, val) in need.items():
            eng.known[k] = val
        return list(need.values())

    def _commit(self, tok, reads, writes):
        for r in reads:
            r.r.append(tok)
            if len(r.r) > 48:
                best = {}
                for (s, v) in r.r:
                    if id(s) not in best or best[id(s)][1] < v:
                        best[id(s)] = (s, v)
                r.r = list(best.values())
        for w in writes:
            w.w = tok
            w.r = []

    def op(self, eng, fn, reads=(), writes=(), inc=True):
        E = self.e[eng]
        waits = self._deps(E, reads, writes)
        if inc:
            E.cnt += 1
            tok = (E.sem, E.cnt)
            E.pending = False
        else:
            tok = (E.sem, E.cnt + 1)
            E.pending = True
        E.ops.append((waits, fn, (E.sem, 1) if inc else None))
        self._commit(tok, reads, writes)
        self.n_ops += 1

    def dma(self, q, out, in_, reads=(), writes=(), **kw):
        E = self.e[q]
        waits = self._deps(E, reads, writes)
        lst, idx = self.dma_sems[q]
        slot = lst[idx % len(lst)]
        self.dma_sems[q][1] = idx + 1
        sem, cur = slot
        if cur > 0 and E.known.get(id(sem), 0) < cur:
            waits.append((sem, cur))
            E.known[id(sem)] = cur
        slot[1] = cur + 16
        tok = (sem, cur + 16)

        def fn(e, out=out, in_=in_, kw=kw):
            return e.dma_start(out=out, in_=in_, **kw)

        E.ops.append((waits, fn, (sem, 16)))
        self._commit(tok, reads, writes)
        self.n_ops += 1

    def barrier(self):
        toks = []
        for n in ENGS:
            E = self.e[n]
            assert not E.pending, n
            if E.cnt > 0:
                toks.append((E.sem, E.cnt))
        for q in self.dma_sems:
            for sem, cur in self.dma_sems[q][0]:
                if cur > 0:
                    toks.append((sem, cur))
        for n in ENGS:
            E = self.e[n]
            waits = []
            for (sem, val) in toks:
                if sem is E.sem:
                    continue
                if E.known.get(id(sem), 0) < val:
                    E.known[id(sem)] = val
                    waits.append((sem, val))
            if waits:
                E.ops.append((waits, None, None))

    def emit(self):
        nc = self.nc
        for n in ENGS:
            assert not self.e[n].pending, f"engine {n} has pending un-inc'd ops"
        with nc.Block() as block:
            def run(E):
                def body(eng):
                    for (waits, fn, inc) in E.ops:
                        for (sem, val) in waits:
                            eng.wait_ge(sem, val)
                        if fn is not None:
                            ins = fn(eng)
                            if inc is not None:
                                ins.then_inc(inc[0], inc[1])
                return body
            if self.e["sp"].ops:
                block.sync(run(self.e["sp"]))
            if self.e["act"].ops:
                block.scalar(run(self.e["act"]))
            if self.e["dve"].ops:
                block.vector(run(self.e["dve"]))
            if self.e["pool"].ops:
                block.gpsimd(run(self.e["pool"]))
            if self.e["pe"].ops:
                block.tensor(run(self.e["pe"]))


class Arena:
    def __init__(self, ap, words):
        self.ap = ap
        self.words = words
        self.off = 0

    def mark(self):
        return self.off

    def reset(self, m=0):
        self.off = m

    def alloc(self, shape, dt):
        n = 1
        for s in shape:
            n *= s
        w = n if dt == F32 else (n + 1) // 2
        w = (w + 7) // 8 * 8
        assert self.off + w <= self.words, f"arena overflow {self.off}+{w}>{self.words}"
        v = self.ap[:, self.off:self.off + w]
        self.off += w
        if dt != F32:
            v = v.bitcast(dt)
        v = v[:, 0:n]
        if len(shape) == 2:
            v = v.rearrange("p (a b) -> p a b", a=shape[0])
        elif len(shape) == 3:
            v = v.rearrange("p (a b c) -> p a b c", a=shape[0], b=shape[1])
        return v


class Ring:
    def __init__(self, items):
        self.items = items
        self.i = 0

    def next(self):
        it = self.items[self.i % len(self.items)]
        self.i += 1
        return it


def ring(A, n, shape, dt, name="r"):
    return Ring([(A.alloc(shape, dt), Res(f"{name}{i}")) for i in range(n)])


PV = {}
_c = 0
for _n, _w in (("e_norm", 16), ("f_norm0", 16), ("f_norm1", 16), ("o_norm", 16), ("final_norm", 16),
               ("conv_w", 32), ("conv_b", 8), ("rg_ba", 16), ("rg_bx", 16), ("rg_lam", 16),
               ("gate_bI", 1), ("gate_bF", 1)):
    PV[_n] = _c
    _c += _w
NPV = _c


def host_pvec(inp):
    pv = np.zeros((128, NPV), np.float32)
    col = lambda v: np.ascontiguousarray(np.asarray(v, np.float32).reshape(-1, 128).T)
    pv[:, PV["e_norm"]:PV["e_norm"] + 16] = col(inp["e_norm"][0])
    pv[:, PV["f_norm0"]:PV["f_norm0"] + 16] = col(inp["f_norm"][0])
    pv[:, PV["f_norm1"]:PV["f_norm1"] + 16] = col(inp["f_norm"][1])
    pv[:, PV["o_norm"]:PV["o_norm"] + 16] = col(inp["o_norm"][0])
    pv[:, PV["final_norm"]:PV["final_norm"] + 16] = col(inp["final_norm"])
    cw = np.asarray(inp["e_conv_w"][0], np.float32)
    for c in range(8):
        for j in range(4):
            pv[:, PV["conv_w"] + c * 4 + j] = cw[j, c * 128:(c + 1) * 128]
    pv[:, PV["conv_b"]:PV["conv_b"] + 8] = col(inp["e_conv_b"][0])
    for d in range(2):
        pv[:, PV["rg_ba"] + d * 8:PV["rg_ba"] + d * 8 + 8] = col(inp["e_rg_ba"][0, d])
        pv[:, PV["rg_bx"] + d * 8:PV["rg_bx"] + d * 8 + 8] = col(inp["e_rg_bx"][0, d])
        pv[:, PV["rg_lam"] + d * 8:PV["rg_lam"] + d * 8 + 8] = col(inp["e_rg_lam"][0, d])
    gb = np.asarray(inp["e_gate_b"][0], np.float32)
    for d in range(2):
        for h in range(4):
            pv[d * 4 + h, PV["gate_bI"]] = gb[(2 * d) * 4 + h]
            pv[d * 4 + h, PV["gate_bF"]] = gb[(2 * d + 1) * 4 + h]
    return pv


def host_consts():
    ident = np.eye(128, dtype=np.float32)
    s = np.arange(128)[:, None]
    t = np.arange(128)[None, :]
    mask_f = (s <= t).astype(np.float32)
    mask_b = (s >= t).astype(np.float32)
    absd = np.zeros((128, 17, 128), np.float32)
    mult = np.zeros((128, 17, 128), np.float32)
    for e in range(17):
        d = (s - t) - (e - 8) * 128
        ad = np.abs(d)
        absd[:, e, :] = ad
        c = (ad <= 64).astype(np.float32) + ((ad <= 256) & (d % 4 == 0)) + ((ad <= 1024) & (d % 16 == 0))
        mult[:, e, :] = c
    return {"c_ident": ident, "c_mask": np.stack([mask_f, mask_b], 1).copy(),
            "c_absd": absd, "c_mult": mult}


class K:
    pass


def mm(out, lhsT, rhs, start, stop):
    return lambda e: e.matmul(out, lhsT, rhs, start=start, stop=stop)


def phase_norm(k, xT, gcol, hT, hres, out_dram=None):
    P, A = k.P, k.A
    m0 = A.mark()
    TW = 256
    xts = ring(A, 2, [16, TW], F32, "nx")
    sqs = ring(A, 2, [16, TW], BF16, "nsq")
    rss = ring(A, 2, [TW], F32, "nr")
    outs = ring(A, 2, [16, TW], F32, "no") if out_dram is not None else None
    pv = k.pv
    for j in range(S // TW):
        xt, rxt = xts.next()
        sq, rsq = sqs.next()
        rs, rrs = rss.next()
        bank, bres = k.psr.next()
        tsl = slice(j * TW, (j + 1) * TW)
        P.dma("sp", xt, xT[:, tsl].rearrange("(k p) n -> p k n", p=128), writes=[rxt])
        P.op("act", lambda e, sq=sq, xt=xt: e.activation(sq, xt, AF.Square), reads=[rxt], writes=[rsq])
        for kc in range(16):
            P.op("pe", mm(bank[:, 0:TW], k.ones_bf, sq[:, kc, :], kc == 0, kc == 15),
                 reads=[rsq, k.rconst], writes=[bres], inc=(kc == 15))
        P.op("act", lambda e, rs=rs, bank=bank: e.activation(rs, bank[:, 0:TW], AF.Sqrt, bias=k.eps_ap, scale=1.0 / D),
             reads=[bres, k.rconst], writes=[rrs])
        P.op("dve", lambda e, rs=rs: e.reciprocal(rs, rs), reads=[rrs], writes=[rrs])
        rb = rs.unsqueeze(1).to_broadcast([128, 16, TW])
        gb = pv[:, gcol:gcol + 16].unsqueeze(2).to_broadcast([128, 16, TW])
        P.op("dve", lambda e, xt=xt, rb=rb: e.tensor_tensor(xt, xt, rb, ALU.mult), reads=[rxt, rrs], writes=[rxt])
        if out_dram is None:
            P.op("pool", lambda e, xt=xt, gb=gb, tsl=tsl: e.tensor_tensor(hT[:, :, tsl], xt, gb, ALU.mult),
                 reads=[rxt, k.rconst], writes=[hres[j * TW // 512]])
        else:
            ot, rot = outs.next()
            P.op("pool", lambda e, xt=xt, gb=gb, ot=ot: e.tensor_tensor(ot, xt, gb, ALU.mult),
                 reads=[rxt, k.rconst], writes=[rot])
            P.dma("sp", out_dram[:, tsl].rearrange("(k p) n -> p k n", p=128), ot, reads=[rot])
    A.reset(m0)


def load_w(k, wring, W, c0, nb, KCn):
    wt, wres = wring.next()
    k.P.dma("pool", wt[:, 0:KCn, 0:nb], W[:, c0:c0 + nb].rearrange("(k p) n -> p k n", p=128), writes=[wres])
    return wt, wres


def linear_fm(k, hT, hres, KCn, W, c0, ncols, wring, blk, ntiles, epi, tok_off=0):
    P = k.P
    for cb in range(0, ncols, blk):
        nb = min(blk, ncols - cb)
        wt, wres = load_w(k, wring, W, c0 + cb, nb, KCn)
        for mo in range(0, nb, 128):
            mw = min(128, nb - mo)
            for n in range(ntiles):
                bank, bres = k.psr.next()
                for kc in range(KCn):
                    P.op("pe", mm(bank[0:mw, :], wt[:, kc, mo:mo + mw], hT[:, kc, n * 512:(n + 1) * 512],
                                  kc == 0, kc == KCn - 1),
                         reads=[wres, hres[n]], writes=[bres], inc=(kc == KCn - 1))
                epi(c0 + cb + mo, mw, n, bank, bres)


def linear_tm(k, hT, hres, KCn, W, c0, ncols, wring, epi):
    P = k.P
    for cb in range(0, ncols, 512):
        nb = min(512, ncols - cb)
        wt, wres = load_w(k, wring, W, c0 + cb, nb, KCn)
        for t in range(S // 128):
            bank, bres = k.psr.next()
            for kc in range(KCn):
                P.op("pe", mm(bank[:, 0:nb], hT[:, kc, t * 128:(t + 1) * 128], wt[:, kc, 0:nb],
                              kc == 0, kc == KCn - 1),
                     reads=[wres, hres[t // 4]], writes=[bres], inc=(kc == KCn - 1))
            epi(cb, nb, t, bank, bres)


def alt_copy(k, out, in_, reads, writes, scale=None):
    P = k.P
    k.alt ^= 1
    if k.alt:
        if scale is None:
            P.op("act", lambda e: e.copy(out, in_), reads=reads, writes=writes)
        else:
            P.op("act", lambda e: e.mul(out, in_, scale), reads=reads, writes=writes)
    else:
        if scale is None:
            P.op("dve", lambda e: e.tensor_copy(out, in_), reads=reads, writes=writes)
        else:
            P.op("dve", lambda e: e.tensor_scalar_mul(out, in_, scale), reads=reads, writes=writes)


def phase_in_proj(k, xT, d):
    P, A = k.P, k.A
    A.reset(k.a0)
    hT = A.alloc([16, S], BF16)
    hres = [Res(f"h{i}") for i in range(8)]
    phase_norm(k, xT, PV["e_norm"], hT, hres)
    wring = ring(A, 3, [16, 512], BF16, "w")
    st_b = ring(A, 4, [512], BF16, "sb")
    st_f = ring(A, 4, [512], F32, "sf")
    W = k.w["e_w_in"]

    def epi_bf(dst, scale, row0):
        def epi(c, mw, n, bank, bres):
            st, rst = st_b.next()
            alt_copy(k, st[0:mw, :], bank[0:mw, :], [bres], [rst], scale)
            P.dma("sp", dst[c - row0:c - row0 + mw, n * 512:(n + 1) * 512], st[0:mw, :], reads=[rst])
        return epi

    linear_fm(k, hT, hres, 16, W, 0, 512, wring, 512, 8, epi_bf(d["qT"], 128.0 ** -0.5, 0))
    linear_fm(k, hT, hres, 16, W, 512, 512, wring, 512, 8, epi_bf(d["kT"], None, 512))

    def epi_v(cb, nb, t, bank, bres):
        st, rst = st_b.next()
        alt_copy(k, st[:, 0:nb], bank[:, 0:nb], [bres], [rst])
        P.dma("sp", d["v"][t * 128:(t + 1) * 128, cb:cb + nb], st[:, 0:nb], reads=[rst])
    linear_tm(k, hT, hres, 16, W, 1024, 1024, wring, epi_v)

    def epi_o(cb, nb, t, bank, bres):
        st, rst = st_f.next()
        P.op("act", lambda e: e.activation(st[:, 0:nb], bank[:, 0:nb], AF.Sigmoid), reads=[bres], writes=[rst])
        P.op("dve", lambda e: e.tensor_tensor(st[:, 0:nb], st[:, 0:nb], k.hg_bc[:, cb:cb + nb], ALU.mult),
             reads=[rst, k.rconst], writes=[rst])
        P.dma("sp", d["og"][t * 128:(t + 1) * 128, cb:cb + nb], st[:, 0:nb], reads=[rst])
    linear_tm(k, hT, hres, 16, W, 2048, 1024, wring, epi_o)

    wgI = A.alloc([16, 8], BF16)
    wgF = A.alloc([16, 8], BF16)
    rwg = Res("wg")
    for dd in range(2):
        for (wg, off) in ((wgI, 0), (wgF, 4)):
            cc = 3072 + dd * 8 + off
            P.dma("pool", wg[:, :, dd * 4:dd * 4 + 4], W[:, cc:cc + 4].rearrange("(k p) n -> p k n", p=128),
                  writes=[rwg])
    for (wg, dst, bcol) in ((wgI, d["gI"], PV["gate_bI"]), (wgF, d["gF"], PV["gate_bF"])):
        for n in range(8):
            bank, bres = k.psr.next()
            for kc in range(16):
                P.op("pe", mm(bank[0:8, :], wg[:, kc, :], hT[:, kc, n * 512:(n + 1) * 512], kc == 0, kc == 15),
                     reads=[rwg, hres[n]], writes=[bres], inc=(kc == 15))
            st, rst = st_f.next()
            P.op("dve", lambda e, st=st, bank=bank, bcol=bcol: e.tensor_scalar(
                st[0:8, :], bank[0:8, :], k.pv[0:8, bcol:bcol + 1], None, ALU.add),
                reads=[bres, k.rconst], writes=[rst])
            P.dma("sp", dst[:, n * 512:(n + 1) * 512], st[0:8, :], reads=[rst])

    def epi_xr(c, mw, n, bank, bres):
        st, rst = st_f.next()
        alt_copy(k, st[0:mw, :], bank[0:mw, :], [bres], [rst])
        P.dma("sp", d["xrT"][c - 3088:c - 3088 + mw, n * 512:(n + 1) * 512], st[0:mw, :], reads=[rst])
    linear_fm(k, hT, hres, 16, W, 3088, 1024, wring, 512, 8, epi_xr)

    def epi_gr(c, mw, n, bank, bres):
        st, rst = st_f.next()
        P.op("act", lambda e: e.activation(st[0:mw, :], bank[0:mw, :], AF.Gelu_apprx_tanh), reads=[bres], writes=[rst])
        P.dma("sp", d["ggT"][c - 4112:c - 4112 + mw, n * 512:(n + 1) * 512], st[0:mw, :], reads=[rst])
    linear_fm(k, hT, hres, 16, W, 4112, 1024, wring, 512, 8, epi_gr)
    P.barrier()


def phase_out_proj(k, srcT, W, xinT, xoutT):
    P, A = k.P, k.A
    A.reset(k.a0)
    hT = A.alloc([16, S], BF16)
    hres = [Res(f"h{i}") for i in range(8)]
    for n in range(8):
        P.dma("sp", hT[:, :, n * 512:(n + 1) * 512],
              srcT[:, n * 512:(n + 1) * 512].rearrange("(k p) n -> p k n", p=128), writes=[hres[n]])
    wring = ring(A, 3, [16, 512], BF16, "w")
    xr_ = ring(A, 4, [512], F32, "xi")

    def epi(c, mw, n, bank, bres):
        xt, rxt = xr_.next()
        P.dma("sp", xt, xinT[c:c + 128, n * 512:(n + 1) * 512], writes=[rxt])
        P.op("dve", lambda e: e.tensor_tensor(xt, xt, bank, ALU.add), reads=[bres, rxt], writes=[rxt])
        P.dma("sp", xoutT[c:c + 128, n * 512:(n + 1) * 512], xt, reads=[rxt])
    linear_fm(k, hT, hres, 16, W, 0, D, wring, 512, 8, epi)
    P.barrier()


def phase_ffn(k, xinT, gcol, W1, W3, W2, uT, xoutT):
    P, A = k.P, k.A
    A.reset(k.a0)
    hT = A.alloc([16, S], BF16)
    hres = [Res(f"h{i}") for i in range(8)]
    phase_norm(k, xinT, gcol, hT, hres)
    m1 = A.mark()
    w1r = ring(A, 2, [16, 256], BF16, "w1")
    w3r = ring(A, 2, [16, 256], BF16, "w3")
    sil = ring(A, 3, [512], F32, "sil")
    ust = ring(A, 4, [512], BF16, "ust")
    for cb in range(0, DFF, 256):
        w1, rw1 = load_w(k, w1r, W1, cb, 256, 16)
        w3, rw3 = load_w(k, w3r, W3, cb, 256, 16)
        for mo in (0, 128):
            for n in range(8):
                b1, rb1 = k.psr.next()
                b3, rb3 = k.psr.next()
                for (bank, bres, wt, wres) in ((b1, rb1, w1, rw1), (b3, rb3, w3, rw3)):
                    for kc in range(16):
                        P.op("pe", mm(bank, wt[:, kc, mo:mo + 128], hT[:, kc, n * 512:(n + 1) * 512], kc == 0, kc == 15),
                             reads=[wres, hres[n]], writes=[bres], inc=(kc == 15))
                sl, rsl = sil.next()
                us, rus = ust.next()
                P.op("act", lambda e, sl=sl, b1=b1: e.activation(sl, b1, AF.Silu), reads=[rb1], writes=[rsl])
                P.op("dve", lambda e, us=us, sl=sl, b3=b3: e.tensor_tensor(us, sl, b3, ALU.mult),
                     reads=[rsl, rb3], writes=[rus])
                P.dma("sp", uT[cb + mo:cb + mo + 128, n * 512:(n + 1) * 512], us, reads=[rus])
    P.barrier()
    A.reset(k.a0)
    TB = 1024
    uts = ring(A, 1, [44, TB], BF16, "ut")
    w2r = ring(A, 3, [44, 256], BF16, "w2")
    xr_ = ring(A, 4, [512], F32, "xi")
    for tb in range(S // TB):
        ut, _ = uts.next()
        ures = [Res("u0"), Res("u1")]
        for n in range(2):
            for half in range(2):
                kk = slice(half * 22, half * 22 + 22)
                P.dma("sp", ut[:, kk, n * 512:(n + 1) * 512],
                      uT[half * 22 * 128:(half + 1) * 22 * 128, tb * TB + n * 512:tb * TB + (n + 1) * 512].rearrange(
                          "(k p) n -> p k n", p=128),
                      writes=[ures[n]])

        def epi(c, mw, n, bank, bres, tb=tb):
            xt, rxt = xr_.next()
            tsl = slice(tb * TB + n * 512, tb * TB + (n + 1) * 512)
            P.dma("sp", xt, xinT[c:c + 128, tsl], writes=[rxt])
            P.op("dve", lambda e: e.tensor_tensor(xt, xt, bank, ALU.add), reads=[bres, rxt], writes=[rxt])
            P.dma("sp", xoutT[c:c + 128, tsl], xt, reads=[rxt])
        linear_fm(k, ut, ures, 44, W2, 0, D, w2r, 256, 2, epi)
        P.barrier()


def act(out, in_, func, **kw):
    return lambda e: e.activation(out, in_, func, **kw)


def phase_rglru(k, d):
    P, A = k.P, k.A
    A.reset(k.a0)
    pv = k.pv
    wa = A.alloc([16, 128], BF16)
    wx = A.alloc([16, 128], BF16)
    rw = Res("rgw")
    P.dma("pool", wa, k.w["rg_wa"].rearrange("d n i j -> i (d n) j"), writes=[rw])
    P.dma("pool", wx, k.w["rg_wx"].rearrange("d n i j -> i (d n) j"), writes=[rw])
    cst = A.alloc([16], F32)
    rc = Res("cst")
    lam = pv[:, PV["rg_lam"]:PV["rg_lam"] + 16]
    P.op("act", act(cst, lam, AF.Exp, scale=-1.0), reads=[k.rconst], writes=[rc])
    P.op("act", act(cst, cst, AF.Ln, bias=1.0), reads=[rc], writes=[rc])
    P.op("dve", lambda e: e.tensor_scalar_mul(cst, cst, -8.0), reads=[rc], writes=[rc])
    xpads = ring(A, 2, [S + 4], F32, "xp")
    for (xp, rxp) in xpads.items:
        P.op("pool", lambda e, xp=xp: e.memset(xp[:, 0:2], 0.0), writes=[rxp])
        P.op("pool", lambda e, xp=xp: e.memset(xp[:, S + 2:S + 4], 0.0), writes=[rxp])
    xc = A.alloc([S], F32); rxc = Res("xc")
    xcb = A.alloc([S], BF16); rxcb = Res("xcb")
    at = A.alloc([S], F32); rat = Res("a")
    ut = A.alloc([S], F32); rut = Res("u")
    tm = A.alloc([S], F32); rtm = Res("tm")
    hd = [A.alloc([S], F32) for _ in range(2)]
    rhd = [Res("hf"), Res("hb")]
    gg = A.alloc([S], F32); rgg = Res("gg")
    yb = A.alloc([S], BF16); ryb = Res("yb")
    for c in range(8):
        xp, rxp = xpads.next()
        P.dma("sp", xp[:, 2:S + 2], d["xrT"][c * 128:(c + 1) * 128, :], writes=[rxp])
        P.dma("sp", gg, d["ggT"][c * 128:(c + 1) * 128, :], writes=[rgg])
        cw = lambda j: pv[:, PV["conv_w"] + c * 4 + j:PV["conv_w"] + c * 4 + j + 1]
        cb = pv[:, PV["conv_b"] + c:PV["conv_b"] + c + 1]
        P.op("dve", lambda e, xp=xp, w0=cw(0), cb=cb: e.tensor_scalar(xc, xp[:, 0:S], w0, cb, ALU.mult, ALU.add),
             reads=[rxp, k.rconst], writes=[rxc])
        for j in (1, 2, 3):
            P.op("dve", lambda e, xp=xp, j=j, wj=cw(j): e.scalar_tensor_tensor(xc, xp[:, j:j + S], wj, xc, ALU.mult, ALU.add),
                 reads=[rxp, rxc, k.rconst], writes=[rxc])
        P.op("act", lambda e: e.copy(xcb, xc), reads=[rxc], writes=[rxcb])
        for dr in range(2):
            wi = dr * 8 + c
            ba = pv[:, PV["rg_ba"] + wi:PV["rg_ba"] + wi + 1]
            bx = pv[:, PV["rg_bx"] + wi:PV["rg_bx"] + wi + 1]
            for (wt_, bias_, dst, rdst) in ((wa, ba, at, rat), (wx, bx, ut, rut)):
                for n in range(8):
                    bank, bres = k.psr.next()
                    P.op("pe", mm(bank, wt_[:, wi, :], xcb[:, n * 512:(n + 1) * 512], True, True),
                         reads=[rw, rxcb], writes=[bres])
                    P.op("act", act(dst[:, n * 512:(n + 1) * 512], bank, AF.Sigmoid, bias=bias_),
                         reads=[bres, k.rconst], writes=[rdst])
            P.op("act", act(at, at, AF.Exp, scale=cst[:, wi:wi + 1]), reads=[rat, rc], writes=[rat])
            P.op("pool", lambda e: e.tensor_tensor(tm, at, at, ALU.mult), reads=[rat], writes=[rtm])
            P.op("act", act(tm, tm, AF.Sqrt, scale=-1.0, bias=1.0), reads=[rtm], writes=[rtm])
            P.op("dve", lambda e: e.tensor_tensor(ut, ut, xc, ALU.mult), reads=[rut, rxc], writes=[rut])
            P.op("pool", lambda e: e.tensor_tensor(ut, ut, tm, ALU.mult), reads=[rut, rtm], writes=[rut])
            if dr == 0:
                P.op("dve", lambda e, h=hd[0]: e.tensor_tensor_scan(h, at, ut, 0.0, ALU.mult, ALU.add),
                     reads=[rat, rut], writes=[rhd[0]])
            else:
                P.op("dve", lambda e, h=hd[1]: e.tensor_tensor_scan(h[:, ::-1], at[:, ::-1], ut[:, ::-1], 0.0, ALU.mult, ALU.add),
                     reads=[rat, rut], writes=[rhd[1]])
        P.op("pool", lambda e: e.tensor_tensor(hd[0], hd[0], hd[1], ALU.add), reads=[rhd[0], rhd[1]], writes=[rhd[0]])
        P.op("dve", lambda e: e.tensor_tensor(yb, hd[0], gg, ALU.mult), reads=[rhd[0], rgg], writes=[ryb])
        P.dma("sp", d["yT"][1024 + c * 128:1024 + (c + 1) * 128, :], yb, reads=[ryb])
    P.barrier()


def tbank_bf(k):
    return k.banks[6][0].bitcast(BF16), k.banks[6][1]


def phase_mlstm(k, d):
    P, A = k.P, k.A
    A.reset(k.a0)
    pv = k.pv
    acol = [A.alloc([32, 8], F32) for _ in range(2)]
    em = [A.alloc([32, 8], F32) for _ in range(2)]
    rcol = Res("cols")
    mg = A.mark()
    T = {}
    for n in ("gi", "lf", "ones", "Bf", "Bb", "mf", "mb", "af", "ab", "tmp"):
        T[n] = (A.alloc([S], F32)[0:8, :], Res(n))
    gi, rgi = T["gi"]; lf, rlf = T["lf"]; ones, rones = T["ones"]
    P.dma("sp", gi, d["gI"], writes=[rgi])
    P.dma("sp", lf, d["gF"], writes=[rlf])
    P.op("act", act(lf, lf, AF.Exp, scale=-1.0), reads=[rlf], writes=[rlf])
    P.op("act", act(lf, lf, AF.Ln, bias=1.0), reads=[rlf], writes=[rlf])
    P.op("dve", lambda e: e.tensor_scalar_mul(lf, lf, -1.0), reads=[rlf], writes=[rlf])
    P.op("pool", lambda e: e.memset(ones, 1.0), writes=[rones])
    rev = lambda a: a[:, ::-1]
    idn = lambda a: a
    for (dr, Bn, mn, an, f) in ((0, "Bf", "mf", "af", idn), (1, "Bb", "mb", "ab", rev)):
        B_, rB = T[Bn]; m_, rm = T[mn]; a_, ra = T[an]; tmp, rtmp = T["tmp"]
        P.op("dve", lambda e, B_=B_, f=f: e.tensor_tensor_scan(f(B_), f(ones), f(lf), 0.0, ALU.mult, ALU.add),
             reads=[rones, rlf], writes=[rB])
        P.op("dve", lambda e, m_=m_, f=f: e.tensor_tensor_scan(f(m_), f(lf), f(gi), -1e30, ALU.add, ALU.max),
             reads=[rlf, rgi], writes=[rm])
        P.op("pool", lambda e, a_=a_, B_=B_: e.tensor_tensor(a_, gi, B_, ALU.subtract), reads=[rgi, rB], writes=[ra])
        P.op("pool", lambda e, m_=m_, B_=B_: e.tensor_tensor(tmp, m_, B_, ALU.subtract), reads=[rm, rB], writes=[rtmp])
        P.dma("sp", d["gsc"][1, dr * 4:dr * 4 + 4, :], T["tmp"][0][dr * 4:dr * 4 + 4, :], reads=[rtmp])
        for (src, rsrc, dst, neg) in ((a_, ra, acol[dr], False), (m_, rm, em[dr], True)):
            bank, bres = k.psr.next()
            for t in range(32):
                P.op("pe", lambda e, bank=bank, src=src, t=t: e.transpose(
                    bank[:, t * 8:(t + 1) * 8], src[:, t * 128:(t + 1) * 128], k.ident_f[0:8, 0:8]),
                    reads=[rsrc, k.rconst], writes=[bres], inc=(t == 31))
            dflat = dst.rearrange("p a b -> p (a b)")
            if neg:
                P.op("act", act(dflat, bank[:, 0:256], AF.Exp, scale=-1.0), reads=[bres], writes=[rcol])
            else:
                P.op("dve", lambda e, dflat=dflat, bank=bank: e.tensor_copy(dflat, bank[:, 0:256]), reads=[bres], writes=[rcol])
    P.barrier()
    A.reset(mg)
    mask = A.alloc([2, 128], BF16); rmask = Res("mask")
    P.dma("pool", mask, k.cin["mask"], writes=[rmask])
    hf = A.alloc([32, 256], F32); rhf = [Res(f"hf{i}") for i in range(32)]
    qTs = ring(A, 1, [S], BF16, "q")
    kTs = ring(A, 1, [S], BF16, "k")
    vas = ring(A, 2, [32, 257], BF16, "v")
    Mbcs = ring(A, 2, [S], F32, "M")
    Mdgs = ring(A, 2, [32, 128], F32, "Md")
    negm = A.alloc([2, 128], F32); rnegm = Res("negm")
    P.dma("sp", negm, k.cin["mask"], writes=[rnegm])
    P.op("dve", lambda e: e.tensor_scalar(negm, negm, -1.0e4, 1.0e4, ALU.mult, ALU.add), reads=[rnegm], writes=[rnegm])
    wts = ring(A, 6, [512], F32, "wt")
    pts = ring(A, 7, [512], BF16, "pt")
    ogs = ring(A, 2, [4, 256], F32, "og")
    hss = ring(A, 4, [256], F32, "hs")
    junk = A.alloc([256], F32); rjunk = Res("junk")
    ybs = ring(A, 8, [256], BF16, "yb")
    ysts = ring(A, 2, [2, 512], BF16, "yst")
    sms = ring(A, 8, [8], F32, "sm")
    accs = k.banks[0:4]
    sbanks = Ring(k.banks[4:7])
    tb_bf = k.banks[7][0].bitcast(BF16)
    rtb = k.banks[7][1]
    for (va, rv) in vas.items:
        P.op("pool", lambda e, va=va: e.memset(va[:, :, 256:257], 1.0), writes=[rv])
    items = []
    for h in range(4):
        qT, rq = qTs.next()
        kT, rk = kTs.next()
        va, rv = vas.next()

        def head_load(h=h, qT=qT, rq=rq, kT=kT, rk=rk, va=va, rv=rv):
            P.dma("sp", qT, d["qT"][h * 128:(h + 1) * 128, :], writes=[rq])
            P.dma("sp", kT, d["kT"][h * 128:(h + 1) * 128, :], writes=[rk])
            P.dma("sp", va[:, :, 0:256], d["v"][:, h * 256:(h + 1) * 256].rearrange("(t p) c -> p t c", p=128), writes=[rv])
        for dr in range(2):
            r = dr * 4 + h
            Mbc, rM = Mbcs.next()
            Mdg, rMd = Mdgs.next()

            def dir_load(r=r, Mbc=Mbc, rM=rM, Mdg=Mdg, rMd=rMd, dr=dr):
                P.dma("sp", Mbc, d["gsc"][1, r:r + 1, :].to_broadcast([128, S]), writes=[rM])
                P.op("pool", lambda e: e.tensor_tensor(
                    Mdg, Mbc.rearrange("p (t c) -> p t c", t=32),
                    negm[:, dr, :].unsqueeze(1).to_broadcast([128, 32, 128]), ALU.add),
                    reads=[rM, rnegm], writes=[rMd])
            pre = [dir_load] + ([head_load] if dr == 0 else [])
            for tb in range(8):
                q0 = tb * 512
                blk = {}
                if dr == 1:
                    def blk_load(tb=tb, h=h, blk=blk):
                        og, rog = ogs.next()
                        P.dma("sp", og, d["og"][tb * 512:(tb + 1) * 512, h * 256:(h + 1) * 256].rearrange("(j p) c -> p j c", p=128),
                              writes=[rog])
                        blk["og"] = (og, rog)
                    pre = pre + [blk_load]
                sis = list(range(0, 4 * tb + 4)) if dr == 0 else list(range(4 * tb, 32))
                for si in sis:
                    if dr == 0:
                        j0, j1 = max(0, si - 4 * tb), 4
                    else:
                        j0, j1 = 0, min(3, si - 4 * tb) + 1
                    c0, c1 = j0 * 128, j1 * 128
                    st = {}

                    def Afn(si=si, c0=c0, c1=c1, tb=tb, q0=q0, st=st, pre=pre, dr=dr, r=r,
                            qT=qT, rq=rq, kT=kT, rk=rk, Mbc=Mbc, rM=rM, Mdg=Mdg, rMd=rMd):
                        for f in pre:
                            f()
                        sb, rsb = sbanks.next()
                        P.op("pe", mm(sb[:, c0:c1], kT[:, si * 128:(si + 1) * 128], qT[:, q0 + c0:q0 + c1], True, True),
                             reads=[rk, rq], writes=[rsb])
                        wt, rwt = wts.next()
                        jd = si - 4 * tb
                        bias_ = acol[dr][:, si, r:r + 1]
                        if 0 <= jd < 4:
                            d0, d1 = jd * 128, (jd + 1) * 128
                            P.op("act", act(wt[:, d0:d1], Mdg[:, si, :], AF.Exp, bias=bias_, scale=-1.0),
                                 reads=[rMd, rcol], writes=[rwt])
                            o0, o1 = (d1, c1) if dr == 0 else (c0, d0)
                            if o1 > o0:
                                P.op("act", act(wt[:, o0:o1], Mbc[:, q0 + o0:q0 + o1], AF.Exp, bias=bias_, scale=-1.0),
                                     reads=[rM, rcol], writes=[rwt])
                        else:
                            P.op("act", act(wt[:, c0:c1], Mbc[:, q0 + c0:q0 + c1], AF.Exp, bias=bias_, scale=-1.0),
                                 reads=[rM, rcol], writes=[rwt])
                        pt, rpt = pts.next()
                        P.op("dve", lambda e: e.tensor_tensor(pt[:, c0:c1], sb[:, c0:c1], wt[:, c0:c1], ALU.mult),
                             reads=[rsb, rwt], writes=[rpt])
                        st["pt"] = (pt, rpt)
                    pre = []
                    is_last = (si == sis[-1])
                    e2box = {}

                    def Bfn(si=si, j0=j0, j1=j1, tb=tb, st=st, is_last=is_last, va=va, rv=rv, dr=dr, r=r, blk=blk, e2box=e2box):
                        pt, rpt = st["pt"]
                        for j in range(j0, j1):
                            tj = 4 * tb + j
                            first = (si == 0) if dr == 0 else (si == tj)
                            last = (si == tj) if dr == 0 else (si == 31)
                            P.op("pe", mm(accs[j][0][:, 0:257], pt[:, j * 128:(j + 1) * 128], va[:, si, :], first, last),
                                 reads=[rpt, rv], writes=[accs[j][1]], inc=last)
                        if not is_last:
                            return
                        ybl = []
                        for j in range(4):
                            tj = 4 * tb + j
                            acc, racc = accs[j]
                            sm, rsm = sms.next()
                            P.op("act", act(sm[:, 0:1], acc[:, 256:257], AF.Abs), reads=[racc], writes=[rsm])
                            P.op("dve", lambda e, sm=sm, tj=tj: e.tensor_tensor(sm[:, 1:2], sm[:, 0:1], em[dr][:, tj, r:r + 1], ALU.max),
                                 reads=[rsm, rcol], writes=[rsm])
                            P.op("dve", lambda e, sm=sm: e.reciprocal(sm[:, 2:3], sm[:, 1:2]), reads=[rsm], writes=[rsm])
                            if dr == 0:
                                P.op("act", act(hf[:, tj, :], acc[:, 0:256], AF.Copy, scale=sm[:, 2:3]),
                                     reads=[racc, rsm], writes=[rhf[tj]])
                            else:
                                og, rog = blk["og"]
                                hs, rhs = hss.next()
                                P.op("dve", lambda e, hs=hs, sm=sm, acc=acc, tj=tj: e.scalar_tensor_tensor(
                                    hs, acc[:, 0:256], sm[:, 2:3], hf[:, tj, :], ALU.mult, ALU.add),
                                    reads=[racc, rsm, rhf[tj]], writes=[rhs])
                                P.op("act", act(junk, hs, AF.Square, accum_out=sm[:, 3:4]), reads=[rhs, rsm], writes=[rjunk, rsm])
                                P.op("act", act(sm[:, 4:5], sm[:, 3:4], AF.Sqrt, scale=1.0 / 256, bias=k.eps_ap),
                                     reads=[rsm, k.rconst], writes=[rsm])
                                P.op("dve", lambda e, sm=sm: e.reciprocal(sm[:, 4:5], sm[:, 4:5]), reads=[rsm], writes=[rsm])
                                yb, ryb = ybs.next()
                                P.op("dve", lambda e, yb=yb, hs=hs, sm=sm, og=og, j=j: e.scalar_tensor_tensor(
                                    yb, hs, sm[:, 4:5], og[:, j, :], ALU.mult, ALU.mult),
                                    reads=[rhs, rsm, rog], writes=[ryb])
                                ybl.append((yb, ryb))
                        e2box["ybl"] = ybl

                    def E2fn(tb=tb, h=h, e2box=e2box):
                        yst, ryst = ysts.next()
                        for j, (yb, ryb) in enumerate(e2box["ybl"]):
                            for cc in range(2):
                                slot = (cc * 4 + j) * 128
                                P.op("pe", lambda e, yb=yb, cc=cc, slot=slot: e.transpose(
                                    tb_bf[:, slot:slot + 128], yb[:, cc * 128:(cc + 1) * 128], k.ident_b),
                                    reads=[ryb, k.rconst], writes=[rtb])
                        for cc in range(2):
                            alt_copy(k, yst[:, cc, :], tb_bf[:, cc * 512:(cc + 1) * 512], [rtb], [ryst])
                            P.dma("sp", d["yT"][h * 256 + cc * 128:h * 256 + (cc + 1) * 128, tb * 512:(tb + 1) * 512],
                                  yst[:, cc, :], reads=[ryst])
                    items.append((Afn, Bfn, E2fn if (is_last and dr == 1) else None))
    run_pipeline(items, L=PIPE_L, defer=PIPE_L + 2)
    P.barrier()


def phase_qkv(k, xinT, d):
    P, A = k.P, k.A
    A.reset(k.a0)
    hT = A.alloc([16, S], BF16)
    hres = [Res(f"h{i}") for i in range(8)]
    phase_norm(k, xinT, PV["o_norm"], hT, hres)
    wring = ring(A, 3, [16, 512], BF16, "w")
    st_b = ring(A, 4, [512], BF16, "sb")
    W = k.w["o_w_qkv"]

    def epi_bf(dst, scale, row0):
        def epi(c, mw, n, bank, bres):
            st, rst = st_b.next()
            alt_copy(k, st[0:mw, :], bank[0:mw, :], [bres], [rst], scale)
            P.dma("sp", dst[c - row0:c - row0 + mw, n * 512:(n + 1) * 512], st[0:mw, :], reads=[rst])
        return epi
    linear_fm(k, hT, hres, 16, W, 0, D, wring, 512, 8, epi_bf(d["aqT"], 128.0 ** -0.5, 0))
    linear_fm(k, hT, hres, 16, W, D, D, wring, 512, 8, epi_bf(d["akT"], None, D))

    def epi_v(cb, nb, t, bank, bres):
        st, rst = st_b.next()
        alt_copy(k, st[:, 0:nb], bank[:, 0:nb], [bres], [rst])
        P.dma("sp", d["av"][t * 128:(t + 1) * 128, cb:cb + nb], st[:, 0:nb], reads=[rst])
    linear_tm(k, hT, hres, 16, W, 2 * D, D, wring, epi_v)
    P.barrier()


PIPE_L = 4


def run_pipeline(items, L=2, defer=3):
    n = len(items)
    pend = []
    for i in range(min(L, n)):
        items[i][0]()
    for i in range(n):
        if i + L < n:
            items[i + L][0]()
        items[i][1]()
        pend = [(c - 1, f) for (c, f) in pend]
        for (c, f) in pend:
            if c <= 0:
                f()
        pend = [(c, f) for (c, f) in pend if c > 0]
        if items[i][2] is not None:
            pend.append((defer, items[i][2]))
    for (c, f) in pend:
        f()


def phase_attn(k, d):
    P, A = k.P, k.A
    A.reset(k.a0)
    absd = A.alloc([17, 128], F32)
    mult = A.alloc([17, 128], F32)
    rcc = Res("acon")
    P.dma("sp", absd, k.cin["absd"], writes=[rcc])
    P.dma("sp", mult, k.cin["mult"], writes=[rcc])
    wrs = ring(A, 2, [17, 128], F32, "wr")
    qTs = ring(A, 2, [S], BF16, "q")
    kTs = ring(A, 2, [S], BF16, "k")
    vas = ring(A, 2, [32, 129], BF16, "v")
    ess = ring(A, 6, [512], F32, "es")
    pts = ring(A, 7, [512], BF16, "pt")
    obs = ring(A, 4, [128], BF16, "ob")
    osts = ring(A, 2, [512], BF16, "ost")
    sms = ring(A, 8, [8], F32, "sm")
    accs = k.banks[0:4]
    sbanks = Ring(k.banks[4:7])
    tb_all = k.banks[7][0].bitcast(BF16)
    tbs = Ring([(tb_all[:, 0:512], Res("tb0")), (tb_all[:, 512:1024], Res("tb1"))])
    for (va, rv) in vas.items:
        P.op("pool", lambda e, va=va: e.memset(va[:, :, 128:129], 1.0), writes=[rv])
    items = []
    for h in range(16):
        slope = 2.0 ** (-8.0 * (h + 1) / 16.0)
        wr, rwr = wrs.next()
        qT, rq = qTs.next()
        kT, rk = kTs.next()
        va, rv = vas.next()

        def head_load(h=h, slope=slope, wr=wr, rwr=rwr, qT=qT, rq=rq, kT=kT, rk=rk, va=va, rv=rv):
            P.op("act", act(wr, absd, AF.Exp, scale=-slope), reads=[rcc], writes=[rwr])
            P.op("pool", lambda e: e.tensor_tensor(wr, wr, mult, ALU.mult), reads=[rwr, rcc], writes=[rwr])
            P.dma("sp", qT, d["aqT"][h * 128:(h + 1) * 128, :], writes=[rq])
            P.dma("sp", kT, d["akT"][h * 128:(h + 1) * 128, :], writes=[rk])
            P.dma("sp", va[:, :, 0:128], d["av"][:, h * 128:(h + 1) * 128].rearrange("(t p) c -> p t c", p=128), writes=[rv])
        first_of_head = True
        for tb in range(8):
            q0 = tb * 512
            sis = list(range(max(0, 4 * tb - 8), min(31, 4 * tb + 11) + 1))
            for si in sis:
                j0 = max(0, si - 8 - 4 * tb)
                j1 = min(3, si + 8 - 4 * tb) + 1
                c0, c1 = j0 * 128, j1 * 128
                st = {}

                def Afn(si=si, c0=c0, c1=c1, j0=j0, j1=j1, tb=tb, q0=q0, st=st, hl=(head_load if first_of_head else None),
                        qT=qT, rq=rq, kT=kT, rk=rk, wr=wr, rwr=rwr):
                    if hl is not None:
                        hl()
                    sb, rsb = sbanks.next()
                    P.op("pe", mm(sb[:, c0:c1], kT[:, si * 128:(si + 1) * 128], qT[:, q0 + c0:q0 + c1], True, True),
                         reads=[rk, rq], writes=[rsb])
                    es_, res_ = ess.next()
                    P.op("act", act(es_[:, c0:c1], sb[:, c0:c1], AF.Exp), reads=[rsb], writes=[res_])
                    pt, rpt = pts.next()
                    e0 = (4 * tb + j0) - si + 8
                    nj = j1 - j0
                    P.op("dve", lambda e: e.tensor_tensor(
                        pt[:, c0:c1].rearrange("p (j c) -> p j c", j=nj), es_[:, c0:c1].rearrange("p (j c) -> p j c", j=nj),
                        wr[:, e0:e0 + nj, :], ALU.mult), reads=[res_, rwr], writes=[rpt])
                    st["pt"] = (pt, rpt)
                first_of_head = False
                is_last = (si == sis[-1])
                e2box = {}

                def Bfn(si=si, j0=j0, j1=j1, tb=tb, q0=q0, st=st, is_last=is_last, va=va, rv=rv, h=h, e2box=e2box):
                    pt, rpt = st["pt"]
                    for j in range(j0, j1):
                        tj = 4 * tb + j
                        first = (si == max(0, tj - 8))
                        last = (si == min(31, tj + 8))
                        P.op("pe", mm(accs[j][0][:, 0:129], pt[:, j * 128:(j + 1) * 128], va[:, si, :], first, last),
                             reads=[rpt, rv], writes=[accs[j][1]], inc=last)
                    if is_last:
                        obl = []
                        for j in range(4):
                            acc, racc = accs[j]
                            sm, rsm = sms.next()
                            P.op("dve", lambda e, sm=sm, acc=acc: e.reciprocal(sm[:, 0:1], acc[:, 128:129]), reads=[racc], writes=[rsm])
                            ob, rob = obs.next()
                            P.op("act", act(ob, acc[:, 0:128], AF.Copy, scale=sm[:, 0:1]), reads=[racc, rsm], writes=[rob])
                            obl.append((ob, rob))
                        e2box["obl"] = obl

                def E2fn(tb=tb, q0=q0, h=h, e2box=e2box):
                    ost, rost = osts.next()
                    tbk, rtb = tbs.next()
                    for j, (ob, rob) in enumerate(e2box["obl"]):
                        P.op("pe", lambda e, ob=ob, j=j: e.transpose(tbk[:, j * 128:(j + 1) * 128], ob, k.ident_b),
                             reads=[rob, k.rconst], writes=[rtb])
                    alt_copy(k, ost, tbk, [rtb], [rost])
                    P.dma("sp", d["aoT"][h * 128:(h + 1) * 128, q0:q0 + 512], ost, reads=[rost])
                items.append((Afn, Bfn, E2fn if is_last else None))
    run_pipeline(items, L=PIPE_L, defer=PIPE_L + 2)
    P.barrier()


def phase_final(k, xinT, outT):
    k.A.reset(k.a0)
    phase_norm(k, xinT, PV["final_norm"], None, None, out_dram=outT)
    k.P.barrier()


def build(phases, ext_out, debug=False):
    nc = bass.Bass("TRN2", target_bir_lowering=False)
    k = K()
    k.nc = nc
    k.alt = 0

    def dram(name, shape, dt, kind="Internal"):
        if name in ext_out:
            kind = "ExternalOutput"
        return nc.dram_tensor(name, list(shape), dt, kind=kind).ap()

    ein = lambda name, shape, dt=F32: nc.dram_tensor(name, list(shape), dt, kind="ExternalInput").ap()
    k.w = {
        "e_w_in": ein("e_w_in", [D, DIN]), "e_w_out": ein("e_w_out", [D, D]),
        "rg_wa": ein("rg_wa", [2, 8, 128, 128]), "rg_wx": ein("rg_wx", [2, 8, 128, 128]),
        "o_w_qkv": ein("o_w_qkv", [D, 3 * D]), "o_w_o": ein("o_w_o", [D, D]),
        "f_w1": ein("f_w1", [2, D, DFF]), "f_w3": ein("f_w3", [2, D, DFF]), "f_w2": ein("f_w2", [2, DFF, D]),
    }
    xT = ein("xT", [D, S])
    pvec = ein("pvec", [128, NPV])
    hg_in = ein("hg_bc", [128, 1024])
    c_ident = ein("c_ident", [128, 128])
    c_mask = ein("c_mask", [128, 2, 128])
    c_absd = ein("c_absd", [128, 17, 128])
    c_mult = ein("c_mult", [128, 17, 128])
    d = {}
    for (n, shp, dt) in (("qT", [512, S], BF16), ("kT", [512, S], BF16), ("v", [S, 1024], BF16),
                         ("og", [S, 1024], F32), ("gI", [8, S], F32), ("gF", [8, S], F32),
                         ("xrT", [1024, S], F32), ("ggT", [1024, S], F32), ("yT", [D, S], BF16),
                         ("x1T", [D, S], F32), ("uT", [DFF, S], BF16), ("x2T", [D, S], F32),
                         ("aqT", [D, S], BF16), ("akT", [D, S], BF16), ("av", [S, D], BF16),
                         ("aoT", [D, S], BF16), ("x3T", [D, S], F32), ("x4T", [D, S], F32),
                         ("gsc", [3, 8, S], F32), ("outT", [D, S], F32)):
        if n in phases.get("ext_in", ()):
            d[n] = ein(n, shp, dt)
        else:
            d[n] = dram(n, shp, dt)
    k.d = d
    with ExitStack() as es:
        P = Prog(nc, es)
        k.P = P
        AW = 51 * 1024
        arena = es.enter_context(nc.sbuf_tensor("arena", [128, AW], F32))
        A = Arena(arena, AW)
        k.A = A
        banks = [(es.enter_context(nc.psum_tensor(f"ps{i}", [128, 512], F32))[:, :], Res(f"ps{i}")) for i in range(8)]
        k.banks = banks
        k.psr = Ring(banks[0:6])
        k.rconst = Res("const")
        k.pv = A.alloc([NPV], F32)
        k.hg_bc = A.alloc([1024], F32)
        k.ones_bf = A.alloc([128], BF16)
        k.eps_ap = A.alloc([1], F32)
        k.ident_f = A.alloc([128], F32)
        k.ident_b = A.alloc([128], BF16)
        P.dma("sp", k.pv, pvec, writes=[k.rconst])
        P.dma("sp", k.hg_bc, hg_in, writes=[k.rconst])
        P.dma("sp", k.ident_f, c_ident, writes=[k.rconst])
        P.dma("pool", k.ident_b, c_ident, writes=[k.rconst])
        P.op("dve", lambda e: e.memset(k.ones_bf, 1.0), writes=[k.rconst])
        P.op("dve", lambda e: e.memset(k.eps_ap, EPS), writes=[k.rconst])
        k.cin = {"mask": c_mask, "absd": c_absd, "mult": c_mult}
        P.barrier()
        k.a0 = A.mark()

        run = phases["run"]
        if "in_proj" in run:
            phase_in_proj(k, xT, d)
        if "rglru" in run:
            phase_rglru(k, d)
        if "mlstm" in run:
            phase_mlstm(k, d)
        if "out_proj0" in run:
            phase_out_proj(k, d["yT"], k.w["e_w_out"], xT, d["x1T"])
        if "ffn0" in run:
            phase_ffn(k, d["x1T"], PV["f_norm0"], k.w["f_w1"][0], k.w["f_w3"][0], k.w["f_w2"][0], d["uT"], d["x2T"])
        if "qkv" in run:
            phase_qkv(k, d["x2T"], d)
        if "attn" in run:
            phase_attn(k, d)
        if "out_proj1" in run:
            phase_out_proj(k, d["aoT"], k.w["o_w_o"], d["x2T"], d["x3T"])
        if "ffn1" in run:
            phase_ffn(k, d["x3T"], PV["f_norm1"], k.w["f_w1"][1], k.w["f_w3"][1], k.w["f_w2"][1], d["uT"], d["x4T"])
        if "final" in run:
            phase_final(k, d["x4T"], d["outT"])
        P.barrier()
        P.emit()
    return nc


def host_inputs(inp, b):
    m = {
        "xT": np.ascontiguousarray(inp["x"][b].T),
        "e_w_in": inp["e_w_in"][0], "e_w_out": inp["e_w_out"][0],
        "rg_wa": inp["e_rg_wa"][0], "rg_wx": inp["e_rg_wx"][0],
        "o_w_qkv": inp["o_w_qkv"][0], "o_w_o": inp["o_w_o"][0],
        "f_w1": inp["f_w1"], "f_w3": inp["f_w3"], "f_w2": inp["f_w2"],
        "pvec": host_pvec(inp),
        "hg_bc": np.ascontiguousarray(np.broadcast_to(np.asarray(inp["e_head_g"][0], np.float32)[None, :], (128, 1024))),
    }
    m.update(host_consts())
    return m


ALL_PHASES = ["in_proj", "rglru", "mlstm", "out_proj0", "ffn0", "qkv", "attn", "out_proj1", "ffn1", "final"]
_NC_CACHE = {}


def kernel(**inputs):
    inp = {k_: np.asarray(v) for k_, v in inputs.items()}
    if "full" not in _NC_CACHE:
        _NC_CACHE["full"] = build({"run": ALL_PHASES, "ext_in": []}, {"outT"})
    nc = _NC_CACHE["full"]
    n = 8
    shared = host_inputs(inp, 0)
    in_maps = []
    for b in range(n):
        m = dict(shared)
        m["xT"] = np.ascontiguousarray(inp["x"][b].T)
        in_maps.append(m)
    res = run_bass_kernel_spmd(nc, in_maps, core_ids=list(range(n)))
    out = np.stack([np.asarray(res.results[b]["outT"]).T for b in range(n)], 0)
    return np.ascontiguousarray(out.astype(np.float32))
```

```python
from contextlib import ExitStack
import numpy as np
import concourse.bass as bass
import concourse.mybir as mybir
from concourse.bass_utils import run_bass_kernel_spmd

F32 = mybir.dt.float32
BF16 = mybir.dt.bfloat16
AF = mybir.ActivationFunctionType
ALU = mybir.AluOpType

S = 4096
D = 2048
DFF = 5632
DIN = 5136
EPS = 1e-6
ENGS = ("pe", "act", "dve", "pool", "sp")


class Res:
    __slots__ = ("name", "w", "r")

    def __init__(self, name=""):
        self.name = name
        self.w = None
        self.r = []


class _Eng:
    def __init__(self, name, sem):
        self.name = name
        self.sem = sem
        self.cnt = 0
        self.known = {}
        self.ops = []
        self.pending = False


class Prog:
    N_DMA_SEMS = 24

    def __init__(self, nc, es):
        self.nc = nc
        self.es = es
        self.e = {}
        for n in ENGS:
            self.e[n] = _Eng(n, es.enter_context(nc.semaphore("s_" + n)))
        self.dma_sems = {}
        for q in ("sp", "pool", "act"):
            lst = [[es.enter_context(nc.semaphore(f"d_{q}{i}")), 0]
                   for i in range(self.N_DMA_SEMS if q != "act" else 12)]
            self.dma_sems[q] = [lst, 0]
        self.n_ops = 0

    def _deps(self, eng, reads, writes, noself=False):
        toks = []
        for r in reads:
            if r.w is not None:
                toks.append(r.w)
        for w in writes:
            if w.w is not None:
                toks.append(w.w)
            toks.extend(w.r)
        need = {}
        for (sem, val) in toks:
            k = id(sem)
            if (eng.name == "pe" or noself) and sem is eng.sem:
                continue
            if eng.known.get(k, 0) >= val:
                continue
            if k not in need or need[k][1] < val:
                need[k] = (sem, val)
        for k, (sem, val) in need.items():
            eng.known[k] = val
        return list(need.values())

    def _commit(self, tok, reads, writes):
        for r in reads:
            r.r.append(tok)
            if len(r.r) > 48:
                best = {}
                for (s, v) in r.r:
                    if id(s) not in best or best[id(s)][1] < v:
                        best[id(s)] = (s, v)
                r.r = list(best.values())
        for w in writes:
            w.w = tok
            w.r = []

    def op(self, eng, fn, reads=(), writes=(), inc=True, noself=False):
        E = self.e[eng]
        waits = self._deps(E, reads, writes, noself)
        if inc:
            E.cnt += 1
            tok = (E.sem, E.cnt)
            E.pending = False
        else:
            tok = (E.sem, E.cnt + 1)
            E.pending = True
        E.ops.append((waits, fn, (E.sem, 1) if inc else None))
        self._commit(tok, reads, writes)
        self.n_ops += 1

    def dma(self, q, out, in_, reads=(), writes=(), **kw):
        E = self.e[q]
        waits = self._deps(E, reads, writes)
        lst, idx = self.dma_sems[q]
        slot = lst[idx % len(lst)]
        self.dma_sems[q][1] = idx + 1
        sem, cur = slot
        if cur > 0 and E.known.get(id(sem), 0) < cur:
            waits.append((sem, cur))
            E.known[id(sem)] = cur
        slot[1] = cur + 16
        tok = (sem, cur + 16)

        def fn(e, out=out, in_=in_, kw=kw):
            return e.dma_start(out=out, in_=in_, **kw)

        E.ops.append((waits, fn, (sem, 16)))
        self._commit(tok, reads, writes)
        self.n_ops += 1

    def barrier(self):
        toks = []
        for n in ENGS:
            E = self.e[n]
            assert not E.pending, n
            if E.cnt > 0:
                toks.append((E.sem, E.cnt))
        for q in self.dma_sems:
            for sem, cur in self.dma_sems[q][0]:
                if cur > 0:
                    toks.append((sem, cur))
        for n in ENGS:
            E = self.e[n]
            waits = []
            for (sem, val) in toks:
                if sem is E.sem:
                    continue
                if E.known.get(id(sem), 0) < val:
                    E.known[id(sem)] = val
                    waits.append((sem, val))
            if waits:
                E.ops.append((waits, None, None))

    def emit(self):
        nc = self.nc
        for n in ENGS:
            assert not self.e[n].pending, f"engine {n} has pending un-inc'd ops"
        with nc.Block() as block:
            def run(E):
                def body(eng):
                    for (waits, fn, inc) in E.ops:
                        for (sem, val) in waits:
                            eng.wait_ge(sem, val)
                        if fn is not None:
                            ins = fn(eng)
                            if inc is not None:
                                ins.then_inc(inc[0], inc[1])
                return body
            if self.e["sp"].ops:
                block.sync(run(self.e["sp"]))
            if self.e["act"].ops:
                block.scalar(run(self.e["act"]))
            if self.e["dve"].ops:
                block.vector(run(self.e["dve"]))
            if self.e["pool"].ops:
                block.gpsimd(run(self.e["pool"]))
            if self.e["pe"].ops:
                block.tensor(run(self.e["pe"]))


class Arena:
    def __init__(self, ap, words):
        self.ap = ap
        self.words = words
        self.off = 0

    def mark(self):
        return self.off

    def reset(self, m=0):
        self.off = m

    def alloc(self, shape, dt):
        n = 1
        for s in shape:
            n *= s
        w = n if dt == F32 else (n + 1) // 2
        w = (w + 7) // 8 * 8
        assert self.off + w <= self.words, f"arena overflow {self.off}+{w}>{self.words}"
        v = self.ap[:, self.off:self.off + w]
        self.off += w
        if dt != F32:
            v = v.bitcast(dt)
        v = v[:, 0:n]
        if len(shape) == 2:
            v = v.rearrange("p (a b) -> p a b", a=shape[0])
        elif len(shape) == 3:
            v = v.rearrange("p (a b c) -> p a b c", a=shape[0], b=shape[1])
        return v


class Ring:
    def __init__(self, items):
        self.items = items
        self.i = 0

    def next(self):
        it = self.items[self.i % len(self.items)]
        self.i += 1
        return it


def ring(A, n, shape, dt, name="r"):
    return Ring([(A.alloc(shape, dt), Res(f"{name}{i}")) for i in range(n)])


PV = {}
_c = 0
for _n, _w in (("e_norm", 16), ("f_norm0", 16), ("f_norm1", 16), ("o_norm", 16), ("final_norm", 16),
               ("conv_w", 32), ("conv_b", 8), ("rg_ba", 16), ("rg_bx", 16), ("rg_lam", 16),
               ("gate_bI", 1), ("gate_bF", 1)):
    PV[_n] = _c
    _c += _w
NPV = _c


def host_pvec(inp):
    pv = np.zeros((128, NPV), np.float32)
    col = lambda v: np.ascontiguousarray(np.asarray(v, np.float32).reshape(-1, 128).T)
    pv[:, PV["e_norm"]:PV["e_norm"] + 16] = col(inp["e_norm"][0])
    pv[:, PV["f_norm0"]:PV["f_norm0"] + 16] = col(inp["f_norm"][0])
    pv[:, PV["f_norm1"]:PV["f_norm1"] + 16] = col(inp["f_norm"][1])
    pv[:, PV["o_norm"]:PV["o_norm"] + 16] = col(inp["o_norm"][0])
    pv[:, PV["final_norm"]:PV["final_norm"] + 16] = col(inp["final_norm"])
    cw = np.asarray(inp["e_conv_w"][0], np.float32)
    for c in range(8):
        for j in range(4):
            pv[:, PV["conv_w"] + c * 4 + j] = cw[j, c * 128:(c + 1) * 128]
    pv[:, PV["conv_b"]:PV["conv_b"] + 8] = col(inp["e_conv_b"][0])
    for d in range(2):
        pv[:, PV["rg_ba"] + d * 8:PV["rg_ba"] + d * 8 + 8] = col(inp["e_rg_ba"][0, d])
        pv[:, PV["rg_bx"] + d * 8:PV["rg_bx"] + d * 8 + 8] = col(inp["e_rg_bx"][0, d])
        pv[:, PV["rg_lam"] + d * 8:PV["rg_lam"] + d * 8 + 8] = col(inp["e_rg_lam"][0, d])
    gb = np.asarray(inp["e_gate_b"][0], np.float32)
    for d in range(2):
        for h in range(4):
            pv[d * 4 + h, PV["gate_bI"]] = gb[(2 * d) * 4 + h]
            pv[d * 4 + h, PV["gate_bF"]] = gb[(2 * d + 1) * 4 + h]
    return pv


def host_consts():
    ident = np.eye(128, dtype=np.float32)
    s = np.arange(128)[:, None]
    t = np.arange(128)[None, :]
    mask_f = (s <= t).astype(np.float32)
    mask_b = (s >= t).astype(np.float32)
    absd = np.zeros((128, 17, 128), np.float32)
    mult = np.zeros((128, 17, 128), np.float32)
    for e in range(17):
        d = (s - t) - (e - 8) * 128
        ad = np.abs(d)
        absd[:, e, :] = ad
        c = (ad <= 64).astype(np.float32) + ((ad <= 256) & (d % 4 == 0)) + ((ad <= 1024) & (d % 16 == 0))
        mult[:, e, :] = c
    return {"c_ident": ident, "c_mask": np.stack([mask_f, mask_b], 1).copy(),
            "c_absd": absd, "c_mult": mult}


class K:
    pass


def mm(out, lhsT, rhs, start, stop):
    return lambda e: e.matmul(out, lhsT, rhs, start=start, stop=stop)


def phase_norm(k, xT, gcol, hT, hres, out_dram=None):
    P, A = k.P, k.A
    m0 = A.mark()
    TW = 256
    xts = ring(A, 3, [16, TW], F32, "nx")
    sqs = ring(A, 2, [16, TW], BF16, "nsq")
    rss = ring(A, 3, [TW], F32, "nr")
    outs = ring(A, 2, [16, TW], F32, "no") if out_dram is not None else None
    pv = k.pv
    for j in range(S // TW):
        xt, rxt = xts.next()
        sq, rsq = sqs.next()
        rs, rrs = rss.next()
        bank, bres = k.psr.next()
        tsl = slice(j * TW, (j + 1) * TW)
        P.dma("sp", xt, xT[:, tsl].rearrange("(k p) n -> p k n", p=128), writes=[rxt])
        P.op("act", lambda e, sq=sq, xt=xt: e.activation(sq, xt, AF.Square), reads=[rxt], writes=[rsq])
        for kc in range(16):
            P.op("pe", mm(bank[:, 0:TW], k.ones_bf, sq[:, kc, :], kc == 0, kc == 15),
                 reads=[rsq, k.rconst], writes=[bres], inc=(kc == 15))
        P.op("act", lambda e, rs=rs, bank=bank: e.activation(rs, bank[:, 0:TW], AF.Sqrt, bias=k.eps_ap, scale=1.0 / D),
             reads=[bres, k.rconst], writes=[rrs])
        P.op("dve", lambda e, rs=rs: e.reciprocal(rs, rs), reads=[rrs], writes=[rrs])
        if out_dram is None:
            for kc in range(16):
                P.op("dve", lambda e, xt=xt, rs=rs, kc=kc, tsl=tsl: e.scalar_tensor_tensor(
                    hT[:, kc, tsl], xt[:, kc, :], pv[:, gcol + kc:gcol + kc + 1], rs, ALU.mult, ALU.mult),
                    reads=[rxt, rrs, k.rconst], writes=[hres[j * TW // 512]], noself=(kc > 0))
        else:
            ot, rot = outs.next()
            for kc in range(16):
                P.op("dve", lambda e, xt=xt, rs=rs, kc=kc, ot=ot: e.scalar_tensor_tensor(
                    ot[:, kc, :], xt[:, kc, :], pv[:, gcol + kc:gcol + kc + 1], rs, ALU.mult, ALU.mult),
                    reads=[rxt, rrs, k.rconst], writes=[rot], noself=(kc > 0))
            P.dma("sp", out_dram[:, tsl].rearrange("(k p) n -> p k n", p=128), ot, reads=[rot])
    P.barrier()
    A.reset(m0)


def load_w(k, wring, W, c0, nb, KCn):
    wt, wres = wring.next()
    k.P.dma("pool", wt[:, 0:KCn, 0:nb], W[:, c0:c0 + nb].rearrange("(k p) n -> p k n", p=128), writes=[wres])
    return wt, wres


def linear_fm(k, hT, hres, KCn, W, c0, ncols, wring, blk, ntiles, epi, tok_off=0, pre=None):
    P = k.P
    for cb in range(0, ncols, blk):
        nb = min(blk, ncols - cb)
        wt, wres = load_w(k, wring, W, c0 + cb, nb, KCn)
        for mo in range(0, nb, 128):
            mw = min(128, nb - mo)
            for n in range(ntiles):
                if pre is not None:
                    pre(c0 + cb + mo, mw, n)
                bank, bres = k.psr.next()
                for kc in range(KCn):
                    P.op("pe", mm(bank[0:mw, :], wt[:, kc, mo:mo + mw], hT[:, kc, n * 512:(n + 1) * 512],
                                  kc == 0, kc == KCn - 1),
                         reads=[wres, hres[n]], writes=[bres], inc=(kc == KCn - 1))
                epi(c0 + cb + mo, mw, n, bank, bres)


def linear_tm(k, hT, hres, KCn, W, c0, ncols, wring, epi):
    P = k.P
    for cb in range(0, ncols, 512):
        nb = min(512, ncols - cb)
        wt, wres = load_w(k, wring, W, c0 + cb, nb, KCn)
        for t in range(S // 128):
            bank, bres = k.psr.next()
            for kc in range(KCn):
                P.op("pe", mm(bank[:, 0:nb], hT[:, kc, t * 128:(t + 1) * 128], wt[:, kc, 0:nb],
                              kc == 0, kc == KCn - 1),
                     reads=[wres, hres[t // 4]], writes=[bres], inc=(kc == KCn - 1))
            epi(cb, nb, t, bank, bres)


def alt_copy(k, out, in_, reads, writes, scale=None):
    P = k.P
    k.alt ^= 1
    if k.alt:
        if scale is None:
            P.op("act", lambda e: e.copy(out, in_), reads=reads, writes=writes)
        else:
            P.op("act", lambda e: e.mul(out, in_, scale), reads=reads, writes=writes)
    else:
        if scale is None:
            P.op("dve", lambda e: e.tensor_copy(out, in_), reads=reads, writes=writes)
        else:
            P.op("dve", lambda e: e.tensor_scalar_mul(out, in_, scale), reads=reads, writes=writes)


def phase_in_proj(k, xT, d):
    P, A = k.P, k.A
    A.reset(k.a0)
    hT = A.alloc([16, S], BF16)
    hres = [Res(f"h{i}") for i in range(8)]
    phase_norm(k, xT, PV["e_norm"], hT, hres)
    wring = ring(A, 3, [16, 512], BF16, "w")
    st_b = ring(A, 4, [512], BF16, "sb")
    st_f = ring(A, 4, [512], F32, "sf")
    W = k.w["e_w_in"]

    def epi_bf(dst, scale, row0):
        def epi(c, mw, n, bank, bres):
            st, rst = st_b.next()
            alt_copy(k, st[0:mw, :], bank[0:mw, :], [bres], [rst], scale)
            P.dma("sp", dst[c - row0:c - row0 + mw, n * 512:(n + 1) * 512], st[0:mw, :], reads=[rst])
        return epi

    linear_fm(k, hT, hres, 16, W, 0, 512, wring, 512, 8, epi_bf(d["qT"], 128.0 ** -0.5, 0))
    linear_fm(k, hT, hres, 16, W, 512, 512, wring, 512, 8, epi_bf(d["kT"], None, 512))

    def epi_v(cb, nb, t, bank, bres):
        st, rst = st_b.next()
        alt_copy(k, st[:, 0:nb], bank[:, 0:nb], [bres], [rst])
        P.dma("sp", d["v"][t * 128:(t + 1) * 128, cb:cb + nb], st[:, 0:nb], reads=[rst])
    linear_tm(k, hT, hres, 16, W, 1024, 1024, wring, epi_v)

    def epi_o(cb, nb, t, bank, bres):
        st, rst = st_f.next()
        P.op("act", lambda e: e.activation(st[:, 0:nb], bank[:, 0:nb], AF.Sigmoid), reads=[bres], writes=[rst])
        P.op("dve", lambda e: e.tensor_tensor(st[:, 0:nb], st[:, 0:nb], k.hg_bc[:, cb:cb + nb], ALU.mult),
             reads=[rst, k.rconst], writes=[rst])
        P.dma("sp", d["og"][t * 128:(t + 1) * 128, cb:cb + nb], st[:, 0:nb], reads=[rst])
    linear_tm(k, hT, hres, 16, W, 2048, 1024, wring, epi_o)

    wgI = A.alloc([16, 8], BF16)
    wgF = A.alloc([16, 8], BF16)
    rwg = Res("wg")
    for dd in range(2):
        for (wg, off) in ((wgI, 0), (wgF, 4)):
            cc = 3072 + dd * 8 + off
            P.dma("pool", wg[:, :, dd * 4:dd * 4 + 4], W[:, cc:cc + 4].rearrange("(k p) n -> p k n", p=128),
                  writes=[rwg])
    for (wg, dst, bcol) in ((wgI, d["gI"], PV["gate_bI"]), (wgF, d["gF"], PV["gate_bF"])):
        for n in range(8):
            bank, bres = k.psr.next()
            for kc in range(16):
                P.op("pe", mm(bank[0:8, :], wg[:, kc, :], hT[:, kc, n * 512:(n + 1) * 512], kc == 0, kc == 15),
                     reads=[rwg, hres[n]], writes=[bres], inc=(kc == 15))
            st, rst = st_f.next()
            P.op("dve", lambda e, st=st, bank=bank, bcol=bcol: e.tensor_scalar(
                st[0:8, :], bank[0:8, :], k.pv[0:8, bcol:bcol + 1], None, ALU.add),
                reads=[bres, k.rconst], writes=[rst])
            P.dma("sp", dst[:, n * 512:(n + 1) * 512], st[0:8, :], reads=[rst])

    def epi_xr(c, mw, n, bank, bres):
        st, rst = st_f.next()
        alt_copy(k, st[0:mw, :], bank[0:mw, :], [bres], [rst])
        P.dma("sp", d["xrT"][c - 3088:c - 3088 + mw, n * 512:(n + 1) * 512], st[0:mw, :], reads=[rst])
    linear_fm(k, hT, hres, 16, W, 3088, 1024, wring, 512, 8, epi_xr)

    def epi_gr(c, mw, n, bank, bres):
        st, rst = st_f.next()
        P.op("act", lambda e: e.activation(st[0:mw, :], bank[0:mw, :], AF.Gelu_apprx_tanh), reads=[bres], writes=[rst])
        P.dma("sp", d["ggT"][c - 4112:c - 4112 + mw, n * 512:(n + 1) * 512], st[0:mw, :], reads=[rst])
    linear_fm(k, hT, hres, 16, W, 4112, 1024, wring, 512, 8, epi_gr)
    P.barrier()


def phase_out_proj(k, srcT, W, xinT, xoutT):
    P, A = k.P, k.A
    A.reset(k.a0)
    hT = A.alloc([16, S], BF16)
    hres = [Res(f"h{i}") for i in range(8)]
    for n in range(8):
        P.dma("sp", hT[:, :, n * 512:(n + 1) * 512],
              srcT[:, n * 512:(n + 1) * 512].rearrange("(k p) n -> p k n", p=128), writes=[hres[n]])
    wring = ring(A, 3, [16, 512], BF16, "w")
    xr_ = ring(A, 8, [512], F32, "xi")
    fifo = []

    def pre(c, mw, n):
        xt, rxt = xr_.next()
        P.dma("act", xt, xinT[c:c + 128, n * 512:(n + 1) * 512], writes=[rxt])
        fifo.append((xt, rxt))

    def epi(c, mw, n, bank, bres):
        xt, rxt = fifo.pop(0)
        P.op("dve", lambda e: e.tensor_tensor(xt, xt, bank, ALU.add), reads=[bres, rxt], writes=[rxt])
        P.dma("sp", xoutT[c:c + 128, n * 512:(n + 1) * 512], xt, reads=[rxt])
    linear_fm(k, hT, hres, 16, W, 0, D, wring, 512, 8, epi, pre=pre)
    P.barrier()


def phase_ffn(k, xinT, gcol, W1, W3, W2, uT, xoutT):
    P, A = k.P, k.A
    A.reset(k.a0)
    hT = A.alloc([16, S], BF16)
    hres = [Res(f"h{i}") for i in range(8)]
    phase_norm(k, xinT, gcol, hT, hres)
    m1 = A.mark()
    w2b = k.d["w2b"]
    w1r = ring(A, 2, [16, 256], BF16, "w1")
    w3r = ring(A, 2, [16, 256], BF16, "w3")
    sil = ring(A, 3, [512], F32, "sil")
    ust = ring(A, 4, [512], BF16, "ust")
    for cb in range(0, DFF, 256):
        w1, rw1 = load_w(k, w1r, W1, cb, 256, 16)
        w3, rw3 = load_w(k, w3r, W3, cb, 256, 16)
        if (cb // 256) % 2 == 1:
            r0 = (cb // 512) * 512
            P.dma("pool", w2b[r0:r0 + 512, :], W2[r0:r0 + 512, :])
        for mo in (0, 128):
            for n in range(8):
                b1, rb1 = k.psr.next()
                b3, rb3 = k.psr.next()
                for (bank, bres, wt, wres) in ((b1, rb1, w1, rw1), (b3, rb3, w3, rw3)):
                    for kc in range(16):
                        P.op("pe", mm(bank, wt[:, kc, mo:mo + 128], hT[:, kc, n * 512:(n + 1) * 512], kc == 0, kc == 15),
                             reads=[wres, hres[n]], writes=[bres], inc=(kc == 15))
                sl, rsl = sil.next()
                us, rus = ust.next()
                P.op("act", lambda e, sl=sl, b1=b1: e.activation(sl, b1, AF.Silu), reads=[rb1], writes=[rsl])
                P.op("dve", lambda e, us=us, sl=sl, b3=b3: e.tensor_tensor(us, sl, b3, ALU.mult),
                     reads=[rsl, rb3], writes=[rus])
                P.dma("sp", uT[cb + mo:cb + mo + 128, n * 512:(n + 1) * 512], us, reads=[rus])
    P.barrier()
    A.reset(k.a0)
    TB = 1024
    uhs = ring(A, 3, [44, 512], BF16, "uh")
    w2r = ring(A, 2, [44, 256], BF16, "w2")
    xr_ = ring(A, 8, [512], F32, "xi")
    fifo = []

    def load_half(tb, n, buf, res):
        for half in range(2):
            kk = slice(half * 22, half * 22 + 22)
            P.dma("sp", buf[:, kk, :],
                  uT[half * 22 * 128:(half + 1) * 22 * 128, tb * TB + n * 512:tb * TB + (n + 1) * 512].rearrange(
                      "(k p) n -> p k n", p=128), writes=[res])
    nblk = S // TB
    halves = [[uhs.next(), uhs.next()] for _ in range(nblk)]
    load_half(0, 0, *halves[0][0])
    for tb in range(nblk):
        load_half(tb, 1, *halves[tb][1])
        if tb + 1 < nblk:
            load_half(tb + 1, 0, *halves[tb + 1][0])

        class _U:
            def __init__(self, hs):
                self.hs = hs

            def __getitem__(self, key):
                p, kc, tsl = key
                n = tsl.start // 512
                return self.hs[n][0][p, kc, :]
        ut = _U(halves[tb])
        ures = [halves[tb][0][1], halves[tb][1][1]]

        def pre(c, mw, n, tb=tb):
            xt, rxt = xr_.next()
            tsl = slice(tb * TB + n * 512, tb * TB + (n + 1) * 512)
            P.dma("act", xt, xinT[c:c + 128, tsl], writes=[rxt])
            fifo.append((xt, rxt))

        def epi(c, mw, n, bank, bres, tb=tb):
            xt, rxt = fifo.pop(0)
            tsl = slice(tb * TB + n * 512, tb * TB + (n + 1) * 512)
            P.op("dve", lambda e: e.tensor_tensor(xt, xt, bank, ALU.add), reads=[bres, rxt], writes=[rxt])
            P.dma("sp", xoutT[c:c + 128, tsl], xt, reads=[rxt])
        linear_fm(k, ut, ures, 44, w2b, 0, D, w2r, 256, 2, epi, pre=pre)
    P.barrier()


def act(out, in_, func, **kw):
    return lambda e: e.activation(out, in_, func, **kw)


def phase_rglru(k, d):
    P, A = k.P, k.A
    A.reset(k.a0)
    pv = k.pv
    wa = A.alloc([16, 128], BF16)
    wx = A.alloc([16, 128], BF16)
    rw = Res("rgw")
    P.dma("pool", wa, k.w["rg_wa"].rearrange("d n i j -> i (d n) j"), writes=[rw])
    P.dma("pool", wx, k.w["rg_wx"].rearrange("d n i j -> i (d n) j"), writes=[rw])
    cst = A.alloc([16], F32)
    rc = Res("cst")
    lam = pv[:, PV["rg_lam"]:PV["rg_lam"] + 16]
    P.op("act", act(cst, lam, AF.Exp, scale=-1.0), reads=[k.rconst], writes=[rc])
    P.op("act", act(cst, cst, AF.Ln, bias=1.0), reads=[rc], writes=[rc])
    P.op("dve", lambda e: e.tensor_scalar_mul(cst, cst, -8.0), reads=[rc], writes=[rc])
    xpads = ring(A, 2, [S + 4], F32, "xp")
    for (xp, rxp) in xpads.items:
        P.op("pool", lambda e, xp=xp: e.memset(xp[:, 0:2], 0.0), writes=[rxp])
        P.op("pool", lambda e, xp=xp: e.memset(xp[:, S + 2:S + 4], 0.0), writes=[rxp])
    xc = A.alloc([S], F32); rxc = Res("xc")
    xcb = A.alloc([S], BF16); rxcb = Res("xcb")
    at = A.alloc([S], F32); rat = Res("a")
    ut = A.alloc([S], F32); rut = Res("u")
    tm = A.alloc([S], F32); rtm = Res("tm")
    hd = [A.alloc([S], F32) for _ in range(2)]
    rhd = [Res("hf"), Res("hb")]
    gg = A.alloc([S], F32); rgg = Res("gg")
    yb = A.alloc([S], BF16); ryb = Res("yb")
    for c in range(8):
        xp, rxp = xpads.next()
        P.dma("sp", xp[:, 2:S + 2], d["xrT"][c * 128:(c + 1) * 128, :], writes=[rxp])
        P.dma("sp", gg, d["ggT"][c * 128:(c + 1) * 128, :], writes=[rgg])
        cw = lambda j: pv[:, PV["conv_w"] + c * 4 + j:PV["conv_w"] + c * 4 + j + 1]
        cb = pv[:, PV["conv_b"] + c:PV["conv_b"] + c + 1]
        P.op("dve", lambda e, xp=xp, w0=cw(0), cb=cb: e.tensor_scalar(xc, xp[:, 0:S], w0, cb, ALU.mult, ALU.add),
             reads=[rxp, k.rconst], writes=[rxc])
        for j in (1, 2, 3):
            P.op("dve", lambda e, xp=xp, j=j, wj=cw(j): e.scalar_tensor_tensor(xc, xp[:, j:j + S], wj, xc, ALU.mult, ALU.add),
                 reads=[rxp, rxc, k.rconst], writes=[rxc])
        P.op("act", lambda e: e.copy(xcb, xc), reads=[rxc], writes=[rxcb])
        for dr in range(2):
            wi = dr * 8 + c
            ba = pv[:, PV["rg_ba"] + wi:PV["rg_ba"] + wi + 1]
            bx = pv[:, PV["rg_bx"] + wi:PV["rg_bx"] + wi + 1]
            for (wt_, bias_, dst, rdst) in ((wa, ba, at, rat), (wx, bx, ut, rut)):
                for n in range(8):
                    bank, bres = k.psr.next()
                    P.op("pe", mm(bank, wt_[:, wi, :], xcb[:, n * 512:(n + 1) * 512], True, True),
                         reads=[rw, rxcb], writes=[bres])
                    P.op("act", act(dst[:, n * 512:(n + 1) * 512], bank, AF.Sigmoid, bias=bias_),
                         reads=[bres, k.rconst], writes=[rdst])
            P.op("act", act(at, at, AF.Exp, scale=cst[:, wi:wi + 1]), reads=[rat, rc], writes=[rat])
            P.op("pool", lambda e: e.tensor_tensor(tm, at, at, ALU.mult), reads=[rat], writes=[rtm])
            P.op("act", act(tm, tm, AF.Sqrt, scale=-1.0, bias=1.0), reads=[rtm], writes=[rtm])
            P.op("dve", lambda e: e.tensor_tensor(ut, ut, xc, ALU.mult), reads=[rut, rxc], writes=[rut])
            P.op("pool", lambda e: e.tensor_tensor(ut, ut, tm, ALU.mult), reads=[rut, rtm], writes=[rut])
            if dr == 0:
                P.op("dve", lambda e, h=hd[0]: e.tensor_tensor_scan(h, at, ut, 0.0, ALU.mult, ALU.add),
                     reads=[rat, rut], writes=[rhd[0]])
            else:
                P.op("dve", lambda e, h=hd[1]: e.tensor_tensor_scan(h[:, ::-1], at[:, ::-1], ut[:, ::-1], 0.0, ALU.mult, ALU.add),
                     reads=[rat, rut], writes=[rhd[1]])
        P.op("pool", lambda e: e.tensor_tensor(hd[0], hd[0], hd[1], ALU.add), reads=[rhd[0], rhd[1]], writes=[rhd[0]])
        P.op("dve", lambda e: e.tensor_tensor(yb, hd[0], gg, ALU.mult), reads=[rhd[0], rgg], writes=[ryb])
        P.dma("sp", d["yT"][1024 + c * 128:1024 + (c + 1) * 128, :], yb, reads=[ryb])
    P.barrier()


def tbank_bf(k):
    return k.banks[6][0].bitcast(BF16), k.banks[6][1]


def phase_mlstm(k, d):
    P, A = k.P, k.A
    A.reset(k.a0)
    pv = k.pv
    acol = [A.alloc([32, 8], F32) for _ in range(2)]
    em = [A.alloc([32, 8], F32) for _ in range(2)]
    rcol = Res("cols")
    mg = A.mark()
    T = {}
    for n in ("gi", "lf", "ones", "Bf", "Bb", "mf", "mb", "af", "ab", "tmp"):
        T[n] = (A.alloc([S], F32)[0:8, :], Res(n))
    gi, rgi = T["gi"]; lf, rlf = T["lf"]; ones, rones = T["ones"]
    P.dma("sp", gi, d["gI"], writes=[rgi])
    P.dma("sp", lf, d["gF"], writes=[rlf])
    P.op("act", act(lf, lf, AF.Exp, scale=-1.0), reads=[rlf], writes=[rlf])
    P.op("act", act(lf, lf, AF.Ln, bias=1.0), reads=[rlf], writes=[rlf])
    P.op("dve", lambda e: e.tensor_scalar_mul(lf, lf, -1.0), reads=[rlf], writes=[rlf])
    P.op("pool", lambda e: e.memset(ones, 1.0), writes=[rones])
    rev = lambda a: a[:, ::-1]
    idn = lambda a: a
    for (dr, Bn, mn, an, f) in ((0, "Bf", "mf", "af", idn), (1, "Bb", "mb", "ab", rev)):
        B_, rB = T[Bn]; m_, rm = T[mn]; a_, ra = T[an]; tmp, rtmp = T["tmp"]
        P.op("dve", lambda e, B_=B_, f=f: e.tensor_tensor_scan(f(B_), f(ones), f(lf), 0.0, ALU.mult, ALU.add),
             reads=[rones, rlf], writes=[rB])
        P.op("dve", lambda e, m_=m_, f=f: e.tensor_tensor_scan(f(m_), f(lf), f(gi), -1e30, ALU.add, ALU.max),
             reads=[rlf, rgi], writes=[rm])
        P.op("pool", lambda e, a_=a_, B_=B_: e.tensor_tensor(a_, gi, B_, ALU.subtract), reads=[rgi, rB], writes=[ra])
        P.op("pool", lambda e, m_=m_, B_=B_: e.tensor_tensor(tmp, m_, B_, ALU.subtract), reads=[rm, rB], writes=[rtmp])
        P.dma("sp", d["gsc"][1, dr * 4:dr * 4 + 4, :], T["tmp"][0][dr * 4:dr * 4 + 4, :], reads=[rtmp])
        for (src, rsrc, dst, neg) in ((a_, ra, acol[dr], False), (m_, rm, em[dr], True)):
            bank, bres = k.psr.next()
            for t in range(32):
                P.op("pe", lambda e, bank=bank, src=src, t=t: e.transpose(
                    bank[:, t * 8:(t + 1) * 8], src[:, t * 128:(t + 1) * 128], k.ident_f[0:8, 0:8]),
                    reads=[rsrc, k.rconst], writes=[bres], inc=(t == 31))
            dflat = dst.rearrange("p a b -> p (a b)")
            if neg:
                P.op("act", act(dflat, bank[:, 0:256], AF.Exp, scale=-1.0), reads=[bres], writes=[rcol])
            else:
                P.op("dve", lambda e, dflat=dflat, bank=bank: e.tensor_copy(dflat, bank[:, 0:256]), reads=[bres], writes=[rcol])
    P.barrier()
    A.reset(mg)
    mask = A.alloc([2, 128], BF16); rmask = Res("mask")
    P.dma("pool", mask, k.cin["mask"], writes=[rmask])
    hf = A.alloc([32, 256], F32); rhf = [Res(f"hf{i}") for i in range(32)]
    qTs = ring(A, 1, [S], BF16, "q")
    kTs = ring(A, 1, [S], BF16, "k")
    vas = ring(A, 2, [32, 257], BF16, "v")
    Mbcs = ring(A, 2, [S], F32, "M")
    Mdgs = ring(A, 2, [32, 128], F32, "Md")
    negm = A.alloc([2, 128], F32); rnegm = Res("negm")
    P.dma("sp", negm, k.cin["mask"], writes=[rnegm])
    P.op("dve", lambda e: e.tensor_scalar(negm, negm, -1.0e4, 1.0e4, ALU.mult, ALU.add), reads=[rnegm], writes=[rnegm])
    wts = ring(A, 6, [512], F32, "wt")
    pts = ring(A, 7, [512], BF16, "pt")
    ogs = ring(A, 2, [4, 256], F32, "og")
    hss = ring(A, 4, [256], F32, "hs")
    junk = A.alloc([256], F32); rjunk = Res("junk")
    ybs = ring(A, 8, [256], BF16, "yb")
    ysts = ring(A, 2, [2, 512], BF16, "yst")
    sms = ring(A, 8, [8], F32, "sm")
    accs = k.banks[0:4]
    sbanks = Ring(k.banks[4:7])
    tb_bf = k.banks[7][0].bitcast(BF16)
    rtb = k.banks[7][1]
    for (va, rv) in vas.items:
        P.op("pool", lambda e, va=va: e.memset(va[:, :, 256:257], 1.0), writes=[rv])
    items = []
    for h in range(4):
        qT, rq = qTs.next()
        kT, rk = kTs.next()
        va, rv = vas.next()

        def head_load(h=h, qT=qT, rq=rq, kT=kT, rk=rk, va=va, rv=rv):
            P.dma("sp", qT, d["qT"][h * 128:(h + 1) * 128, :], writes=[rq])
            P.dma("sp", kT, d["kT"][h * 128:(h + 1) * 128, :], writes=[rk])
            P.dma("sp", va[:, :, 0:256], d["v"][:, h * 256:(h + 1) * 256].rearrange("(t p) c -> p t c", p=128), writes=[rv])
        for dr in range(2):
            r = dr * 4 + h
            Mbc, rM = Mbcs.next()
            Mdg, rMd = Mdgs.next()

            def dir_load(r=r, Mbc=Mbc, rM=rM, Mdg=Mdg, rMd=rMd, dr=dr):
                P.dma("sp", Mbc, d["gsc"][1, r:r + 1, :].to_broadcast([128, S]), writes=[rM])
                P.op("pool", lambda e: e.tensor_tensor(
                    Mdg, Mbc.rearrange("p (t c) -> p t c", t=32),
                    negm[:, dr, :].unsqueeze(1).to_broadcast([128, 32, 128]), ALU.add),
                    reads=[rM, rnegm], writes=[rMd])
            pre = [dir_load] + ([head_load] if dr == 0 else [])
            for tb in range(8):
                q0 = tb * 512
                blk = {}
                if dr == 1:
                    def blk_load(tb=tb, h=h, blk=blk):
                        og, rog = ogs.next()
                        P.dma("sp", og, d["og"][tb * 512:(tb + 1) * 512, h * 256:(h + 1) * 256].rearrange("(j p) c -> p j c", p=128),
                              writes=[rog])
                        blk["og"] = (og, rog)
                    pre = pre + [blk_load]
                sis = list(range(0, 4 * tb + 4)) if dr == 0 else list(range(4 * tb, 32))
                for si in sis:
                    if dr == 0:
                        j0, j1 = max(0, si - 4 * tb), 4
                    else:
                        j0, j1 = 0, min(3, si - 4 * tb) + 1
                    c0, c1 = j0 * 128, j1 * 128
                    st = {}

                    def Afn(si=si, c0=c0, c1=c1, tb=tb, q0=q0, st=st, pre=pre, dr=dr, r=r,
                            qT=qT, rq=rq, kT=kT, rk=rk, Mbc=Mbc, rM=rM, Mdg=Mdg, rMd=rMd):
                        for f in pre:
                            f()
                        sb, rsb = sbanks.next()
                        P.op("pe", mm(sb[:, c0:c1], kT[:, si * 128:(si + 1) * 128], qT[:, q0 + c0:q0 + c1], True, True),
                             reads=[rk, rq], writes=[rsb])
                        wt, rwt = wts.next()
                        jd = si - 4 * tb
                        bias_ = acol[dr][:, si, r:r + 1]
                        if 0 <= jd < 4:
                            d0, d1 = jd * 128, (jd + 1) * 128
                            P.op("act", act(wt[:, d0:d1], Mdg[:, si, :], AF.Exp, bias=bias_, scale=-1.0),
                                 reads=[rMd, rcol], writes=[rwt])
                            o0, o1 = (d1, c1) if dr == 0 else (c0, d0)
                            if o1 > o0:
                                P.op("act", act(wt[:, o0:o1], Mbc[:, q0 + o0:q0 + o1], AF.Exp, bias=bias_, scale=-1.0),
                                     reads=[rM, rcol], writes=[rwt])
                        else:
                            P.op("act", act(wt[:, c0:c1], Mbc[:, q0 + c0:q0 + c1], AF.Exp, bias=bias_, scale=-1.0),
                                 reads=[rM, rcol], writes=[rwt])
                        pt, rpt = pts.next()
                        P.op("dve", lambda e: e.tensor_tensor(pt[:, c0:c1], sb[:, c0:c1], wt[:, c0:c1], ALU.mult),
                             reads=[rsb, rwt], writes=[rpt])
                        st["pt"] = (pt, rpt)
                    pre = []
                    is_last = (si == sis[-1])
                    e2box = {}

                    def Bfn(si=si, j0=j0, j1=j1, tb=tb, st=st, is_last=is_last, va=va, rv=rv, dr=dr, r=r, blk=blk, e2box=e2box):
                        pt, rpt = st["pt"]
                        for j in range(j0, j1):
                            tj = 4 * tb + j
                            first = (si == 0) if dr == 0 else (si == tj)
                            last = (si == tj) if dr == 0 else (si == 31)
                            P.op("pe", mm(accs[j][0][:, 0:257], pt[:, j * 128:(j + 1) * 128], va[:, si, :], first, last),
                                 reads=[rpt, rv], writes=[accs[j][1]], inc=last)
                        if not is_last:
                            return
                        ybl = []
                        for j in range(4):
                            tj = 4 * tb + j
                            acc, racc = accs[j]
                            sm, rsm = sms.next()
                            P.op("act", act(sm[:, 0:1], acc[:, 256:257], AF.Abs), reads=[racc], writes=[rsm])
                            P.op("dve", lambda e, sm=sm, tj=tj: e.tensor_tensor(sm[:, 1:2], sm[:, 0:1], em[dr][:, tj, r:r + 1], ALU.max),
                                 reads=[rsm, rcol], writes=[rsm])
                            P.op("dve", lambda e, sm=sm: e.reciprocal(sm[:, 2:3], sm[:, 1:2]), reads=[rsm], writes=[rsm])
                            if dr == 0:
                                P.op("act", act(hf[:, tj, :], acc[:, 0:256], AF.Copy, scale=sm[:, 2:3]),
                                     reads=[racc, rsm], writes=[rhf[tj]])
                            else:
                                og, rog = blk["og"]
                                hs, rhs = hss.next()
                                P.op("dve", lambda e, hs=hs, sm=sm, acc=acc, tj=tj: e.scalar_tensor_tensor(
                                    hs, acc[:, 0:256], sm[:, 2:3], hf[:, tj, :], ALU.mult, ALU.add),
                                    reads=[racc, rsm, rhf[tj]], writes=[rhs])
                                P.op("act", act(junk, hs, AF.Square, accum_out=sm[:, 3:4]), reads=[rhs, rsm], writes=[rjunk, rsm])
                                P.op("act", act(sm[:, 4:5], sm[:, 3:4], AF.Sqrt, scale=1.0 / 256, bias=k.eps_ap),
                                     reads=[rsm, k.rconst], writes=[rsm])
                                P.op("dve", lambda e, sm=sm: e.reciprocal(sm[:, 4:5], sm[:, 4:5]), reads=[rsm], writes=[rsm])
                                yb, ryb = ybs.next()
                                P.op("dve", lambda e, yb=yb, hs=hs, sm=sm, og=og, j=j: e.scalar_tensor_tensor(
                                    yb, hs, sm[:, 4:5], og[:, j, :], ALU.mult, ALU.mult),
                                    reads=[rhs, rsm, rog], writes=[ryb])
                                ybl.append((yb, ryb))
                        e2box["ybl"] = ybl

                    def E2fn(tb=tb, h=h, e2box=e2box):
                        yst, ryst = ysts.next()
                        for j, (yb, ryb) in enumerate(e2box["ybl"]):
                            for cc in range(2):
                                slot = (cc * 4 + j) * 128
                                P.op("pe", lambda e, yb=yb, cc=cc, slot=slot: e.transpose(
                                    tb_bf[:, slot:slot + 128], yb[:, cc * 128:(cc + 1) * 128], k.ident_b),
                                    reads=[ryb, k.rconst], writes=[rtb])
                        for cc in range(2):
                            alt_copy(k, yst[:, cc, :], tb_bf[:, cc * 512:(cc + 1) * 512], [rtb], [ryst])
                            P.dma("sp", d["yT"][h * 256 + cc * 128:h * 256 + (cc + 1) * 128, tb * 512:(tb + 1) * 512],
                                  yst[:, cc, :], reads=[ryst])
                    items.append((Afn, Bfn, E2fn if (is_last and dr == 1) else None))
    run_pipeline(items, L=PIPE_L, defer=PIPE_L + 2)
    P.barrier()


def phase_qkv(k, xinT, d):
    P, A = k.P, k.A
    A.reset(k.a0)
    hT = A.alloc([16, S], BF16)
    hres = [Res(f"h{i}") for i in range(8)]
    phase_norm(k, xinT, PV["o_norm"], hT, hres)
    wring = ring(A, 3, [16, 512], BF16, "w")
    st_b = ring(A, 4, [512], BF16, "sb")
    W = k.w["o_w_qkv"]

    def epi_bf(dst, scale, row0):
        def epi(c, mw, n, bank, bres):
            st, rst = st_b.next()
            alt_copy(k, st[0:mw, :], bank[0:mw, :], [bres], [rst], scale)
            P.dma("sp", dst[c - row0:c - row0 + mw, n * 512:(n + 1) * 512], st[0:mw, :], reads=[rst])
        return epi
    linear_fm(k, hT, hres, 16, W, 0, D, wring, 512, 8, epi_bf(d["aqT"], 128.0 ** -0.5, 0))
    linear_fm(k, hT, hres, 16, W, D, D, wring, 512, 8, epi_bf(d["akT"], None, D))

    def epi_v(cb, nb, t, bank, bres):
        st, rst = st_b.next()
        alt_copy(k, st[:, 0:nb], bank[:, 0:nb], [bres], [rst])
        P.dma("sp", d["av"][t * 128:(t + 1) * 128, cb:cb + nb], st[:, 0:nb], reads=[rst])
    linear_tm(k, hT, hres, 16, W, 2 * D, D, wring, epi_v)
    P.barrier()


PIPE_L = 4


def run_pipeline(items, L=2, defer=3):
    n = len(items)
    pend = []
    for i in range(min(L, n)):
        items[i][0]()
    for i in range(n):
        if i + L < n:
            items[i + L][0]()
        items[i][1]()
        pend = [(c - 1, f) for (c, f) in pend]
        for (c, f) in pend:
            if c <= 0:
                f()
        pend = [(c, f) for (c, f) in pend if c > 0]
        if items[i][2] is not None:
            pend.append((defer, items[i][2]))
    for (c, f) in pend:
        f()


def phase_attn(k, d):
    P, A = k.P, k.A
    A.reset(k.a0)
    absd = A.alloc([17, 128], F32)
    mult = A.alloc([17, 128], F32)
    rcc = Res("acon")
    P.dma("sp", absd, k.cin["absd"], writes=[rcc])
    P.dma("sp", mult, k.cin["mult"], writes=[rcc])
    wrs = ring(A, 2, [17, 128], F32, "wr")
    qTs = ring(A, 2, [S], BF16, "q")
    kTs = ring(A, 2, [S], BF16, "k")
    vas = ring(A, 2, [32, 129], BF16, "v")
    ess = ring(A, 6, [512], F32, "es")
    pts = ring(A, 7, [512], BF16, "pt")
    obs = ring(A, 4, [128], BF16, "ob")
    osts = ring(A, 2, [512], BF16, "ost")
    sms = ring(A, 8, [8], F32, "sm")
    accs = k.banks[0:4]
    sbanks = Ring(k.banks[4:7])
    tb_all = k.banks[7][0].bitcast(BF16)
    tbs = Ring([(tb_all[:, 0:512], Res("tb0")), (tb_all[:, 512:1024], Res("tb1"))])
    for (va, rv) in vas.items:
        P.op("pool", lambda e, va=va: e.memset(va[:, :, 128:129], 1.0), writes=[rv])
    items = []
    for h in range(16):
        slope = 2.0 ** (-8.0 * (h + 1) / 16.0)
        wr, rwr = wrs.next()
        qT, rq = qTs.next()
        kT, rk = kTs.next()
        va, rv = vas.next()

        def head_load(h=h, slope=slope, wr=wr, rwr=rwr, qT=qT, rq=rq, kT=kT, rk=rk, va=va, rv=rv):
            P.op("act", act(wr, absd, AF.Exp, scale=-slope), reads=[rcc], writes=[rwr])
            P.op("pool", lambda e: e.tensor_tensor(wr, wr, mult, ALU.mult), reads=[rwr, rcc], writes=[rwr])
            P.dma("sp", qT, d["aqT"][h * 128:(h + 1) * 128, :], writes=[rq])
            P.dma("sp", kT, d["akT"][h * 128:(h + 1) * 128, :], writes=[rk])
            P.dma("sp", va[:, :, 0:128], d["av"][:, h * 128:(h + 1) * 128].rearrange("(t p) c -> p t c", p=128), writes=[rv])
        first_of_head = True
        for tb in range(8):
            q0 = tb * 512
            sis = list(range(max(0, 4 * tb - 8), min(31, 4 * tb + 11) + 1))
            for si in sis:
                j0 = max(0, si - 8 - 4 * tb)
                j1 = min(3, si + 8 - 4 * tb) + 1
                c0, c1 = j0 * 128, j1 * 128
                st = {}

                def Afn(si=si, c0=c0, c1=c1, j0=j0, j1=j1, tb=tb, q0=q0, st=st, hl=(head_load if first_of_head else None),
                        qT=qT, rq=rq, kT=kT, rk=rk, wr=wr, rwr=rwr):
                    if hl is not None:
                        hl()
                    sb, rsb = sbanks.next()
                    P.op("pe", mm(sb[:, c0:c1], kT[:, si * 128:(si + 1) * 128], qT[:, q0 + c0:q0 + c1], True, True),
                         reads=[rk, rq], writes=[rsb])
                    es_, res_ = ess.next()
                    P.op("act", act(es_[:, c0:c1], sb[:, c0:c1], AF.Exp), reads=[rsb], writes=[res_])
                    pt, rpt = pts.next()
                    e0 = (4 * tb + j0) - si + 8
                    nj = j1 - j0
                    P.op("dve", lambda e: e.tensor_tensor(
                        pt[:, c0:c1].rearrange("p (j c) -> p j c", j=nj), es_[:, c0:c1].rearrange("p (j c) -> p j c", j=nj),
                        wr[:, e0:e0 + nj, :], ALU.mult), reads=[res_, rwr], writes=[rpt])
                    st["pt"] = (pt, rpt)
                first_of_head = False
                is_last = (si == sis[-1])
                e2box = {}

                def Bfn(si=si, j0=j0, j1=j1, tb=tb, q0=q0, st=st, is_last=is_last, va=va, rv=rv, h=h, e2box=e2box):
                    pt, rpt = st["pt"]
                    for j in range(j0, j1):
                        tj = 4 * tb + j
                        first = (si == max(0, tj - 8))
                        last = (si == min(31, tj + 8))
                        P.op("pe", mm(accs[j][0][:, 0:129], pt[:, j * 128:(j + 1) * 128], va[:, si, :], first, last),
                             reads=[rpt, rv], writes=[accs[j][1]], inc=last)
                    if is_last:
                        obl = []
                        for j in range(4):
                            acc, racc = accs[j]
                            sm, rsm = sms.next()
                            P.op("dve", lambda e, sm=sm, acc=acc: e.reciprocal(sm[:, 0:1], acc[:, 128:129]), reads=[racc], writes=[rsm])
                            ob, rob = obs.next()
                            P.op("act", act(ob, acc[:, 0:128], AF.Copy, scale=sm[:, 0:1]), reads=[racc, rsm], writes=[rob])
                            obl.append((ob, rob))
                        e2box["obl"] = obl

                def E2fn(tb=tb, q0=q0, h=h, e2box=e2box):
                    ost, rost = osts.next()
                    tbk, rtb = tbs.next()
                    for j, (ob, rob) in enumerate(e2box["obl"]):
                        P.op("pe", lambda e, ob=ob, j=j: e.transpose(tbk[:, j * 128:(j + 1) * 128], ob, k.ident_b),
                             reads=[rob, k.rconst], writes=[rtb])
                    alt_copy(k, ost, tbk, [rtb], [rost])
                    P.dma("sp", d["aoT"][h * 128:(h + 1) * 128, q0:q0 + 512], ost, reads=[rost])
                items.append((Afn, Bfn, E2fn if is_last else None))
    run_pipeline(items, L=PIPE_L, defer=PIPE_L + 2)
    P.barrier()


def phase_final(k, xinT, outT):
    k.A.reset(k.a0)
    phase_norm(k, xinT, PV["final_norm"], None, None, out_dram=outT)
    k.P.barrier()


def build(phases, ext_out, debug=False):
    nc = bass.Bass("TRN2", target_bir_lowering=False)
    k = K()
    k.nc = nc
    k.alt = 0

    def dram(name, shape, dt, kind="Internal"):
        if name in ext_out:
            kind = "ExternalOutput"
        return nc.dram_tensor(name, list(shape), dt, kind=kind).ap()

    ein = lambda name, shape, dt=F32: nc.dram_tensor(name, list(shape), dt, kind="ExternalInput").ap()
    k.w = {
        "e_w_in": ein("e_w_in", [D, DIN]), "e_w_out": ein("e_w_out", [D, D]),
        "rg_wa": ein("rg_wa", [2, 8, 128, 128]), "rg_wx": ein("rg_wx", [2, 8, 128, 128]),
        "o_w_qkv": ein("o_w_qkv", [D, 3 * D]), "o_w_o": ein("o_w_o", [D, D]),
        "f_w1": ein("f_w1", [2, D, DFF]), "f_w3": ein("f_w3", [2, D, DFF]), "f_w2": ein("f_w2", [2, DFF, D]),
    }
    xT = ein("xT", [D, S])
    pvec = ein("pvec", [128, NPV])
    hg_in = ein("hg_bc", [128, 1024])
    c_ident = ein("c_ident", [128, 128])
    c_mask = ein("c_mask", [128, 2, 128])
    c_absd = ein("c_absd", [128, 17, 128])
    c_mult = ein("c_mult", [128, 17, 128])
    d = {}
    for (n, shp, dt) in (("qT", [512, S], BF16), ("kT", [512, S], BF16), ("v", [S, 1024], BF16),
                         ("og", [S, 1024], F32), ("gI", [8, S], F32), ("gF", [8, S], F32),
                         ("xrT", [1024, S], F32), ("ggT", [1024, S], F32), ("yT", [D, S], BF16),
                         ("x1T", [D, S], F32), ("uT", [DFF, S], BF16), ("x2T", [D, S], F32),
                         ("aqT", [D, S], BF16), ("akT", [D, S], BF16), ("av", [S, D], BF16),
                         ("aoT", [D, S], BF16), ("x3T", [D, S], F32), ("x4T", [D, S], F32),
                         ("gsc", [3, 8, S], F32), ("outT", [D, S], F32), ("w2b", [DFF, D], BF16)):
        if n in phases.get("ext_in", ()):
            d[n] = ein(n, shp, dt)
        else:
            d[n] = dram(n, shp, dt)
    k.d = d
    with ExitStack() as es:
        P = Prog(nc, es)
        k.P = P
        AW = 51 * 1024
        arena = es.enter_context(nc.sbuf_tensor("arena", [128, AW], F32))
        A = Arena(arena, AW)
        k.A = A
        banks = [(es.enter_context(nc.psum_tensor(f"ps{i}", [128, 512], F32))[:, :], Res(f"ps{i}")) for i in range(8)]
        k.banks = banks
        k.psr = Ring(banks[0:6])
        k.rconst = Res("const")
        k.pv = A.alloc([NPV], F32)
        k.hg_bc = A.alloc([1024], F32)
        k.ones_bf = A.alloc([128], BF16)
        k.eps_ap = A.alloc([1], F32)
        k.ident_f = A.alloc([128], F32)
        k.ident_b = A.alloc([128], BF16)
        P.dma("sp", k.pv, pvec, writes=[k.rconst])
        P.dma("sp", k.hg_bc, hg_in, writes=[k.rconst])
        P.dma("sp", k.ident_f, c_ident, writes=[k.rconst])
        P.dma("pool", k.ident_b, c_ident, writes=[k.rconst])
        P.op("dve", lambda e: e.memset(k.ones_bf, 1.0), writes=[k.rconst])
        P.op("dve", lambda e: e.memset(k.eps_ap, EPS), writes=[k.rconst])
        k.cin = {"mask": c_mask, "absd": c_absd, "mult": c_mult}
        P.barrier()
        k.a0 = A.mark()

        run = phases["run"]
        if "norm_dbg" in run:
            A.reset(k.a0)
            hT_ = A.alloc([16, S], BF16)
            hres_ = [Res(f"h{i}") for i in range(8)]
            phase_norm(k, xT, PV["e_norm"], hT_, hres_)
            for n_ in range(8):
                P.dma("sp", d["yT"][:, n_ * 512:(n_ + 1) * 512].rearrange("(k p) n -> p k n", p=128),
                      hT_[:, :, n_ * 512:(n_ + 1) * 512], reads=[hres_[n_]])
            P.barrier()
        if "in_proj" in run:
            phase_in_proj(k, xT, d)
        if "rglru" in run:
            phase_rglru(k, d)
        if "mlstm" in run:
            phase_mlstm(k, d)
        if "out_proj0" in run:
            phase_out_proj(k, d["yT"], k.w["e_w_out"], xT, d["x1T"])
        if "ffn0" in run:
            phase_ffn(k, d["x1T"], PV["f_norm0"], k.w["f_w1"][0], k.w["f_w3"][0], k.w["f_w2"][0], d["uT"], d["x2T"])
        if "qkv" in run:
            phase_qkv(k, d["x2T"], d)
        if "attn" in run:
            phase_attn(k, d)
        if "out_proj1" in run:
            phase_out_proj(k, d["aoT"], k.w["o_w_o"], d["x2T"], d["x3T"])
        if "ffn1" in run:
            phase_ffn(k, d["x3T"], PV["f_norm1"], k.w["f_w1"][1], k.w["f_w3"][1], k.w["f_w2"][1], d["uT"], d["x4T"])
        if "final" in run:
            phase_final(k, d["x4T"], d["outT"])
        P.barrier()
        P.emit()
    return nc


def host_inputs(inp, b):
    m = {
        "xT": np.ascontiguousarray(inp["x"][b].T),
        "e_w_in": inp["e_w_in"][0], "e_w_out": inp["e_w_out"][0],
        "rg_wa": inp["e_rg_wa"][0], "rg_wx": inp["e_rg_wx"][0],
        "o_w_qkv": inp["o_w_qkv"][0], "o_w_o": inp["o_w_o"][0],
        "f_w1": inp["f_w1"], "f_w3": inp["f_w3"], "f_w2": inp["f_w2"],
        "pvec": host_pvec(inp),
        "hg_bc": np.ascontiguousarray(np.broadcast_to(np.asarray(inp["e_head_g"][0], np.float32)[None, :], (128, 1024))),
    }
    m.update(host_consts())
    return m


ALL_PHASES = ["in_proj", "rglru", "mlstm", "out_proj0", "ffn0", "qkv", "attn", "out_proj1", "ffn1", "final"]
_NC_CACHE = {}


def kernel(**inputs):
    inp = {k_: np.asarray(v) for k_, v in inputs.items()}
    if "full" not in _NC_CACHE:
        _NC_CACHE["full"] = build({"run": ALL_PHASES, "ext_in": []}, {"outT"})
    nc = _NC_CACHE["full"]
    n = 8
    shared = host_inputs(inp, 0)
    in_maps = []
    for b in range(n):
        m = dict(shared)
        m["xT"] = np.ascontiguousarray(inp["x"][b].T)
        in_maps.append(m)
    res = run_bass_kernel_spmd(nc, in_maps, core_ids=list(range(n)))
    out = np.stack([np.asarray(res.results[b]["outT"]).T for b in range(n)], 0)
    return np.ascontiguousarray(out.astype(np.float32))
```

```python
from contextlib import ExitStack
import numpy as np
import concourse.bass as bass
import concourse.mybir as mybir
from concourse.bass_utils import run_bass_kernel_spmd

F32 = mybir.dt.float32
BF16 = mybir.dt.bfloat16
AF = mybir.ActivationFunctionType
ALU = mybir.AluOpType

S = 4096
D = 2048
DFF = 5632
DIN = 5136
EPS = 1e-6
ENGS = ("pe", "act", "dve", "pool", "sp")


class Res:
    __slots__ = ("name", "w", "r")

    def __init__(self, name=""):
        self.name = name
        self.w = None
        self.r = []


class _Eng:
    def __init__(self, name, sem):
        self.name = name
        self.sem = sem
        self.cnt = 0
        self.known = {}
        self.ops = []
        self.pending = False


class Prog:
    N_DMA_SEMS = 24

    def __init__(self, nc, es):
        self.nc = nc
        self.es = es
        self.e = {}
        for n in ENGS:
            self.e[n] = _Eng(n, es.enter_context(nc.semaphore("s_" + n)))
        self.dma_sems = {}
        for q in ("sp", "pool", "act"):
            lst = [[es.enter_context(nc.semaphore(f"d_{q}{i}")), 0]
                   for i in range(self.N_DMA_SEMS if q != "act" else 12)]
            self.dma_sems[q] = [lst, 0]
        self.n_ops = 0

    def _deps(self, eng, reads, writes, noself=False):
        toks = []
        for r in reads:
            if r.w is not None:
                toks.append(r.w)
        for w in writes:
            if w.w is not None:
                toks.append(w.w)
            toks.extend(w.r)
        need = {}
        for (sem, val) in toks:
            k = id(sem)
            if (eng.name == "pe" or noself) and sem is eng.sem:
                continue
            if eng.known.get(k, 0) >= val:
                continue
            if k not in need or need[k][1] < val:
                need[k] = (sem, val)
        for k, (sem, val) in need.items():
            eng.known[k] = val
        return list(need.values())

    def _commit(self, tok, reads, writes):
        for r in reads:
            r.r.append(tok)
            if len(r.r) > 48:
                best = {}
                for (s, v) in r.r:
                    if id(s) not in best or best[id(s)][1] < v:
                        best[id(s)] = (s, v)
                r.r = list(best.values())
        for w in writes:
            w.w = tok
            w.r = []

    def op(self, eng, fn, reads=(), writes=(), inc=True, noself=False):
        E = self.e[eng]
        waits = self._deps(E, reads, writes, noself)
        if inc:
            E.cnt += 1
            tok = (E.sem, E.cnt)
            E.pending = False
        else:
            tok = (E.sem, E.cnt + 1)
            E.pending = True
        E.ops.append((waits, fn, (E.sem, 1) if inc else None))
        self._commit(tok, reads, writes)
        self.n_ops += 1

    def dma(self, q, out, in_, reads=(), writes=(), **kw):
        E = self.e[q]
        waits = self._deps(E, reads, writes)
        lst, idx = self.dma_sems[q]
        slot = lst[idx % len(lst)]
        self.dma_sems[q][1] = idx + 1
        sem, cur = slot
        if cur > 0 and E.known.get(id(sem), 0) < cur:
            waits.append((sem, cur))
            E.known[id(sem)] = cur
        slot[1] = cur + 16
        tok = (sem, cur + 16)

        def fn(e, out=out, in_=in_, kw=kw):
            return e.dma_start(out=out, in_=in_, **kw)

        E.ops.append((waits, fn, (sem, 16)))
        self._commit(tok, reads, writes)
        self.n_ops += 1

    def barrier(self):
        toks = []
        for n in ENGS:
            E = self.e[n]
            assert not E.pending, n
            if E.cnt > 0:
                toks.append((E.sem, E.cnt))
        for q in self.dma_sems:
            for sem, cur in self.dma_sems[q][0]:
                if cur > 0:
                    toks.append((sem, cur))
        for n in ENGS:
            E = self.e[n]
            waits = []
            for (sem, val) in toks:
                if sem is E.sem:
                    continue
                if E.known.get(id(sem), 0) < val:
                    E.known[id(sem)] = val
                    waits.append((sem, val))
            if waits:
                E.ops.append((waits, None, None))

    def emit(self):
        nc = self.nc
        for n in ENGS:
            assert not self.e[n].pending, f"engine {n} has pending un-inc'd ops"
        with nc.Block() as block:
            def run(E):
                def body(eng):
                    for (waits, fn, inc) in E.ops:
                        for (sem, val) in waits:
                            eng.wait_ge(sem, val)
                        if fn is not None:
                            ins = fn(eng)
                            if inc is not None:
                                ins.then_inc(inc[0], inc[1])
                return body
            if self.e["sp"].ops:
                block.sync(run(self.e["sp"]))
            if self.e["act"].ops:
                block.scalar(run(self.e["act"]))
            if self.e["dve"].ops:
                block.vector(run(self.e["dve"]))
            if self.e["pool"].ops:
                block.gpsimd(run(self.e["pool"]))
            if self.e["pe"].ops:
                block.tensor(run(self.e["pe"]))


class Arena:
    def __init__(self, ap, words):
        self.ap = ap
        self.words = words
        self.off = 0

    def mark(self):
        return self.off

    def reset(self, m=0):
        self.off = m

    def alloc(self, shape, dt):
        n = 1
        for s in shape:
            n *= s
        w = n if dt == F32 else (n + 1) // 2
        w = (w + 7) // 8 * 8
        assert self.off + w <= self.words, f"arena overflow {self.off}+{w}>{self.words}"
        v = self.ap[:, self.off:self.off + w]
        self.off += w
        if dt != F32:
            v = v.bitcast(dt)
        v = v[:, 0:n]
        if len(shape) == 2:
            v = v.rearrange("p (a b) -> p a b", a=shape[0])
        elif len(shape) == 3:
            v = v.rearrange("p (a b c) -> p a b c", a=shape[0], b=shape[1])
        return v


class Ring:
    def __init__(self, items):
        self.items = items
        self.i = 0

    def next(self):
        it = self.items[self.i % len(self.items)]
        self.i += 1
        return it


def ring(A, n, shape, dt, name="r"):
    return Ring([(A.alloc(shape, dt), Res(f"{name}{i}")) for i in range(n)])


PV = {}
_c = 0
for _n, _w in (("e_norm", 16), ("f_norm0", 16), ("f_norm1", 16), ("o_norm", 16), ("final_norm", 16),
               ("conv_w", 32), ("conv_b", 8), ("rg_ba", 16), ("rg_bx", 16), ("rg_lam", 16),
               ("gate_bI", 1), ("gate_bF", 1)):
    PV[_n] = _c
    _c += _w
NPV = _c


def host_pvec(inp):
    pv = np.zeros((128, NPV), np.float32)
    col = lambda v: np.ascontiguousarray(np.asarray(v, np.float32).reshape(-1, 128).T)
    pv[:, PV["e_norm"]:PV["e_norm"] + 16] = col(inp["e_norm"][0])
    pv[:, PV["f_norm0"]:PV["f_norm0"] + 16] = col(inp["f_norm"][0])
    pv[:, PV["f_norm1"]:PV["f_norm1"] + 16] = col(inp["f_norm"][1])
    pv[:, PV["o_norm"]:PV["o_norm"] + 16] = col(inp["o_norm"][0])
    pv[:, PV["final_norm"]:PV["final_norm"] + 16] = col(inp["final_norm"])
    cw = np.asarray(inp["e_conv_w"][0], np.float32)
    for c in range(8):
        for j in range(4):
            pv[:, PV["conv_w"] + c * 4 + j] = cw[j, c * 128:(c + 1) * 128]
    pv[:, PV["conv_b"]:PV["conv_b"] + 8] = col(inp["e_conv_b"][0])
    for d in range(2):
        pv[:, PV["rg_ba"] + d * 8:PV["rg_ba"] + d * 8 + 8] = col(inp["e_rg_ba"][0, d])
        pv[:, PV["rg_bx"] + d * 8:PV["rg_bx"] + d * 8 + 8] = col(inp["e_rg_bx"][0, d])
        pv[:, PV["rg_lam"] + d * 8:PV["rg_lam"] + d * 8 + 8] = col(inp["e_rg_lam"][0, d])
    gb = np.asarray(inp["e_gate_b"][0], np.float32)
    for d in range(2):
        for h in range(4):
            pv[d * 4 + h, PV["gate_bI"]] = gb[(2 * d) * 4 + h]
            pv[d * 4 + h, PV["gate_bF"]] = gb[(2 * d + 1) * 4 + h]
    return pv


def host_consts():
    ident = np.eye(128, dtype=np.float32)
    s = np.arange(128)[:, None]
    t = np.arange(128)[None, :]
    mask_f = (s <= t).astype(np.float32)
    mask_b = (s >= t).astype(np.float32)
    absd = np.zeros((128, 17, 128), np.float32)
    mult = np.zeros((128, 17, 128), np.float32)
    for e in range(17):
        d = (s - t) - (e - 8) * 128
        ad = np.abs(d)
        absd[:, e, :] = ad
        c = (ad <= 64).astype(np.float32) + ((ad <= 256) & (d % 4 == 0)) + ((ad <= 1024) & (d % 16 == 0))
        mult[:, e, :] = c
    return {"c_ident": ident, "c_mask": np.stack([mask_f, mask_b], 1).copy(),
            "c_absd": absd, "c_mult": mult}


class K:
    pass


def mm(out, lhsT, rhs, start, stop):
    return lambda e: e.matmul(out, lhsT, rhs, start=start, stop=stop)


def norm_steps(k, xT, gcol, hT, hres, out_dram=None, TW=128):
    P, A = k.P, k.A
    xts = ring(A, 3, [16, TW], F32, "nx")
    sqs = ring(A, 2, [16, TW], BF16, "nsq")
    rss = ring(A, 3, [TW], F32, "nr")
    outs = ring(A, 2, [16, TW], F32, "no") if out_dram is not None else None
    pv = k.pv

    def tile(j):
        xt, rxt = xts.next()
        sq, rsq = sqs.next()
        rs, rrs = rss.next()
        bank, bres = k.psr.next()
        tsl = slice(j * TW, (j + 1) * TW)
        P.dma("sp" if out_dram is None else "pool", xt, xT[:, tsl].rearrange("(k p) n -> p k n", p=128), writes=[rxt])
        P.op("act", lambda e: e.activation(sq, xt, AF.Square), reads=[rxt], writes=[rsq])
        for kc in range(16):
            P.op("pe", mm(bank[:, 0:TW], k.ones_bf, sq[:, kc, :], kc == 0, kc == 15),
                 reads=[rsq, k.rconst], writes=[bres], inc=(kc == 15))
        P.op("act", lambda e: e.activation(rs, bank[:, 0:TW], AF.Sqrt, bias=k.eps_ap, scale=1.0 / D),
             reads=[bres, k.rconst], writes=[rrs])
        P.op("dve", lambda e: e.reciprocal(rs, rs), reads=[rrs], writes=[rrs])
        if out_dram is None:
            for kc in range(16):
                P.op("dve", lambda e, kc=kc: e.scalar_tensor_tensor(
                    hT[:, kc, tsl], xt[:, kc, :], pv[:, gcol + kc:gcol + kc + 1], rs, ALU.mult, ALU.mult),
                    reads=[rxt, rrs, k.rconst], writes=[hres[j * TW // 512]], noself=(kc > 0))
        else:
            ot, rot = outs.next()
            for kc in range(16):
                P.op("dve", lambda e, kc=kc: e.scalar_tensor_tensor(
                    ot[:, kc, :], xt[:, kc, :], pv[:, gcol + kc:gcol + kc + 1], rs, ALU.mult, ALU.mult),
                    reads=[rxt, rrs, k.rconst], writes=[rot], noself=(kc > 0))
            P.dma("sp", out_dram[:, tsl].rearrange("(k p) n -> p k n", p=128), ot, reads=[rot])

    per = 512 // TW
    return [[(lambda j=j: tile(j)) for j in range(n * per, (n + 1) * per)] for n in range(S // 512)]


def run_all(steps):
    for st in steps:
        for f in st:
            f()


def interleave(norm, lin, lead=2):
    for i in range(min(lead, len(norm))):
        for f in norm[i]:
            f()
    for n in range(len(lin)):
        nxt = list(norm[n + lead]) if n + lead < len(norm) else []
        groups = lin[n]
        every = max(1, len(groups) // max(1, len(nxt))) if nxt else 0
        for gi, g in enumerate(groups):
            g()
            if nxt and (gi + 1) % every == 0:
                nxt.pop(0)()
        for f in nxt:
            f()


def load_w(k, wring, W, c0, nb, KCn):
    wt, wres = wring.next()
    k.P.dma("pool", wt[:, 0:KCn, 0:nb], W[:, c0:c0 + nb].rearrange("(k p) n -> p k n", p=128), writes=[wres])
    return wt, wres


def linear_fm(k, hT, hres, KCn, W, c0, ncols, wring, blk, ntiles, epi, tok_off=0, pre=None, nouter=0, norm=None):
    P = k.P

    def group(cb, mo, mw, n, wt, wres):
        if pre is not None:
            pre(c0 + cb + mo, mw, n)
        bank, bres = k.psr.next()
        for kc in range(KCn):
            P.op("pe", mm(bank[0:mw, :], wt[:, kc, mo:mo + mw], hT[:, kc, n * 512:(n + 1) * 512],
                          kc == 0, kc == KCn - 1),
                 reads=[wres, hres[n]], writes=[bres], inc=(kc == KCn - 1))
        epi(c0 + cb + mo, mw, n, bank, bres)

    blocks = [(cb, min(blk, ncols - cb)) for cb in range(0, ncols, blk)]
    head, tail = blocks[:nouter], blocks[nouter:]
    if head:
        loaded = [(cb, nb) + tuple(load_w(k, wring, W, c0 + cb, nb, KCn)) for (cb, nb) in head]

        def lstep(n):
            return [(lambda cb=cb, mo=mo, nb=nb, wt=wt, wres=wres: group(cb, mo, min(128, nb - mo), n, wt, wres))
                    for (cb, nb, wt, wres) in loaded for mo in range(0, nb, 128)]
        lin = [lstep(n) for n in range(ntiles)]
        if norm is not None:
            interleave(norm, lin)
        else:
            run_all(lin)
    elif norm is not None:
        run_all(norm)
    for (cb, nb) in tail:
        wt, wres = load_w(k, wring, W, c0 + cb, nb, KCn)
        for mo in range(0, nb, 128):
            mw = min(128, nb - mo)
            for n in range(ntiles):
                group(cb, mo, mw, n, wt, wres)


def linear_tm(k, hT, hres, KCn, W, c0, ncols, wring, epi):
    P = k.P
    for cb in range(0, ncols, 512):
        nb = min(512, ncols - cb)
        wt, wres = load_w(k, wring, W, c0 + cb, nb, KCn)
        for t in range(S // 128):
            bank, bres = k.psr.next()
            for kc in range(KCn):
                P.op("pe", mm(bank[:, 0:nb], hT[:, kc, t * 128:(t + 1) * 128], wt[:, kc, 0:nb],
                              kc == 0, kc == KCn - 1),
                     reads=[wres, hres[t // 4]], writes=[bres], inc=(kc == KCn - 1))
            epi(cb, nb, t, bank, bres)


def alt_copy(k, out, in_, reads, writes, scale=None):
    P = k.P
    k.alt ^= 1
    if k.alt:
        if scale is None:
            P.op("act", lambda e: e.copy(out, in_), reads=reads, writes=writes)
        else:
            P.op("act", lambda e: e.mul(out, in_, scale), reads=reads, writes=writes)
    else:
        if scale is None:
            P.op("dve", lambda e: e.tensor_copy(out, in_), reads=reads, writes=writes)
        else:
            P.op("dve", lambda e: e.tensor_scalar_mul(out, in_, scale), reads=reads, writes=writes)


def phase_in_proj(k, xT, d):
    P, A = k.P, k.A
    A.reset(k.a0)
    hT = A.alloc([16, S], BF16)
    hres = [Res(f"h{i}") for i in range(8)]
    m_ = A.mark()
    run_all(norm_steps(k, xT, PV["e_norm"], hT, hres, TW=256))
    P.barrier()
    A.reset(m_)
    wring = ring(A, 3, [16, 512], BF16, "w")
    st_b = ring(A, 4, [512], BF16, "sb")
    st_f = ring(A, 4, [512], F32, "sf")
    W = k.w["e_w_in"]

    def epi_qk(c, mw, n, bank, bres):
        st, rst = st_b.next()
        if c < 512:
            alt_copy(k, st[0:mw, :], bank[0:mw, :], [bres], [rst], 128.0 ** -0.5)
            P.dma("sp", d["qT"][c:c + mw, n * 512:(n + 1) * 512], st[0:mw, :], reads=[rst])
        else:
            alt_copy(k, st[0:mw, :], bank[0:mw, :], [bres], [rst], None)
            P.dma("sp", d["kT"][c - 512:c - 512 + mw, n * 512:(n + 1) * 512], st[0:mw, :], reads=[rst])
    linear_fm(k, hT, hres, 16, W, 0, 1024, wring, 512, 8, epi_qk)

    def epi_v(cb, nb, t, bank, bres):
        st, rst = st_b.next()
        alt_copy(k, st[:, 0:nb], bank[:, 0:nb], [bres], [rst])
        P.dma("sp", d["v"][t * 128:(t + 1) * 128, cb:cb + nb], st[:, 0:nb], reads=[rst])
    linear_tm(k, hT, hres, 16, W, 1024, 1024, wring, epi_v)

    def epi_o(cb, nb, t, bank, bres):
        st, rst = st_f.next()
        P.op("act", lambda e: e.activation(st[:, 0:nb], bank[:, 0:nb], AF.Sigmoid), reads=[bres], writes=[rst])
        P.op("dve", lambda e: e.tensor_tensor(st[:, 0:nb], st[:, 0:nb], k.hg_bc[:, cb:cb + nb], ALU.mult),
             reads=[rst, k.rconst], writes=[rst])
        P.dma("sp", d["og"][t * 128:(t + 1) * 128, cb:cb + nb], st[:, 0:nb], reads=[rst])
    linear_tm(k, hT, hres, 16, W, 2048, 1024, wring, epi_o)

    wgI = A.alloc([16, 8], BF16)
    wgF = A.alloc([16, 8], BF16)
    rwg = Res("wg")
    for dd in range(2):
        for (wg, off) in ((wgI, 0), (wgF, 4)):
            cc = 3072 + dd * 8 + off
            P.dma("pool", wg[:, :, dd * 4:dd * 4 + 4], W[:, cc:cc + 4].rearrange("(k p) n -> p k n", p=128),
                  writes=[rwg])
    for (wg, dst, bcol) in ((wgI, d["gI"], PV["gate_bI"]), (wgF, d["gF"], PV["gate_bF"])):
        for n in range(8):
            bank, bres = k.psr.next()
            for kc in range(16):
                P.op("pe", mm(bank[0:8, :], wg[:, kc, :], hT[:, kc, n * 512:(n + 1) * 512], kc == 0, kc == 15),
                     reads=[rwg, hres[n]], writes=[bres], inc=(kc == 15))
            st, rst = st_f.next()
            P.op("dve", lambda e, st=st, bank=bank, bcol=bcol: e.tensor_scalar(
                st[0:8, :], bank[0:8, :], k.pv[0:8, bcol:bcol + 1], None, ALU.add),
                reads=[bres, k.rconst], writes=[rst])
            P.dma("sp", dst[:, n * 512:(n + 1) * 512], st[0:8, :], reads=[rst])

    def epi_xr(c, mw, n, bank, bres):
        st, rst = st_f.next()
        alt_copy(k, st[0:mw, :], bank[0:mw, :], [bres], [rst])
        P.dma("sp", d["xrT"][c - 3088:c - 3088 + mw, n * 512:(n + 1) * 512], st[0:mw, :], reads=[rst])
    linear_fm(k, hT, hres, 16, W, 3088, 1024, wring, 512, 8, epi_xr)

    def epi_gr(c, mw, n, bank, bres):
        st, rst = st_f.next()
        P.op("act", lambda e: e.activation(st[0:mw, :], bank[0:mw, :], AF.Gelu_apprx_tanh), reads=[bres], writes=[rst])
        P.dma("sp", d["ggT"][c - 4112:c - 4112 + mw, n * 512:(n + 1) * 512], st[0:mw, :], reads=[rst])
    linear_fm(k, hT, hres, 16, W, 4112, 1024, wring, 512, 8, epi_gr)
    P.barrier()


def phase_out_proj(k, srcT, W, xinT, xoutT):
    P, A = k.P, k.A
    A.reset(k.a0)
    hT = A.alloc([16, S], BF16)
    hres = [Res(f"h{i}") for i in range(8)]
    for n in range(8):
        P.dma("sp", hT[:, :, n * 512:(n + 1) * 512],
              srcT[:, n * 512:(n + 1) * 512].rearrange("(k p) n -> p k n", p=128), writes=[hres[n]])
    wring = ring(A, 3, [16, 512], BF16, "w")
    xr_ = ring(A, 8, [512], F32, "xi")
    fifo = []

    def pre(c, mw, n):
        xt, rxt = xr_.next()
        P.dma("act", xt, xinT[c:c + 128, n * 512:(n + 1) * 512], writes=[rxt])
        fifo.append((xt, rxt))

    def epi(c, mw, n, bank, bres):
        xt, rxt = fifo.pop(0)
        P.op("dve", lambda e: e.tensor_tensor(xt, xt, bank, ALU.add), reads=[bres, rxt], writes=[rxt])
        P.dma("sp", xoutT[c:c + 128, n * 512:(n + 1) * 512], xt, reads=[rxt])
    linear_fm(k, hT, hres, 16, W, 0, D, wring, 512, 8, epi, pre=pre)
    P.barrier()


def phase_ffn(k, xinT, gcol, W1, W3, W2, uT, xoutT):
    P, A = k.P, k.A
    A.reset(k.a0)
    hT = A.alloc([16, S], BF16)
    hres = [Res(f"h{i}") for i in range(8)]
    w2b = k.d["w2b"]
    m_ = A.mark()
    run_all(norm_steps(k, xinT, gcol, hT, hres, TW=256))
    P.barrier()
    A.reset(m_)
    w1r = ring(A, 2, [16, 256], BF16, "w1")
    w3r = ring(A, 2, [16, 256], BF16, "w3")
    sil = ring(A, 3, [512], F32, "sil")
    ust = ring(A, 4, [512], BF16, "ust")

    def grp(cb, mo, n, w1, rw1, w3, rw3):
        b1, rb1 = k.psr.next()
        b3, rb3 = k.psr.next()
        for (bank, bres, wt, wres) in ((b1, rb1, w1, rw1), (b3, rb3, w3, rw3)):
            for kc in range(16):
                P.op("pe", mm(bank, wt[:, kc, mo:mo + 128], hT[:, kc, n * 512:(n + 1) * 512], kc == 0, kc == 15),
                     reads=[wres, hres[n]], writes=[bres], inc=(kc == 15))
        sl, rsl = sil.next()
        us, rus = ust.next()
        P.op("act", lambda e: e.activation(sl, b1, AF.Silu), reads=[rb1], writes=[rsl])
        P.op("dve", lambda e: e.tensor_tensor(us, sl, b3, ALU.mult), reads=[rsl, rb3], writes=[rus])
        P.dma("sp", uT[cb + mo:cb + mo + 128, n * 512:(n + 1) * 512], us, reads=[rus])

    def loadblk(cb):
        w1, rw1 = load_w(k, w1r, W1, cb, 256, 16)
        w3, rw3 = load_w(k, w3r, W3, cb, 256, 16)
        if (cb // 256) % 2 == 1:
            r0 = (cb // 512) * 512
            P.dma("pool", w2b[r0:r0 + 512, :], W2[r0:r0 + 512, :])
        return (cb, w1, rw1, w3, rw3)
    for cb in range(0, DFF, 256):
        (_, w1, rw1, w3, rw3) = loadblk(cb)
        for mo in (0, 128):
            for n in range(8):
                grp(cb, mo, n, w1, rw1, w3, rw3)
    P.barrier()
    A.reset(k.a0)
    TB = 1024
    uhs = ring(A, 3, [44, 512], BF16, "uh")
    w2r = ring(A, 2, [44, 256], BF16, "w2")
    xr_ = ring(A, 8, [512], F32, "xi")
    fifo = []

    def load_half(tb, n, buf, res):
        for half in range(2):
            kk = slice(half * 22, half * 22 + 22)
            P.dma("sp", buf[:, kk, :],
                  uT[half * 22 * 128:(half + 1) * 22 * 128, tb * TB + n * 512:tb * TB + (n + 1) * 512].rearrange(
                      "(k p) n -> p k n", p=128), writes=[res])
    nblk = S // TB
    halves = [[uhs.next(), uhs.next()] for _ in range(nblk)]
    load_half(0, 0, *halves[0][0])
    for tb in range(nblk):
        load_half(tb, 1, *halves[tb][1])
        if tb + 1 < nblk:
            load_half(tb + 1, 0, *halves[tb + 1][0])

        class _U:
            def __init__(self, hs):
                self.hs = hs

            def __getitem__(self, key):
                p, kc, tsl = key
                n = tsl.start // 512
                return self.hs[n][0][p, kc, :]
        ut = _U(halves[tb])
        ures = [halves[tb][0][1], halves[tb][1][1]]

        def pre(c, mw, n, tb=tb):
            xt, rxt = xr_.next()
            tsl = slice(tb * TB + n * 512, tb * TB + (n + 1) * 512)
            P.dma("act", xt, xinT[c:c + 128, tsl], writes=[rxt])
            fifo.append((xt, rxt))

        def epi(c, mw, n, bank, bres, tb=tb):
            xt, rxt = fifo.pop(0)
            tsl = slice(tb * TB + n * 512, tb * TB + (n + 1) * 512)
            P.op("dve", lambda e: e.tensor_tensor(xt, xt, bank, ALU.add), reads=[bres, rxt], writes=[rxt])
            P.dma("sp", xoutT[c:c + 128, tsl], xt, reads=[rxt])
        linear_fm(k, ut, ures, 44, w2b, 0, D, w2r, 256, 2, epi, pre=pre)
    P.barrier()


def act(out, in_, func, **kw):
    return lambda e: e.activation(out, in_, func, **kw)


def phase_rglru(k, d):
    P, A = k.P, k.A
    A.reset(k.a0)
    pv = k.pv
    wa = A.alloc([16, 128], BF16)
    wx = A.alloc([16, 128], BF16)
    rw = Res("rgw")
    P.dma("pool", wa, k.w["rg_wa"].rearrange("d n i j -> i (d n) j"), writes=[rw])
    P.dma("pool", wx, k.w["rg_wx"].rearrange("d n i j -> i (d n) j"), writes=[rw])
    cst = A.alloc([16], F32)
    rc = Res("cst")
    lam = pv[:, PV["rg_lam"]:PV["rg_lam"] + 16]
    P.op("act", act(cst, lam, AF.Exp, scale=-1.0), reads=[k.rconst], writes=[rc])
    P.op("act", act(cst, cst, AF.Ln, bias=1.0), reads=[rc], writes=[rc])
    P.op("dve", lambda e: e.tensor_scalar_mul(cst, cst, -8.0), reads=[rc], writes=[rc])
    xpads = ring(A, 2, [S + 4], F32, "xp")
    for (xp, rxp) in xpads.items:
        P.op("pool", lambda e, xp=xp: e.memset(xp[:, 0:2], 0.0), writes=[rxp])
        P.op("pool", lambda e, xp=xp: e.memset(xp[:, S + 2:S + 4], 0.0), writes=[rxp])
    xc = A.alloc([S], F32); rxc = Res("xc")
    xcb = A.alloc([S], BF16); rxcb = Res("xcb")
    ats = [A.alloc([S], F32) for _ in range(2)]; rats = [Res("a0"), Res("a1")]
    uts = [A.alloc([S], F32) for _ in range(2)]; ruts = [Res("u0"), Res("u1")]
    hd = [A.alloc([S], F32) for _ in range(2)]
    rhd = [Res("hf"), Res("hb")]
    gg = A.alloc([S], F32); rgg = Res("gg")
    yb = A.alloc([S], BF16); ryb = Res("yb")
    for c in range(8):
        xp, rxp = xpads.next()
        P.dma("sp", xp[:, 2:S + 2], d["xrT"][c * 128:(c + 1) * 128, :], writes=[rxp])
        P.dma("sp", gg, d["ggT"][c * 128:(c + 1) * 128, :], writes=[rgg])
        cw = lambda j: pv[:, PV["conv_w"] + c * 4 + j:PV["conv_w"] + c * 4 + j + 1]
        cb = pv[:, PV["conv_b"] + c:PV["conv_b"] + c + 1]
        P.op("dve", lambda e, xp=xp, w0=cw(0), cb=cb: e.tensor_scalar(xc, xp[:, 0:S], w0, cb, ALU.mult, ALU.add),
             reads=[rxp, k.rconst], writes=[rxc])
        for j in (1, 2, 3):
            P.op("dve", lambda e, xp=xp, j=j, wj=cw(j): e.scalar_tensor_tensor(xc, xp[:, j:j + S], wj, xc, ALU.mult, ALU.add),
                 reads=[rxp, rxc, k.rconst], writes=[rxc])
        P.op("act", lambda e: e.copy(xcb, xc), reads=[rxc], writes=[rxcb])
        for dr in range(2):
            at, rat = ats[dr], rats[dr]
            ut, rut = uts[dr], ruts[dr]
            tm, rtm = hd[dr], rhd[dr]
            wi = dr * 8 + c
            ba = pv[:, PV["rg_ba"] + wi:PV["rg_ba"] + wi + 1]
            bx = pv[:, PV["rg_bx"] + wi:PV["rg_bx"] + wi + 1]
            for (wt_, bias_, dst, rdst) in ((wa, ba, at, rat), (wx, bx, ut, rut)):
                for n in range(8):
                    bank, bres = k.psr.next()
                    P.op("pe", mm(bank, wt_[:, wi, :], xcb[:, n * 512:(n + 1) * 512], True, True),
                         reads=[rw, rxcb], writes=[bres])
                    P.op("act", act(dst[:, n * 512:(n + 1) * 512], bank, AF.Sigmoid, bias=bias_),
                         reads=[bres, k.rconst], writes=[rdst], noself=(n > 0))
            P.op("act", act(at, at, AF.Exp, scale=cst[:, wi:wi + 1]), reads=[rat, rc], writes=[rat])
            P.op("act", act(tm, at, AF.Square), reads=[rat], writes=[rtm])
            P.op("act", act(tm, tm, AF.Sqrt, scale=-1.0, bias=1.0), reads=[rtm], writes=[rtm])
            P.op("dve", lambda e, ut=ut: e.tensor_tensor(ut, ut, xc, ALU.mult), reads=[rut, rxc], writes=[rut])
            P.op("dve", lambda e, ut=ut, tm=tm: e.tensor_tensor(ut, ut, tm, ALU.mult), reads=[rut, rtm], writes=[rut])
            if dr == 0:
                P.op("dve", lambda e, h=hd[0], at=at, ut=ut: e.tensor_tensor_scan(h, at, ut, 0.0, ALU.mult, ALU.add),
                     reads=[rat, rut], writes=[rhd[0]])
            else:
                P.op("dve", lambda e, h=hd[1], at=at, ut=ut: e.tensor_tensor_scan(
                    h[:, ::-1], at[:, ::-1], ut[:, ::-1], 0.0, ALU.mult, ALU.add),
                    reads=[rat, rut], writes=[rhd[1]])
        P.op("pool", lambda e: e.tensor_tensor(hd[0], hd[0], hd[1], ALU.add), reads=[rhd[0], rhd[1]], writes=[rhd[0]])
        P.op("pool", lambda e: e.tensor_tensor(yb, hd[0], gg, ALU.mult), reads=[rhd[0], rgg], writes=[ryb])
        P.dma("sp", d["yT"][1024 + c * 128:1024 + (c + 1) * 128, :], yb, reads=[ryb])
    P.barrier()


def tbank_bf(k):
    return k.banks[6][0].bitcast(BF16), k.banks[6][1]


def phase_mlstm(k, d):
    P, A = k.P, k.A
    A.reset(k.a0)
    pv = k.pv
    acol = [A.alloc([32, 8], F32) for _ in range(2)]
    em = [A.alloc([32, 8], F32) for _ in range(2)]
    rcol = Res("cols")
    mg = A.mark()
    T = {}
    for n in ("gi", "lf", "ones", "Bf", "Bb", "mf", "mb", "af", "ab", "tmp"):
        T[n] = (A.alloc([S], F32)[0:8, :], Res(n))
    gi, rgi = T["gi"]; lf, rlf = T["lf"]; ones, rones = T["ones"]
    P.dma("sp", gi, d["gI"], writes=[rgi])
    P.dma("sp", lf, d["gF"], writes=[rlf])
    P.op("act", act(lf, lf, AF.Exp, scale=-1.0), reads=[rlf], writes=[rlf])
    P.op("act", act(lf, lf, AF.Ln, bias=1.0), reads=[rlf], writes=[rlf])
    P.op("dve", lambda e: e.tensor_scalar_mul(lf, lf, -1.0), reads=[rlf], writes=[rlf])
    P.op("pool", lambda e: e.memset(ones, 1.0), writes=[rones])
    rev = lambda a: a[:, ::-1]
    idn = lambda a: a
    for (dr, Bn, mn, an, f) in ((0, "Bf", "mf", "af", idn), (1, "Bb", "mb", "ab", rev)):
        B_, rB = T[Bn]; m_, rm = T[mn]; a_, ra = T[an]; tmp, rtmp = T["tmp"]
        P.op("dve", lambda e, B_=B_, f=f: e.tensor_tensor_scan(f(B_), f(ones), f(lf), 0.0, ALU.mult, ALU.add),
             reads=[rones, rlf], writes=[rB])
        P.op("dve", lambda e, m_=m_, f=f: e.tensor_tensor_scan(f(m_), f(lf), f(gi), -1e30, ALU.add, ALU.max),
             reads=[rlf, rgi], writes=[rm])
        P.op("pool", lambda e, a_=a_, B_=B_: e.tensor_tensor(a_, gi, B_, ALU.subtract), reads=[rgi, rB], writes=[ra])
        P.op("pool", lambda e, m_=m_, B_=B_: e.tensor_tensor(tmp, m_, B_, ALU.subtract), reads=[rm, rB], writes=[rtmp])
        P.dma("sp", d["gsc"][1, dr * 4:dr * 4 + 4, :], T["tmp"][0][dr * 4:dr * 4 + 4, :], reads=[rtmp])
        for (src, rsrc, dst, neg) in ((a_, ra, acol[dr], False), (m_, rm, em[dr], True)):
            bank, bres = k.psr.next()
            for t in range(32):
                P.op("pe", lambda e, bank=bank, src=src, t=t: e.transpose(
                    bank[:, t * 8:(t + 1) * 8], src[:, t * 128:(t + 1) * 128], k.ident_f[0:8, 0:8]),
                    reads=[rsrc, k.rconst], writes=[bres], inc=(t == 31))
            dflat = dst.rearrange("p a b -> p (a b)")
            if neg:
                P.op("act", act(dflat, bank[:, 0:256], AF.Exp, scale=-1.0), reads=[bres], writes=[rcol])
            else:
                P.op("dve", lambda e, dflat=dflat, bank=bank: e.tensor_copy(dflat, bank[:, 0:256]), reads=[bres], writes=[rcol])
    P.barrier()
    A.reset(mg)
    mask = A.alloc([2, 128], BF16); rmask = Res("mask")
    P.dma("pool", mask, k.cin["mask"], writes=[rmask])
    hf = A.alloc([32, 256], F32); rhf = [Res(f"hf{i}") for i in range(32)]
    qTs = ring(A, 1, [S], BF16, "q")
    kTs = ring(A, 1, [S], BF16, "k")
    vas = ring(A, 2, [32, 257], BF16, "v")
    Mbcs = ring(A, 2, [S], F32, "M")
    Mdgs = ring(A, 2, [32, 128], F32, "Md")
    negm = A.alloc([2, 128], F32); rnegm = Res("negm")
    P.dma("sp", negm, k.cin["mask"], writes=[rnegm])
    P.op("dve", lambda e: e.tensor_scalar(negm, negm, -1.0e4, 1.0e4, ALU.mult, ALU.add), reads=[rnegm], writes=[rnegm])
    wts = ring(A, 6, [512], F32, "wt")
    pts = ring(A, 7, [512], BF16, "pt")
    ogs = ring(A, 2, [4, 256], F32, "og")
    hss = ring(A, 4, [256], F32, "hs")
    junk = A.alloc([256], F32); rjunk = Res("junk")
    ybs = ring(A, 8, [256], BF16, "yb")
    ysts = ring(A, 2, [2, 512], BF16, "yst")
    sms = ring(A, 8, [8], F32, "sm")
    accs = k.banks[0:4]
    sbanks = Ring(k.banks[4:7])
    tb_bf = k.banks[7][0].bitcast(BF16)
    rtb = k.banks[7][1]
    for (va, rv) in vas.items:
        P.op("pool", lambda e, va=va: e.memset(va[:, :, 256:257], 1.0), writes=[rv])
    items = []
    for h in range(4):
        qT, rq = qTs.next()
        kT, rk = kTs.next()
        va, rv = vas.next()

        def head_load(h=h, qT=qT, rq=rq, kT=kT, rk=rk, va=va, rv=rv):
            P.dma("sp", qT, d["qT"][h * 128:(h + 1) * 128, :], writes=[rq])
            P.dma("sp", kT, d["kT"][h * 128:(h + 1) * 128, :], writes=[rk])
            P.dma("sp", va[:, :, 0:256], d["v"][:, h * 256:(h + 1) * 256].rearrange("(t p) c -> p t c", p=128), writes=[rv])
        for dr in range(2):
            r = dr * 4 + h
            Mbc, rM = Mbcs.next()
            Mdg, rMd = Mdgs.next()

            def dir_load(r=r, Mbc=Mbc, rM=rM, Mdg=Mdg, rMd=rMd, dr=dr):
                P.dma("sp", Mbc, d["gsc"][1, r:r + 1, :].to_broadcast([128, S]), writes=[rM])
                P.op("pool", lambda e: e.tensor_tensor(
                    Mdg, Mbc.rearrange("p (t c) -> p t c", t=32),
                    negm[:, dr, :].unsqueeze(1).to_broadcast([128, 32, 128]), ALU.add),
                    reads=[rM, rnegm], writes=[rMd])
            pre = [dir_load] + ([head_load] if dr == 0 else [])
            for tb in range(8):
                q0 = tb * 512
                blk = {}
                if dr == 1:
                    def blk_load(tb=tb, h=h, blk=blk):
                        og, rog = ogs.next()
                        P.dma("sp", og, d["og"][tb * 512:(tb + 1) * 512, h * 256:(h + 1) * 256].rearrange("(j p) c -> p j c", p=128),
                              writes=[rog])
                        blk["og"] = (og, rog)
                    pre = pre + [blk_load]
                sis = list(range(0, 4 * tb + 4)) if dr == 0 else list(range(4 * tb, 32))
                for si in sis:
                    if dr == 0:
                        j0, j1 = max(0, si - 4 * tb), 4
                    else:
                        j0, j1 = 0, min(3, si - 4 * tb) + 1
                    c0, c1 = j0 * 128, j1 * 128
                    st = {}

                    def Afn(si=si, c0=c0, c1=c1, tb=tb, q0=q0, st=st, pre=pre, dr=dr, r=r,
                            qT=qT, rq=rq, kT=kT, rk=rk, Mbc=Mbc, rM=rM, Mdg=Mdg, rMd=rMd):
                        for f in pre:
                            f()
                        sb, rsb = sbanks.next()
                        P.op("pe", mm(sb[:, c0:c1], kT[:, si * 128:(si + 1) * 128], qT[:, q0 + c0:q0 + c1], True, True),
                             reads=[rk, rq], writes=[rsb])
                        wt, rwt = wts.next()
                        jd = si - 4 * tb
                        bias_ = acol[dr][:, si, r:r + 1]
                        if 0 <= jd < 4:
                            d0, d1 = jd * 128, (jd + 1) * 128
                            P.op("act", act(wt[:, d0:d1], Mdg[:, si, :], AF.Exp, bias=bias_, scale=-1.0),
                                 reads=[rMd, rcol], writes=[rwt])
                            o0, o1 = (d1, c1) if dr == 0 else (c0, d0)
                            if o1 > o0:
                                P.op("act", act(wt[:, o0:o1], Mbc[:, q0 + o0:q0 + o1], AF.Exp, bias=bias_, scale=-1.0),
                                     reads=[rM, rcol], writes=[rwt])
                        else:
                            P.op("act", act(wt[:, c0:c1], Mbc[:, q0 + c0:q0 + c1], AF.Exp, bias=bias_, scale=-1.0),
                                 reads=[rM, rcol], writes=[rwt])
                        pt, rpt = pts.next()
                        P.op("dve", lambda e: e.tensor_tensor(pt[:, c0:c1], sb[:, c0:c1], wt[:, c0:c1], ALU.mult),
                             reads=[rsb, rwt], writes=[rpt])
                        st["pt"] = (pt, rpt)
                    pre = []
                    is_last = (si == sis[-1])
                    e2box = {}

                    def Bfn(si=si, j0=j0, j1=j1, tb=tb, st=st, is_last=is_last, va=va, rv=rv, dr=dr, r=r, blk=blk, e2box=e2box):
                        pt, rpt = st["pt"]
                        for j in range(j0, j1):
                            tj = 4 * tb + j
                            first = (si == 0) if dr == 0 else (si == tj)
                            last = (si == tj) if dr == 0 else (si == 31)
                            P.op("pe", mm(accs[j][0][:, 0:257], pt[:, j * 128:(j + 1) * 128], va[:, si, :], first, last),
                                 reads=[rpt, rv], writes=[accs[j][1]], inc=last)
                        if not is_last:
                            return
                        ybl = []
                        for j in range(4):
                            tj = 4 * tb + j
                            acc, racc = accs[j]
                            sm, rsm = sms.next()
                            P.op("act", act(sm[:, 0:1], acc[:, 256:257], AF.Abs), reads=[racc], writes=[rsm])
                            P.op("dve", lambda e, sm=sm, tj=tj: e.tensor_tensor(sm[:, 1:2], sm[:, 0:1], em[dr][:, tj, r:r + 1], ALU.max),
                                 reads=[rsm, rcol], writes=[rsm])
                            P.op("dve", lambda e, sm=sm: e.reciprocal(sm[:, 2:3], sm[:, 1:2]), reads=[rsm], writes=[rsm])
                            if dr == 0:
                                P.op("act", act(hf[:, tj, :], acc[:, 0:256], AF.Copy, scale=sm[:, 2:3]),
                                     reads=[racc, rsm], writes=[rhf[tj]])
                            else:
                                og, rog = blk["og"]
                                hs, rhs = hss.next()
                                P.op("dve", lambda e, hs=hs, sm=sm, acc=acc, tj=tj: e.scalar_tensor_tensor(
                                    hs, acc[:, 0:256], sm[:, 2:3], hf[:, tj, :], ALU.mult, ALU.add),
                                    reads=[racc, rsm, rhf[tj]], writes=[rhs])
                                P.op("act", act(junk, hs, AF.Square, accum_out=sm[:, 3:4]), reads=[rhs, rsm], writes=[rjunk, rsm])
                                P.op("act", act(sm[:, 4:5], sm[:, 3:4], AF.Sqrt, scale=1.0 / 256, bias=k.eps_ap),
                                     reads=[rsm, k.rconst], writes=[rsm])
                                P.op("dve", lambda e, sm=sm: e.reciprocal(sm[:, 4:5], sm[:, 4:5]), reads=[rsm], writes=[rsm])
                                yb, ryb = ybs.next()
                                P.op("dve", lambda e, yb=yb, hs=hs, sm=sm, og=og, j=j: e.scalar_tensor_tensor(
                                    yb, hs, sm[:, 4:5], og[:, j, :], ALU.mult, ALU.mult),
                                    reads=[rhs, rsm, rog], writes=[ryb])
                                ybl.append((yb, ryb))
                        e2box["ybl"] = ybl

                    def E2fn(tb=tb, h=h, e2box=e2box):
                        yst, ryst = ysts.next()
                        for j, (yb, ryb) in enumerate(e2box["ybl"]):
                            for cc in range(2):
                                slot = (cc * 4 + j) * 128
                                P.op("pe", lambda e, yb=yb, cc=cc, slot=slot: e.transpose(
                                    tb_bf[:, slot:slot + 128], yb[:, cc * 128:(cc + 1) * 128], k.ident_b),
                                    reads=[ryb, k.rconst], writes=[rtb])
                        for cc in range(2):
                            alt_copy(k, yst[:, cc, :], tb_bf[:, cc * 512:(cc + 1) * 512], [rtb], [ryst])
                            P.dma("sp", d["yT"][h * 256 + cc * 128:h * 256 + (cc + 1) * 128, tb * 512:(tb + 1) * 512],
                                  yst[:, cc, :], reads=[ryst])
                    items.append((Afn, Bfn, E2fn if (is_last and dr == 1) else None))
    run_pipeline(items, L=PIPE_L, defer=PIPE_L + 2)
    P.barrier()


def phase_qkv(k, xinT, d):
    P, A = k.P, k.A
    A.reset(k.a0)
    hT = A.alloc([16, S], BF16)
    hres = [Res(f"h{i}") for i in range(8)]
    m_ = A.mark()
    run_all(norm_steps(k, xinT, PV["o_norm"], hT, hres, TW=256))
    P.barrier()
    A.reset(m_)
    wring = ring(A, 3, [16, 512], BF16, "w")
    st_b = ring(A, 4, [512], BF16, "sb")
    W = k.w["o_w_qkv"]

    def epi_bf(dst, scale, row0):
        def epi(c, mw, n, bank, bres):
            st, rst = st_b.next()
            alt_copy(k, st[0:mw, :], bank[0:mw, :], [bres], [rst], scale)
            P.dma("sp", dst[c - row0:c - row0 + mw, n * 512:(n + 1) * 512], st[0:mw, :], reads=[rst])
        return epi
    linear_fm(k, hT, hres, 16, W, 0, D, wring, 512, 8, epi_bf(d["aqT"], 128.0 ** -0.5, 0))
    linear_fm(k, hT, hres, 16, W, D, D, wring, 512, 8, epi_bf(d["akT"], None, D))

    def epi_v(cb, nb, t, bank, bres):
        st, rst = st_b.next()
        alt_copy(k, st[:, 0:nb], bank[:, 0:nb], [bres], [rst])
        P.dma("sp", d["av"][t * 128:(t + 1) * 128, cb:cb + nb], st[:, 0:nb], reads=[rst])
    linear_tm(k, hT, hres, 16, W, 2 * D, D, wring, epi_v)
    P.barrier()


PIPE_L = 4


def run_pipeline(items, L=2, defer=3):
    n = len(items)
    pend = []
    for i in range(min(L, n)):
        items[i][0]()
    for i in range(n):
        if i + L < n:
            items[i + L][0]()
        items[i][1]()
        pend = [(c - 1, f) for (c, f) in pend]
        for (c, f) in pend:
            if c <= 0:
                f()
        pend = [(c, f) for (c, f) in pend if c > 0]
        if items[i][2] is not None:
            pend.append((defer, items[i][2]))
    for (c, f) in pend:
        f()


def phase_attn(k, d):
    P, A = k.P, k.A
    A.reset(k.a0)
    absd = A.alloc([17, 128], F32)
    mult = A.alloc([17, 128], F32)
    rcc = Res("acon")
    P.dma("sp", absd, k.cin["absd"], writes=[rcc])
    P.dma("sp", mult, k.cin["mult"], writes=[rcc])
    wrs = ring(A, 2, [17, 128], F32, "wr")
    qTs = ring(A, 2, [S], BF16, "q")
    kTs = ring(A, 2, [S], BF16, "k")
    vas = ring(A, 2, [32, 129], BF16, "v")
    ess = ring(A, 6, [512], F32, "es")
    pts = ring(A, 7, [512], BF16, "pt")
    obs = ring(A, 4, [128], BF16, "ob")
    osts = ring(A, 2, [512], BF16, "ost")
    sms = ring(A, 8, [8], F32, "sm")
    accs = k.banks[0:4]
    sbanks = Ring(k.banks[4:7])
    tb_all = k.banks[7][0].bitcast(BF16)
    tbs = Ring([(tb_all[:, 0:512], Res("tb0")), (tb_all[:, 512:1024], Res("tb1"))])
    for (va, rv) in vas.items:
        P.op("pool", lambda e, va=va: e.memset(va[:, :, 128:129], 1.0), writes=[rv])
    items = []
    for h in range(16):
        slope = 2.0 ** (-8.0 * (h + 1) / 16.0)
        wr, rwr = wrs.next()
        qT, rq = qTs.next()
        kT, rk = kTs.next()
        va, rv = vas.next()

        def head_load(h=h, slope=slope, wr=wr, rwr=rwr, qT=qT, rq=rq, kT=kT, rk=rk, va=va, rv=rv):
            P.op("act", act(wr, absd, AF.Exp, scale=-slope), reads=[rcc], writes=[rwr])
            P.op("pool", lambda e: e.tensor_tensor(wr, wr, mult, ALU.mult), reads=[rwr, rcc], writes=[rwr])
            P.dma("sp", qT, d["aqT"][h * 128:(h + 1) * 128, :], writes=[rq])
            P.dma("sp", kT, d["akT"][h * 128:(h + 1) * 128, :], writes=[rk])
            P.dma("sp", va[:, :, 0:128], d["av"][:, h * 128:(h + 1) * 128].rearrange("(t p) c -> p t c", p=128), writes=[rv])
        first_of_head = True
        for tb in range(8):
            q0 = tb * 512
            sis = list(range(max(0, 4 * tb - 8), min(31, 4 * tb + 11) + 1))
            for si in sis:
                j0 = max(0, si - 8 - 4 * tb)
                j1 = min(3, si + 8 - 4 * tb) + 1
                c0, c1 = j0 * 128, j1 * 128
                st = {}

                def Afn(si=si, c0=c0, c1=c1, j0=j0, j1=j1, tb=tb, q0=q0, st=st, hl=(head_load if first_of_head else None),
                        qT=qT, rq=rq, kT=kT, rk=rk, wr=wr, rwr=rwr):
                    if hl is not None:
                        hl()
                    sb, rsb = sbanks.next()
                    P.op("pe", mm(sb[:, c0:c1], kT[:, si * 128:(si + 1) * 128], qT[:, q0 + c0:q0 + c1], True, True),
                         reads=[rk, rq], writes=[rsb])
                    es_, res_ = ess.next()
                    P.op("act", act(es_[:, c0:c1], sb[:, c0:c1], AF.Exp), reads=[rsb], writes=[res_])
                    pt, rpt = pts.next()
                    e0 = (4 * tb + j0) - si + 8
                    nj = j1 - j0
                    P.op("dve", lambda e: e.tensor_tensor(
                        pt[:, c0:c1].rearrange("p (j c) -> p j c", j=nj), es_[:, c0:c1].rearrange("p (j c) -> p j c", j=nj),
                        wr[:, e0:e0 + nj, :], ALU.mult), reads=[res_, rwr], writes=[rpt])
                    st["pt"] = (pt, rpt)
                first_of_head = False
                is_last = (si == sis[-1])
                e2box = {}

                def Bfn(si=si, j0=j0, j1=j1, tb=tb, q0=q0, st=st, is_last=is_last, va=va, rv=rv, h=h, e2box=e2box):
                    pt, rpt = st["pt"]
                    for j in range(j0, j1):
                        tj = 4 * tb + j
                        first = (si == max(0, tj - 8))
                        last = (si == min(31, tj + 8))
                        P.op("pe", mm(accs[j][0][:, 0:129], pt[:, j * 128:(j + 1) * 128], va[:, si, :], first, last),
                             reads=[rpt, rv], writes=[accs[j][1]], inc=last)
                    if is_last:
                        obl = []
                        for j in range(4):
                            acc, racc = accs[j]
                            sm, rsm = sms.next()
                            P.op("dve", lambda e, sm=sm, acc=acc: e.reciprocal(sm[:, 0:1], acc[:, 128:129]), reads=[racc], writes=[rsm])
                            ob, rob = obs.next()
                            P.op("act", act(ob, acc[:, 0:128], AF.Copy, scale=sm[:, 0:1]), reads=[racc, rsm], writes=[rob])
                            obl.append((ob, rob))
                        e2box["obl"] = obl

                def E2fn(tb=tb, q0=q0, h=h, e2box=e2box):
                    ost, rost = osts.next()
                    tbk, rtb = tbs.next()
                    for j, (ob, rob) in enumerate(e2box["obl"]):
                        P.op("pe", lambda e, ob=ob, j=j: e.transpose(tbk[:, j * 128:(j + 1) * 128], ob, k.ident_b),
                             reads=[rob, k.rconst], writes=[rtb])
                    alt_copy(k, ost, tbk, [rtb], [rost])
                    P.dma("sp", d["aoT"][h * 128:(h + 1) * 128, q0:q0 + 512], ost, reads=[rost])
                items.append((Afn, Bfn, E2fn if is_last else None))
    run_pipeline(items, L=PIPE_L, defer=PIPE_L + 2)
    P.barrier()


def phase_final(k, xinT, outT):
    k.A.reset(k.a0)
    run_all(norm_steps(k, xinT, PV["final_norm"], None, None, out_dram=outT, TW=256))
    k.P.barrier()


def build(phases, ext_out, debug=False):
    nc = bass.Bass("TRN2", target_bir_lowering=False)
    k = K()
    k.nc = nc
    k.alt = 0

    def dram(name, shape, dt, kind="Internal"):
        if name in ext_out:
            kind = "ExternalOutput"
        return nc.dram_tensor(name, list(shape), dt, kind=kind).ap()

    ein = lambda name, shape, dt=F32: nc.dram_tensor(name, list(shape), dt, kind="ExternalInput").ap()
    k.w = {
        "e_w_in": ein("e_w_in", [D, DIN]), "e_w_out": ein("e_w_out", [D, D]),
        "rg_wa": ein("rg_wa", [2, 8, 128, 128]), "rg_wx": ein("rg_wx", [2, 8, 128, 128]),
        "o_w_qkv": ein("o_w_qkv", [D, 3 * D]), "o_w_o": ein("o_w_o", [D, D]),
        "f_w1": ein("f_w1", [2, D, DFF]), "f_w3": ein("f_w3", [2, D, DFF]), "f_w2": ein("f_w2", [2, DFF, D]),
    }
    xT = ein("xT", [D, S])
    pvec = ein("pvec", [128, NPV])
    hg_in = ein("hg_bc", [128, 1024])
    c_ident = ein("c_ident", [128, 128])
    c_mask = ein("c_mask", [128, 2, 128])
    c_absd = ein("c_absd", [128, 17, 128])
    c_mult = ein("c_mult", [128, 17, 128])
    d = {}
    for (n, shp, dt) in (("qT", [512, S], BF16), ("kT", [512, S], BF16), ("v", [S, 1024], BF16),
                         ("og", [S, 1024], F32), ("gI", [8, S], F32), ("gF", [8, S], F32),
                         ("xrT", [1024, S], F32), ("ggT", [1024, S], F32), ("yT", [D, S], BF16),
                         ("x1T", [D, S], F32), ("uT", [DFF, S], BF16), ("x2T", [D, S], F32),
                         ("aqT", [D, S], BF16), ("akT", [D, S], BF16), ("av", [S, D], BF16),
                         ("aoT", [D, S], BF16), ("x3T", [D, S], F32), ("x4T", [D, S], F32),
                         ("gsc", [3, 8, S], F32), ("outT", [D, S], F32), ("w2b", [DFF, D], BF16)):
        if n in phases.get("ext_in", ()):
            d[n] = ein(n, shp, dt)
        else:
            d[n] = dram(n, shp, dt)
    k.d = d
    with ExitStack() as es:
        P = Prog(nc, es)
        k.P = P
        AW = 53200
        arena = es.enter_context(nc.sbuf_tensor("arena", [128, AW], F32))
        A = Arena(arena, AW)
        k.A = A
        banks = [(es.enter_context(nc.psum_tensor(f"ps{i}", [128, 512], F32))[:, :], Res(f"ps{i}")) for i in range(8)]
        k.banks = banks
        k.psr = Ring(banks[0:6])
        k.rconst = Res("const")
        k.pv = A.alloc([NPV], F32)
        k.hg_bc = A.alloc([1024], F32)
        k.ones_bf = A.alloc([128], BF16)
        k.eps_ap = A.alloc([1], F32)
        k.ident_f = A.alloc([128], F32)
        k.ident_b = A.alloc([128], BF16)
        P.dma("sp", k.pv, pvec, writes=[k.rconst])
        P.dma("sp", k.hg_bc, hg_in, writes=[k.rconst])
        P.dma("sp", k.ident_f, c_ident, writes=[k.rconst])
        P.dma("pool", k.ident_b, c_ident, writes=[k.rconst])
        P.op("dve", lambda e: e.memset(k.ones_bf, 1.0), writes=[k.rconst])
        P.op("dve", lambda e: e.memset(k.eps_ap, EPS), writes=[k.rconst])
        k.cin = {"mask": c_mask, "absd": c_absd, "mult": c_mult}
        P.barrier()
        k.a0 = A.mark()

        run = phases["run"]
        if "norm_dbg" in run:
            A.reset(k.a0)
            hT_ = A.alloc([16, S], BF16)
            hres_ = [Res(f"h{i}") for i in range(8)]
            run_all(norm_steps(k, xT, PV["e_norm"], hT_, hres_))
            for n_ in range(8):
                P.dma("sp", d["yT"][:, n_ * 512:(n_ + 1) * 512].rearrange("(k p) n -> p k n", p=128),
                      hT_[:, :, n_ * 512:(n_ + 1) * 512], reads=[hres_[n_]])
            P.barrier()
        if "in_proj" in run:
            phase_in_proj(k, xT, d)
        if "rglru" in run:
            phase_rglru(k, d)
        if "mlstm" in run:
            phase_mlstm(k, d)
        if "out_proj0" in run:
            phase_out_proj(k, d["yT"], k.w["e_w_out"], xT, d["x1T"])
        if "ffn0" in run:
            phase_ffn(k, d["x1T"], PV["f_norm0"], k.w["f_w1"][0], k.w["f_w3"][0], k.w["f_w2"][0], d["uT"], d["x2T"])
        if "qkv" in run:
            phase_qkv(k, d["x2T"], d)
        if "attn" in run:
            phase_attn(k, d)
        if "out_proj1" in run:
            phase_out_proj(k, d["aoT"], k.w["o_w_o"], d["x2T"], d["x3T"])
        if "ffn1" in run:
            phase_ffn(k, d["x3T"], PV["f_norm1"], k.w["f_w1"][1], k.w["f_w3"][1], k.w["f_w2"][1], d["uT"], d["x4T"])
        if "final" in run:
            phase_final(k, d["x4T"], d["outT"])
        P.barrier()
        P.emit()
    return nc


def host_inputs(inp, b):
    m = {
        "xT": np.ascontiguousarray(inp["x"][b].T),
        "e_w_in": inp["e_w_in"][0], "e_w_out": inp["e_w_out"][0],
        "rg_wa": inp["e_rg_wa"][0], "rg_wx": inp["e_rg_wx"][0],
        "o_w_qkv": inp["o_w_qkv"][0], "o_w_o": inp["o_w_o"][0],
        "f_w1": inp["f_w1"], "f_w3": inp["f_w3"], "f_w2": inp["f_w2"],
        "pvec": host_pvec(inp),
        "hg_bc": np.ascontiguousarray(np.broadcast_to(np.asarray(inp["e_head_g"][0], np.float32)[None, :], (128, 1024))),
    }
    m.update(host_consts())
    return m


ALL_PHASES = ["in_proj", "rglru", "mlstm", "out_proj0", "ffn0", "qkv", "attn", "out_proj1", "ffn1", "final"]
_NC_CACHE = {}


def kernel(**inputs):
    inp = {k_: np.asarray(v) for k_, v in inputs.items()}
    if "full" not in _NC_CACHE:
        _NC_CACHE["full"] = build({"run": ALL_PHASES, "ext_in": []}, {"outT"})
    nc = _NC_CACHE["full"]
    n = 8
    shared = host_inputs(inp, 0)
    in_maps = []
    for b in range(n):
        m = dict(shared)
        m["xT"] = np.ascontiguousarray(inp["x"][b].T)
        in_maps.append(m)
    res = run_bass_kernel_spmd(nc, in_maps, core_ids=list(range(n)))
    out = np.stack([np.asarray(res.results[b]["outT"]).T for b in range(n)], 0)
    return np.ascontiguousarray(out.astype(np.float32))
```
